# Optimizing a Trainium2 kernel written in Bass

```python
import math
import jax, jax.numpy as jnp
from jax import lax
import numpy as np

D_MODEL = 1024
BATCH = 4
SEQ = 4096
DEPTH = 1

NSA_HEADS = 8
NSA_KV_GROUPS = 2
NSA_HPG = NSA_HEADS // NSA_KV_GROUPS
NSA_HEAD_DIM = 64
NSA_WIDTH = NSA_HEADS * NSA_HEAD_DIM
NSA_KV_WIDTH = NSA_KV_GROUPS * NSA_HEAD_DIM
CMP_BLOCK = 32
CMP_STRIDE = 16
SLC_BLOCK = 64
SLC_TOPN = 16
WINDOW = 512
Q_BLOCK = 128
SLC_Q_BLOCK = 64
CONV_CH = 512
CONV_WIDTH = 31
PEER_HEADS = 8
PEER_NKEYS = 128
PEER_EXPERTS = PEER_NKEYS * PEER_NKEYS
PEER_QDIM = 256
PEER_HALF = PEER_QDIM // 2
PEER_TOPK = 16
PEER_CHUNK = 128

EPS = 1e-6
NEG = -1e30
FORCED = 1e9

IN_SPLIT_SIZES = (NSA_WIDTH,
                  NSA_KV_WIDTH, NSA_KV_WIDTH,
                  NSA_KV_WIDTH, NSA_KV_WIDTH,
                  NSA_KV_WIDTH, NSA_KV_WIDTH,
                  3 * NSA_HEADS,
                  2 * CONV_CH,
                  2 * D_MODEL)
IN_COLS = sum(IN_SPLIT_SIZES)

kernel_name = "hybrid_nsa_conformer_peer_block"


def rmsnorm(x, g):
    xf = x.astype(jnp.float32)
    y = xf * lax.rsqrt(jnp.mean(xf * xf, axis=-1, keepdims=True) + EPS)
    return (y * g.astype(jnp.float32)).astype(x.dtype)


def alibi_slopes(n):
    return jnp.asarray(np.array([2.0 ** (-8.0 * (h + 1) / n) for h in range(n)], np.float32))


def masked_softmax(s, mask):
    s = jnp.where(mask, s.astype(jnp.float32), NEG)
    return jnp.where(mask, jax.nn.softmax(s, axis=-1), 0.0)


def nsa_compressed(q, k, v, pe_k, pe_v, w_k1, w_k2, w_v1, w_v2, slopes_gh):
    B, S = q.shape[0], q.shape[1]
    n_cmp = (S - CMP_BLOCK) // CMP_STRIDE + 1
    idx = np.arange(n_cmp)[:, None] * CMP_STRIDE + np.arange(CMP_BLOCK)[None, :]

    def compress(t, pe, w1, w2):
        blk = t[:, idx] + pe[None, None, :, None, :]
        blk = blk.transpose(0, 1, 3, 2, 4).reshape(B, n_cmp, NSA_KV_GROUPS, CMP_BLOCK * NSA_HEAD_DIM)
        return jax.nn.gelu(blk @ w1, approximate=False) @ w2

    kc = compress(k, pe_k, w_k1, w_k2)
    vc = compress(v, pe_v, w_v1, w_v2)
    t = np.arange(S)
    end = idx[:, -1]
    mask = end[None, :] <= t[:, None]
    dist = (t[:, None] - end[None, :]).astype(np.float32)
    s = jnp.einsum('bsghd,bngd->bghsn', q, kc).astype(jnp.float32) / math.sqrt(NSA_HEAD_DIM)
    s = s - slopes_gh[:, :, None, None] * dist
    p = masked_softmax(s, mask)
    o = jnp.einsum('bghsn,bngd->bsghd', p.astype(vc.dtype), vc)
    return o, p, idx


def nsa_selected(q, k, v, p_cmp, cmp_idx, slopes_gh):
    B, S = q.shape[0], q.shape[1]
    n_slc = S // SLC_BLOCK
    n_sel = min(SLC_TOPN, n_slc)
    starts = cmp_idx[:, 0]
    ends = cmp_idx[:, -1]
    j = np.arange(n_slc)
    overlap = ((starts[:, None] <= (j[None, :] + 1) * SLC_BLOCK - 1) &
               (ends[:, None] >= j[None, :] * SLC_BLOCK)).astype(np.float32)
    imp = jnp.einsum('bghsn,nj->bgsj', p_cmp, overlap)
    t = np.arange(S)
    t_blk = t // SLC_BLOCK
    future = j[None, :] > t_blk[:, None]
    forced = (j[None, :] == 0) | (j[None, :] == t_blk[:, None]) | (j[None, :] == t_blk[:, None] - 1)
    imp = jnp.where(forced, FORCED, jnp.where(future, NEG, imp))
    _, sel = lax.top_k(imp, n_sel)

    kblk = k.reshape(B, n_slc, SLC_BLOCK, NSA_KV_GROUPS, NSA_HEAD_DIM).transpose(0, 3, 1, 2, 4)
    vblk = v.reshape(B, n_slc, SLC_BLOCK, NSA_KV_GROUPS, NSA_HEAD_DIM).transpose(0, 3, 1, 2, 4)
    nqb = S // SLC_Q_BLOCK
    qb_all = q.transpose(0, 2, 1, 3, 4).reshape(B, NSA_KV_GROUPS, nqb, SLC_Q_BLOCK, NSA_HPG, NSA_HEAD_DIM)
    qb_all = jnp.moveaxis(qb_all, 2, 0)
    sel_all = jnp.moveaxis(sel.reshape(B, NSA_KV_GROUPS, nqb, SLC_Q_BLOCK, n_sel), 2, 0)
    q0_all = jnp.arange(nqb, dtype=jnp.int32) * SLC_Q_BLOCK
    bi = jnp.arange(B)[:, None, None, None]
    gi = jnp.arange(NSA_KV_GROUPS)[None, :, None, None]
    n_keys = n_sel * SLC_BLOCK

    def body(xs):
        qb, sb, q0 = xs
        kg = kblk[bi, gi, sb].reshape(B, NSA_KV_GROUPS, SLC_Q_BLOCK, n_keys, NSA_HEAD_DIM)
        vg = vblk[bi, gi, sb].reshape(B, NSA_KV_GROUPS, SLC_Q_BLOCK, n_keys, NSA_HEAD_DIM)
        kpos = (sb[..., None] * SLC_BLOCK + jnp.arange(SLC_BLOCK)).reshape(B, NSA_KV_GROUPS, SLC_Q_BLOCK, n_keys)
        tq = q0 + jnp.arange(SLC_Q_BLOCK)
        diff = tq[None, None, :, None] - kpos
        s = jnp.einsum('bgqhd,bgqkd->bgqhk', qb, kg).astype(jnp.float32) / math.sqrt(NSA_HEAD_DIM)
        s = s - slopes_gh[None, :, None, :, None] * diff[:, :, :, None, :].astype(jnp.float32)
        p = masked_softmax(s, (diff >= 0)[:, :, :, None, :])
        return jnp.einsum('bgqhk,bgqkd->bgqhd', p.astype(vg.dtype), vg)

    o = lax.map(body, (qb_all, sel_all, q0_all))
    return o.transpose(1, 0, 3, 2, 4, 5).reshape(B, S, NSA_KV_GROUPS, NSA_HPG, NSA_HEAD_DIM)


def nsa_window(q, k, v, slopes_gh):
    B, S = q.shape[0], q.shape[1]
    nqb = S // Q_BLOCK
    span = WINDOW + Q_BLOCK
    kp = jnp.pad(k, ((0, 0), (WINDOW, 0), (0, 0), (0, 0)))
    vp = jnp.pad(v, ((0, 0), (WINDOW, 0), (0, 0), (0, 0)))
    idx = np.arange(nqb)[:, None] * Q_BLOCK + np.arange(span)[None, :]
    kw = kp[:, idx]
    vw = vp[:, idx]
    kpos = idx - WINDOW
    tq = np.arange(nqb)[:, None] * Q_BLOCK + np.arange(Q_BLOCK)[None, :]
    diff = tq[:, :, None] - kpos[:, None, :]
    mask = (diff >= 0) & (diff < WINDOW) & (kpos >= 0)[:, None, :]
    qw = q.reshape(B, nqb, Q_BLOCK, NSA_KV_GROUPS, NSA_HPG, NSA_HEAD_DIM)
    s = jnp.einsum('bcqghd,bckgd->bcghqk', qw, kw).astype(jnp.float32) / math.sqrt(NSA_HEAD_DIM)
    s = s - slopes_gh[None, None, :, :, None, None] * diff.astype(np.float32)[None, :, None, None, :, :]
    p = masked_softmax(s, mask[None, :, None, None, :, :])
    o = jnp.einsum('bcghqk,bckgd->bcqghd', p.astype(vw.dtype), vw)
    return o.reshape(B, S, NSA_KV_GROUPS, NSA_HPG, NSA_HEAD_DIM)


def conformer_conv(glu_in, w_dw, b_dw, g_ln, b_ln):
    a, b = jnp.split(glu_in, 2, axis=-1)
    u = a * jax.nn.sigmoid(b)
    u = lax.conv_general_dilated(u, w_dw, window_strides=(1,), padding=[(CONV_WIDTH - 1, 0)],
                                 dimension_numbers=('NWC', 'WIO', 'NWC'),
                                 feature_group_count=CONV_CH) + b_dw
    uf = u.astype(jnp.float32)
    mu = jnp.mean(uf, axis=-1, keepdims=True)
    var = jnp.mean(jnp.square(uf - mu), axis=-1, keepdims=True)
    un = ((uf - mu) * lax.rsqrt(var + EPS) * g_ln.astype(jnp.float32) + b_ln.astype(jnp.float32)).astype(u.dtype)
    return jax.nn.silu(un)


def peer_ffn(xn, w_q, sub_keys, u_tab, v_tab):
    B, S, D = xn.shape
    q = (xn @ w_q).reshape(B, S, PEER_HEADS, 2, PEER_HALF)
    s1 = jnp.einsum('bshd,hkd->bshk', q[..., 0, :], sub_keys[:, 0]).astype(jnp.float32)
    s2 = jnp.einsum('bshd,hkd->bshk', q[..., 1, :], sub_keys[:, 1]).astype(jnp.float32)
    v1, i1 = lax.top_k(s1, PEER_TOPK)
    v2, i2 = lax.top_k(s2, PEER_TOPK)
    cand = (v1[..., :, None] + v2[..., None, :]).reshape(B, S, PEER_HEADS, PEER_TOPK * PEER_TOPK)
    cs, ci = lax.top_k(cand, PEER_TOPK)
    e = (jnp.take_along_axis(i1, ci // PEER_TOPK, axis=-1) * PEER_NKEYS +
         jnp.take_along_axis(i2, ci % PEER_TOPK, axis=-1))
    g = jax.nn.softmax(cs, axis=-1)
    T = B * S
    nch = T // PEER_CHUNK
    xs = (xn.reshape(nch, PEER_CHUNK, D),
          e.reshape(nch, PEER_CHUNK, PEER_HEADS, PEER_TOPK),
          g.reshape(nch, PEER_CHUNK, PEER_HEADS, PEER_TOPK))

    def body(args):
        xc, ec, gc = args
        u = u_tab[ec]
        h = jnp.einsum('cd,chkd->chk', xc, u)
        a = (jax.nn.gelu(h.astype(jnp.float32), approximate=False) * gc).astype(xc.dtype)
        return jnp.einsum('chk,chkd->cd', a, v_tab[ec])

    return lax.map(body, xs).reshape(B, S, D)


def setup_inputs(seed: int = 0) -> dict:
    key = jax.random.key(seed)
    ks = jax.random.split(key, 24)
    L = DEPTH
    f = jnp.float32

    def nrm(k, shape, scale):
        return jax.random.normal(k, shape, f) * scale

    return {
        "x": jax.random.normal(ks[0], (BATCH, SEQ, D_MODEL), f),
        "g_mix": 1.0 + nrm(ks[1], (L, D_MODEL), 0.01),
        "w_in": nrm(ks[2], (L, D_MODEL, IN_COLS), D_MODEL ** -0.5),
        "pe_cmp_k": nrm(ks[3], (L, CMP_BLOCK, NSA_HEAD_DIM), 0.02),
        "pe_cmp_v": nrm(ks[4], (L, CMP_BLOCK, NSA_HEAD_DIM), 0.02),
        "w_cmp_k1": nrm(ks[5], (L, CMP_BLOCK * NSA_HEAD_DIM, NSA_HEAD_DIM), (CMP_BLOCK * NSA_HEAD_DIM) ** -0.5),
        "w_cmp_k2": nrm(ks[6], (L, NSA_HEAD_DIM, NSA_HEAD_DIM), NSA_HEAD_DIM ** -0.5),
        "w_cmp_v1": nrm(ks[7], (L, CMP_BLOCK * NSA_HEAD_DIM, NSA_HEAD_DIM), (CMP_BLOCK * NSA_HEAD_DIM) ** -0.5),
        "w_cmp_v2": nrm(ks[8], (L, NSA_HEAD_DIM, NSA_HEAD_DIM), NSA_HEAD_DIM ** -0.5),
        "w_nsa_out": nrm(ks[9], (L, NSA_WIDTH, D_MODEL), NSA_WIDTH ** -0.5),
        "w_dw": nrm(ks[10], (L, CONV_WIDTH, 1, CONV_CH), CONV_WIDTH ** -0.5),
        "b_dw": nrm(ks[11], (L, CONV_CH), 0.01),
        "g_conv_ln": 1.0 + nrm(ks[12], (L, CONV_CH), 0.01),
        "b_conv_ln": nrm(ks[13], (L, CONV_CH), 0.01),
        "w_conv_out": nrm(ks[14], (L, CONV_CH, D_MODEL), CONV_CH ** -0.5),
        "w_o": nrm(ks[15], (L, D_MODEL, D_MODEL), D_MODEL ** -0.5),
        "g_ffn": 1.0 + nrm(ks[16], (L, D_MODEL), 0.01),
        "w_peer_q": nrm(ks[17], (L, D_MODEL, PEER_HEADS * PEER_QDIM), D_MODEL ** -0.5),
        "peer_sub_keys": nrm(ks[18], (L, PEER_HEADS, 2, PEER_NKEYS, PEER_HALF), PEER_HALF ** -0.5),
        "peer_u": nrm(ks[19], (L, PEER_EXPERTS, D_MODEL), D_MODEL ** -0.5),
        "peer_v": nrm(ks[20], (L, PEER_EXPERTS, D_MODEL), (PEER_HEADS * PEER_TOPK) ** -0.5),
        "g_final": 1.0 + nrm(ks[21], (D_MODEL,), 0.01),
    }


def reference(x, g_mix, w_in, pe_cmp_k, pe_cmp_v, w_cmp_k1, w_cmp_k2, w_cmp_v1, w_cmp_v2,
              w_nsa_out, w_dw, b_dw, g_conv_ln, b_conv_ln, w_conv_out, w_o,
              g_ffn, w_peer_q, peer_sub_keys, peer_u, peer_v, g_final):
    B, S, D = x.shape
    slopes_gh = alibi_slopes(NSA_HEADS).reshape(NSA_KV_GROUPS, NSA_HPG)
    split_pts = np.cumsum(IN_SPLIT_SIZES)[:-1].tolist()
    kv_shape = (B, S, NSA_KV_GROUPS, NSA_HEAD_DIM)
    for l in range(DEPTH):
        xn = rmsnorm(x, g_mix[l])
        proj = xn @ w_in[l]
        (q, kc, vc, ksl, vsl, kwn, vwn, nsa_g, glu_in, merge_g) = jnp.split(proj, split_pts, axis=-1)
        q = q.reshape(B, S, NSA_KV_GROUPS, NSA_HPG, NSA_HEAD_DIM)
        o_cmp, p_cmp, cmp_idx = nsa_compressed(q, kc.reshape(kv_shape), vc.reshape(kv_shape),
                                               pe_cmp_k[l], pe_cmp_v[l], w_cmp_k1[l], w_cmp_k2[l],
                                               w_cmp_v1[l], w_cmp_v2[l], slopes_gh)
        o_slc = nsa_selected(q, ksl.reshape(kv_shape), vsl.reshape(kv_shape), p_cmp, cmp_idx, slopes_gh)
        o_win = nsa_window(q, kwn.reshape(kv_shape), vwn.reshape(kv_shape), slopes_gh)
        gts = jax.nn.sigmoid(nsa_g).reshape(B, S, 3, NSA_KV_GROUPS, NSA_HPG, 1)
        o_nsa = gts[:, :, 0] * o_cmp + gts[:, :, 1] * o_slc + gts[:, :, 2] * o_win
        y_a = o_nsa.reshape(B, S, NSA_WIDTH) @ w_nsa_out[l]
        y_b = conformer_conv(glu_in, w_dw[l], b_dw[l], g_conv_ln[l], b_conv_ln[l]) @ w_conv_out[l]
        g_a, g_b = jnp.split(jax.nn.sigmoid(merge_g), 2, axis=-1)
        x = x + (g_a * y_a + g_b * y_b) @ w_o[l]
        x = x + peer_ffn(rmsnorm(x, g_ffn[l]), w_peer_q[l], peer_sub_keys[l], peer_u[l], peer_v[l])
    return rmsnorm(x, g_final)
```

```python
import numpy as np
import ml_dtypes
from contextlib import ExitStack
import concourse.bass as bass
import concourse.mybir as mybir
from concourse.bass_utils import run_bass_kernel_spmd


F32 = mybir.dt.float32
BF16 = mybir.dt.bfloat16
U32 = mybir.dt.uint32
I32 = mybir.dt.int32
ALU = mybir.AluOpType
AF = mybir.ActivationFunctionType
AX = mybir.AxisListType


class Dep:
    __slots__ = ("name", "w", "r", "dsem", "dcnt")

    def __init__(self, name=""):
        self.name = name
        self.w = None
        self.r = []
        self.dsem = None
        self.dcnt = 0


class K:
    def __init__(self, nc, stack):
        self.nc = nc
        self.stack = stack
        self.eng = {"pe": nc.tensor, "act": nc.scalar, "dve": nc.vector, "pool": nc.gpsimd, "sp": nc.sync}
        self.sems = {}
        self.cnt = {}
        for e in self.eng:
            self.sems[e] = stack.enter_context(nc.semaphore("c_" + e))
            self.cnt[e] = 0
        self.seen = {e: {} for e in self.eng}
        self.ndsem = 0
        self.dfree = {q: [] for q in self.eng}
        self.downers = []
        self.dtot = {}
        self.n_inst = 0

    def dep(self, name=""):
        return Dep(name)

    MAXD = 88

    def _dsem(self, d, e):
        if d.dsem is None:
            d.dsem = {}
        if e not in d.dsem:
            if self.dfree[e]:
                key = self.dfree[e].pop()
            else:
                if self.ndsem >= self.MAXD:
                    raise RuntimeError("out of DMA semaphores")
                key = "d%d" % self.ndsem
                self.ndsem += 1
                self.sems[key] = self.stack.enter_context(self.nc.semaphore(key))
                self.dtot[key] = 0
            d.dsem[e] = key
            self.downers.append((d, e))
        return d.dsem[e]

    def barrier(self):
        tot = dict(self.cnt)
        tot.update(self.dtot)
        for e, eng in self.eng.items():
            for key, v in tot.items():
                if key == e or v == 0:
                    continue
                if self.seen[e].get(key, 0) < v:
                    self.seen[e][key] = v
                    eng.wait_ge(self.sems[key], v)
        for d, q in self.downers:
            self.dfree[q].append(d.dsem.pop(q))
        self.downers = []

    def _need(self, e, reads, writes, pe_accum=False, same_ok=False):
        need = {}
        def add(p):
            if p is None:
                return
            k, v = p
            if same_ok and k == e:
                return
            if need.get(k, 0) < v:
                need[k] = v
        for d in reads:
            add(d.w)
        for d in writes:
            if not (pe_accum and d.w is not None and d.w[0] == "pe"):
                add(d.w)
            for p in d.r:
                add(p)
        out = []
        for k, v in need.items():
            if self.seen[e].get(k, 0) >= v:
                continue
            self.seen[e][k] = v
            out.append((k, v))
        return out

    def op(self, e, fn, reads=(), writes=(), pe_accum=False, same_ok=False):
        eng = self.eng[e]
        for k, v in self._need(e, reads, writes, pe_accum, same_ok):
            eng.wait_ge(self.sems[k], v)
        ins = fn(eng)
        self.cnt[e] += 1
        ins.then_inc(self.sems[e], 1)
        tag = (e, self.cnt[e])
        for d in writes:
            d.w = tag
            d.r = []
        for d in reads:
            d.r.append(tag)
        self.n_inst += 1
        return ins

    def dma(self, e, out, in_, reads=(), writes=(), join=False, anchor=None, **kw):
        eng = self.eng[e]
        saved = None
        if join and writes:
            saved = [(d, d.w) for d in writes]
            for d in writes:
                if d.w is not None and d.dsem and d.w[0] in d.dsem.values():
                    d.w = None
        for k, v in self._need(e, reads, writes):
            eng.wait_ge(self.sems[k], v)
        if anchor is None:
            anchor = writes[0] if writes else reads[0]
        key = self._dsem(anchor, e)
        self.dtot[key] += 16
        ins = eng.dma_start(out=out, in_=in_, **kw)
        ins.then_inc(self.sems[key], 16)
        tag = (key, self.dtot[key])
        for d in writes:
            d.w = tag
            d.r = []
        for d in reads:
            d.r.append(tag)
        self.n_inst += 1
        return ins

    def finish(self, deps):
        eng = self.eng["sp"]
        for d in deps:
            for p in ([d.w] if d.w else []) + d.r:
                if self.seen["sp"].get(p[0], 0) < p[1]:
                    self.seen["sp"][p[0]] = p[1]
                    eng.wait_ge(self.sems[p[0]], p[1])


BF = ml_dtypes.bfloat16
NCORES = 8
D = 1024
TOK = 2048
LCTX = 4096
NQT = 16
BIG8 = 240000.0
NEGF = -1.0e30
NW = 4376
C_Q, C_KC, C_VC, C_KS, C_VS, C_KW, C_VW, C_G, C_GLU, C_MG = 0, 512, 640, 768, 896, 1024, 1152, 1280, 1304, 2328
NA = 2328
TB = 256


class T:
    n = 0
    def __init__(self, k, st, name, shape, dt, psum=False):
        nc = k.nc
        T.n += 1
        name = "s%d_%s" % (T.n, name)
        self.t = st.enter_context(nc.psum_tensor(name, shape, dt) if psum else nc.sbuf_tensor(name, shape, dt))
        self.d = k.dep(name)


class DD:
    def __init__(self, k, ap, name=""):
        self.t = ap
        self.d = k.dep(name)


def host_consts(half):
    c = {}
    c["ident"] = np.eye(128, dtype=np.float32)
    p = np.arange(LCTX)
    kaug = np.zeros((5, LCTX), np.float32)
    kaug[0] = 128 * (p // 128); kaug[1] = p % 128; kaug[2] = 1; kaug[3] = 1
    kaug[4] = ((p < 2048) & (half == 0)).astype(np.float32)
    c["kaug"] = kaug.astype(BF)
    n = np.arange(256); end = 16 * n + 31
    caug = np.zeros((5, 256), np.float32)
    caug[0] = 128 * (end // 128); caug[1] = end % 128; caug[2] = 1; caug[3] = 1
    caug[4] = ((n == 255) | ((n < 128) & (half == 0))).astype(np.float32)
    c["caug"] = caug.astype(BF)
    qaug = np.zeros((5, 2, NQT, 4, 128), np.float32)
    for g in range(2):
        for hh in range(4):
            h = 4 * g + hh
            sp = 8.0 * 2.0 ** (-(h + 1))
            for i in range(NQT):
                qpos = 2048 + 128 * i + np.arange(128)
                qaug[0, g, i, hh] = sp; qaug[1, g, i, hh] = sp
                qaug[2, g, i, hh] = -sp * (128 * (qpos // 128)); qaug[3, g, i, hh] = -sp * (qpos % 128)
                qaug[4, g, i, hh] = -BIG8
    c["qaug"] = qaug.astype(BF)
    j = np.arange(64)
    c["expand"] = (p[None, :] // 64 == j[:, None]).astype(np.float32).astype(BF)
    st_ = 16 * n
    ovl = ((st_[:, None] <= 64 * j[None, :] + 63) & (end[:, None] >= 64 * j[None, :])).astype(np.float32)
    ovl[255] = 0
    c["ovl"] = ovl.astype(BF)
    keep = np.zeros((NQT, 128, 64), np.float32); add = np.zeros_like(keep); allow = np.zeros_like(keep)
    gb0 = 0 if half == 1 else 32
    for i in range(NQT):
        tblk = 32 + 2 * i + (np.arange(128) // 64)
        jj = j[None, :]; tb = tblk[:, None]
        valid = (jj >= 32) if half == 0 else np.ones_like(jj, bool)
        valid = np.broadcast_to(valid, (128, 64))
        forced = (jj == gb0) | (jj == tb) | ((jj == tb - 1) & valid)
        future = jj > tb
        bad = (future | ~valid) & ~forced
        keep[i] = (~forced & ~bad).astype(np.float32)
        add[i] = np.where(forced, 1e9, np.where(bad, NEGF, 0.0))
        allow[i] = (valid & ~future).astype(np.float32)
    c["selkeep"] = keep; c["seladd"] = add; c["selallow"] = allow
    c["iotaf"] = np.broadcast_to(np.arange(128, dtype=np.float32)[None, :], (128, 128)).copy()
    return c


CONST_SPECS = [("ident", [128, 128], "f"), ("kaug", [5, LCTX], "b"), ("caug", [5, 256], "b"),
               ("qaug", [5, 2, NQT, 4, 128], "b"), ("expand", [64, LCTX], "b"), ("ovl", [256, 64], "b"),
               ("selkeep", [NQT, 128, 64], "f"), ("seladd", [NQT, 128, 64], "f"), ("selallow", [NQT, 128, 64], "f"),
               ("iotaf", [128, 128], "f")]

W_SPECS = [("g_mix", [D]), ("w_in", [D, NW]), ("pe_cmp_k", [32, 64]), ("pe_cmp_v", [32, 64]),
           ("w_cmp_k1", [2048, 64]), ("w_cmp_k2", [64, 64]), ("w_cmp_v1", [2048, 64]), ("w_cmp_v2", [64, 64]),
           ("w_nsa_out", [512, D]), ("w_dw", [31, 512]), ("b_dw", [512]), ("g_conv_ln", [512]), ("b_conv_ln", [512]),
           ("w_conv_out", [512, D]), ("w_o", [D, D]), ("g_ffn", [D]), ("w_peer_q", [D, 2048]),
           ("peer_sub_keys", [16, 128, 128]), ("peer_u", [16384, D]), ("peer_v", [16384, D]), ("g_final", [D])]


def build(debug=()):
    nc = bass.Bass("TRN2", target_bir_lowering=False)
    A = {}
    A["xc"] = nc.dram_tensor("xc", [LCTX, D], F32, kind="ExternalInput").ap()
    for name, shp in W_SPECS:
        A[name] = nc.dram_tensor(name, shp, F32, kind="ExternalInput").ap()
    for name, shp, ty in CONST_SPECS:
        A[name] = nc.dram_tensor(name, shp, F32 if ty == "f" else BF16, kind="ExternalInput").ap()
    out = nc.dram_tensor("out", [TOK, D], F32, kind="ExternalOutput").ap()
    x1d_ap = nc.dram_tensor("x1d", [TOK, D], F32, kind="Internal").ap()
    sTd_ap = nc.dram_tensor("sTd", [128, 4, TOK], BF16, kind="Internal").ap()
    xn2Td_ap = nc.dram_tensor("xn2Td", [128, 8, TOK], BF16, kind="Internal").ap()
    uvd_ap = nc.dram_tensor("uvd", [128, 128, 2 * D], BF16, kind="Internal").ap()
    dbg_aps = {}

    with ExitStack() as st:
        k = K(nc, st)
        x1d = [DD(k, x1d_ap[i * 128:(i + 1) * 128, :], "x1d%d" % i) for i in range(NQT)]
        sTd = DD(k, sTd_ap, "sTd")
        outdeps = []
        PB = [T(k, st, "pb%d" % i, [128, 512], F32, psum=True) for i in range(8)]

        def pbf(i):
            return PB[i].t[:].bitcast(BF16)

        def dump(name, src, shape, dt):
            if name in debug:
                ap = nc.dram_tensor("dbg_" + name, shape, dt, kind="ExternalOutput").ap()
                dd = DD(k, ap, "dbg_" + name)
                k.dma("sp", ap, src.t[:], reads=[src.d], writes=[dd.d], anchor=src.d)
                outdeps.append(dd.d)

        ident = T(k, st, "ident", [128, 128], F32)
        identb = T(k, st, "identb", [128, 128], BF16)
        k.dma("sp", ident.t[:], A["ident"], writes=[ident.d])
        k.op("dve", lambda e: e.tensor_copy(identb.t[:], ident.t[:]), reads=[ident.d], writes=[identb.d])

        def load_vec_cols(stk, name, ap1d, ncol):
            t = T(k, stk, name, [128, ncol], F32)
            k.dma("sp", t.t[:], ap1d.rearrange("(c p) -> p c", p=128), writes=[t.d], allow_slow_non_contiguous=True)
            return t

        def load_bcast(stk, name, ap1d, n):
            t = T(k, stk, name, [128, n], F32)
            k.dma("sp", t.t[:], ap1d.partition_broadcast(128), writes=[t.d])
            return t

        NSTG = 2
        SW = 1024
        stg = [T(k, st, "stg%d" % i, [128, SW], F32) for i in range(NSTG)]
        stg_i = [0]
        cast_eng = ["dve", "act"]

        def load_w(dst_ap, dst_dep, src_ap, ncols, scale=None, npart=128):
            for c0 in range(0, ncols, SW):
                w_ = min(SW, ncols - c0)
                s = stg[stg_i[0] % NSTG]
                eng = cast_eng[stg_i[0] % 2]
                stg_i[0] += 1
                k.dma("sp", s.t[0:npart, 0:w_], src_ap[:, c0:c0 + w_], writes=[s.d])
                dv = dst_ap[:, c0:c0 + w_]
                if scale is not None:
                    sc, col = scale
                    if eng == "act":
                        k.op("act", lambda e: e.activation(dv, s.t[0:npart, 0:w_], AF.Copy, scale=sc.t[0:npart, col:col + 1]),
                             reads=[s.d, sc.d], writes=[dst_dep])
                    else:
                        k.op(eng, lambda e: e.tensor_scalar(dv, s.t[0:npart, 0:w_], sc.t[0:npart, col:col + 1], None, ALU.mult),
                             reads=[s.d, sc.d], writes=[dst_dep])
                elif eng == "act":
                    k.op("act", lambda e: e.copy(dv, s.t[0:npart, 0:w_]), reads=[s.d], writes=[dst_dep])
                else:
                    k.op(eng, lambda e: e.tensor_copy(dv, s.t[0:npart, 0:w_]), reads=[s.d], writes=[dst_dep])

        xin = [T(k, st, "xin%d" % i, [128, D], F32) for i in range(2)]
        xjunk = T(k, st, "xjunk", [128, D], BF16)
        xs_b = [T(k, st, "xsb%d" % i, [128, D], BF16) for i in range(2)]
        nstat = [T(k, st, "nstat%d" % i, [128, 4], F32) for i in range(2)]
        xin_i = [0]

        def rms_scale(src, dstat):
            k.op("act", lambda e: e.activation(xjunk.t[:], src.t[:], AF.Square, accum_out=dstat.t[:, 0:1]),
                 reads=[src.d], writes=[xjunk.d, dstat.d])
            k.op("dve", lambda e: e.tensor_scalar(dstat.t[:, 1:2], dstat.t[:, 0:1], 1.0 / D, 1e-6, ALU.mult, ALU.add),
                 reads=[dstat.d], writes=[dstat.d])
            k.op("act", lambda e: e.sqrt(dstat.t[:, 1:2], dstat.t[:, 1:2]), reads=[dstat.d], writes=[dstat.d])
            k.op("dve", lambda e: e.reciprocal(dstat.t[:, 1:2], dstat.t[:, 1:2]), reads=[dstat.d], writes=[dstat.d])

        def norm_transpose(src, dstT, col0, bank):
            norm_transpose_many([(src, dstT, col0)])

        def norm_transpose_many(items):
            slots = []
            for (src, dstT, col0) in items:
                j = xin_i[0] % 2
                xin_i[0] += 1
                slots.append((nstat[j], xs_b[j]))
            for (src, _, _), (ns, xb) in zip(items, slots):
                k.op("act", lambda e: e.activation(xjunk.t[:], src.t[:], AF.Square, accum_out=ns.t[:, 0:1]),
                     reads=[src.d], writes=[xjunk.d, ns.d])
            for (src, _, _), (ns, xb) in zip(items, slots):
                k.op("dve", lambda e: e.tensor_scalar(ns.t[:, 1:2], ns.t[:, 0:1], 1.0 / D, 1e-6, ALU.mult, ALU.add), reads=[ns.d], writes=[ns.d])
            for (src, _, _), (ns, xb) in zip(items, slots):
                k.op("act", lambda e: e.sqrt(ns.t[:, 1:2], ns.t[:, 1:2]), reads=[ns.d], writes=[ns.d])
            for (src, _, _), (ns, xb) in zip(items, slots):
                k.op("dve", lambda e: e.reciprocal(ns.t[:, 1:2], ns.t[:, 1:2]), reads=[ns.d], writes=[ns.d])
            for (src, _, _), (ns, xb) in zip(items, slots):
                k.op("dve", lambda e: e.tensor_scalar(xb.t[:], src.t[:], ns.t[:, 1:2], None, ALU.mult), reads=[src.d, ns.d], writes=[xb.d])
            banks = [0, 7]
            for idx, ((src, dstT, col0), (ns, xb)) in enumerate(zip(items, slots)):
                pv = pbf(banks[idx])
                for c in range(8):
                    k.op("pe", lambda e: e.transpose(pv[:, c * 128:(c + 1) * 128], xb.t[:, c * 128:(c + 1) * 128], identb.t[:]),
                         reads=[xb.d, identb.d], writes=[PB[banks[idx]].d], pe_accum=(c > 0))
            for idx, ((src, dstT, col0), (ns, xb)) in enumerate(zip(items, slots)):
                pv = pbf(banks[idx])
                dv = dstT.t[:, 0:8, col0:col0 + 128]
                sv = pv.rearrange("p (c t) -> p c t", c=8)
                if idx == 0:
                    k.op("act", lambda e: e.copy(dv, sv), reads=[PB[banks[idx]].d], writes=[dstT.d])
                else:
                    k.op("dve", lambda e: e.tensor_copy(dv, sv), reads=[PB[banks[idx]].d], writes=[dstT.d])

        def load_xn_block(xb, row0, ntile):
            for j0 in range(0, ntile, 2):
                items = []
                for j in range(j0, min(j0 + 2, ntile)):
                    xi = xin[(xin_i[0] + len(items)) % 2]
                    k.dma("sp", xi.t[:], A["xc"][row0 + j * 128:row0 + (j + 1) * 128, :], writes=[xi.d])
                    items.append((xi, xb, j * 128))
                norm_transpose_many(items)

        bank_rr = [0]

        def next_bank():
            b = 1 + (bank_rr[0] % 6)
            bank_rr[0] += 1
            return b

        ev_i = [0]

        def evac_copy(dst_ap, dst_dep, bank, src_ap, func=None):
            ev_i[0] += 1
            if func is not None:
                k.op("act", lambda e: e.activation(dst_ap, src_ap, func), reads=[PB[bank].d], writes=[dst_dep])
            elif ev_i[0] % 2:
                k.op("act", lambda e: e.copy(dst_ap, src_ap), reads=[PB[bank].d], writes=[dst_dep])
            else:
                k.op("dve", lambda e: e.tensor_copy(dst_ap, src_ap), reads=[PB[bank].d], writes=[dst_dep])

        def mm8(bank, out_ap, lhs_fn, rhs_fn, rdeps, n=8):
            for c in range(n):
                k.op("pe", lambda e: e.matmul(out_ap, lhs_fn(c), rhs_fn(c), start=(c == 0), stop=(c == n - 1)),
                     reads=rdeps, writes=[PB[bank].d], pe_accum=True)

        gmix = load_vec_cols(st, "gmix", A["g_mix"], 8)
        gffn = load_vec_cols(st, "gffn", A["g_ffn"], 8)
        iotaf = T(k, st, "iotaf", [128, 128], F32)
        k.dma("sp", iotaf.t[:], A["iotaf"], writes=[iotaf.d])
        iotab = T(k, st, "iotab", [128, 128], BF16)
        k.op("dve", lambda e: e.tensor_copy(iotab.t[:], iotaf.t[:]), reads=[iotaf.d], writes=[iotab.d])
        pX = st.enter_context(ExitStack())
        onsaT = T(k, pX, "onsaT", [128, 4, TOK], BF16)

        with ExitStack() as pc:
            wG = T(k, pc, "wG", [128, 8, 1024], BF16)
            for c in range(8):
                load_w(wG.t[:, c, :], wG.d, A["w_in"][c * 128:(c + 1) * 128, C_GLU:C_GLU + 1024], 1024, scale=(gmix, c))
            xnT = [T(k, pc, "xnTc%d" % i, [128, 8, 512], BF16) for i in range(2)]
            U = T(k, pc, "U", [128, 4, 128 + TOK], BF16)
            Cv = T(k, pc, "Cv", [128, 4, TOK], F32)
            sgt = T(k, pc, "sgt", [128, 512], F32)
            sTl = T(k, pc, "sTl", [128, 4, TOK], BF16)
            wdw = T(k, pc, "wdw", [128, 4, 31], F32)
            for ct in range(4):
                k.dma("sp", wdw.t[:, ct, :], A["w_dw"][:, ct * 128:(ct + 1) * 128].rearrange("k c -> c k"), writes=[wdw.d],
                      allow_slow_non_contiguous=True, join=True)
            bdw = load_vec_cols(pc, "bdw", A["b_dw"], 4)
            gln = load_bcast(pc, "gln", A["g_conv_ln"], 512)
            bln = load_bcast(pc, "bln", A["b_conv_ln"], 512)
            for bi, (row0, ntile, ucol) in enumerate([(1920, 1, 0)] + [(2048 + 512 * b, 4, 128 + 512 * b) for b in range(4)]):
                xb = xnT[bi % 2]
                load_xn_block(xb, row0, ntile)
                ntok = ntile * 128
                for ct in range(4):
                    ba = next_bank()
                    mm8(ba, PB[ba].t[:, 0:ntok], lambda c: wG.t[:, c, ct * 128:(ct + 1) * 128], lambda c: xb.t[:, c, 0:ntok], [wG.d, xb.d])
                    bb = next_bank()
                    mm8(bb, PB[bb].t[:, 0:ntok], lambda c: wG.t[:, c, 512 + ct * 128:512 + (ct + 1) * 128], lambda c: xb.t[:, c, 0:ntok], [wG.d, xb.d])
                    uv = U.t[:, ct, ucol:ucol + ntok]
                    k.op("act", lambda e: e.activation(sgt.t[:, 0:ntok], PB[bb].t[:, 0:ntok], AF.Sigmoid), reads=[PB[bb].d], writes=[sgt.d])
                    k.op("dve", lambda e: e.tensor_tensor(uv, sgt.t[:, 0:ntok], PB[ba].t[:, 0:ntok], ALU.mult), reads=[sgt.d, PB[ba].d], writes=[U.d])
            Dg = T(k, pc, "Dg", [128, 4, 31, 128], BF16)
            for ct in range(4):
                for kk in range(31):
                    k.op("dve", lambda e: e.tensor_scalar(Dg.t[:, ct, kk, :], identb.t[:, :], wdw.t[:, ct, kk:kk + 1], None, ALU.mult),
                         reads=[identb.d, wdw.d], writes=[Dg.d], same_ok=(ct + kk > 0))
            cdeps = [k.dep("cv%d" % ct) for ct in range(4)]
            for ct in range(4):
                for blk in range(4):
                    b = next_bank()
                    for kk in range(31):
                        c0 = 98 + kk + blk * 512
                        k.op("pe", lambda e: e.matmul(PB[b].t[:, :], Dg.t[:, ct, kk, :], U.t[:, ct, c0:c0 + 512], start=(kk == 0), stop=(kk == 30)),
                             reads=[Dg.d, U.d], writes=[PB[b].d], pe_accum=(kk > 0))
                    k.op("act", lambda e: e.activation(Cv.t[:, ct, blk * 512:(blk + 1) * 512], PB[b].t[:, :], AF.Identity, bias=bdw.t[:, ct:ct + 1]),
                         reads=[PB[b].d, bdw.d], writes=[cdeps[ct]], same_ok=(blk > 0))
            lnst = T(k, pc, "lnst", [128, 8], F32)
            un = T(k, pc, "un", [128, 512], F32)
            unb = T(k, pc, "unb", [128, 512], BF16)
            lnj = T(k, pc, "lnj", [128, 512], F32)
            for i in range(NQT):
                for ct in range(4):
                    k.op("pe", lambda e: e.transpose(PB[7].t[:, ct * 128:(ct + 1) * 128], Cv.t[:, ct, i * 128:(i + 1) * 128], ident.t[:]),
                         reads=[cdeps[ct], ident.d], writes=[PB[7].d], pe_accum=True)
                k.op("dve", lambda e: e.reduce_sum(lnst.t[:, 0:1], PB[7].t[:], axis=AX.X), reads=[PB[7].d], writes=[lnst.d])
                k.op("act", lambda e: e.activation(lnj.t[:], PB[7].t[:], AF.Square, accum_out=lnst.t[:, 1:2]), reads=[PB[7].d], writes=[lnj.d, lnst.d])
                k.op("dve", lambda e: e.tensor_scalar(lnst.t[:, 2:3], lnst.t[:, 0:1], 1.0 / 512, None, ALU.mult), reads=[lnst.d], writes=[lnst.d])
                k.op("dve", lambda e: e.tensor_tensor(lnst.t[:, 3:4], lnst.t[:, 2:3], lnst.t[:, 2:3], ALU.mult), reads=[lnst.d], writes=[lnst.d])
                k.op("dve", lambda e: e.scalar_tensor_tensor(lnst.t[:, 4:5], lnst.t[:, 1:2], 1.0 / 512, lnst.t[:, 3:4], ALU.mult, ALU.subtract),
                     reads=[lnst.d], writes=[lnst.d])
                k.op("dve", lambda e: e.tensor_scalar(lnst.t[:, 4:5], lnst.t[:, 4:5], 1e-6, None, ALU.add), reads=[lnst.d], writes=[lnst.d])
                k.op("act", lambda e: e.sqrt(lnst.t[:, 4:5], lnst.t[:, 4:5]), reads=[lnst.d], writes=[lnst.d])
                k.op("dve", lambda e: e.reciprocal(lnst.t[:, 4:5], lnst.t[:, 4:5]), reads=[lnst.d], writes=[lnst.d])
                k.op("dve", lambda e: e.tensor_scalar(un.t[:], PB[7].t[:], lnst.t[:, 2:3], lnst.t[:, 4:5], ALU.subtract, ALU.mult),
                     reads=[PB[7].d, lnst.d], writes=[un.d])
                k.op("dve", lambda e: e.tensor_tensor(un.t[:], un.t[:], gln.t[:], ALU.mult), reads=[un.d, gln.d], writes=[un.d])
                k.op("dve", lambda e: e.tensor_tensor(un.t[:], un.t[:], bln.t[:], ALU.add), reads=[un.d, bln.d], writes=[un.d])
                k.op("act", lambda e: e.activation(unb.t[:], un.t[:], AF.Silu), reads=[un.d], writes=[unb.d])
                pv = pbf(6)
                for ct in range(4):
                    k.op("pe", lambda e: e.transpose(pv[:, ct * 128:(ct + 1) * 128], unb.t[:, ct * 128:(ct + 1) * 128], identb.t[:]),
                         reads=[unb.d, identb.d], writes=[PB[6].d], pe_accum=True)
                k.op("act", lambda e: e.copy(sTl.t[:, :, i * 128:(i + 1) * 128], pv[:, 0:512].rearrange("p (c t) -> p c t", c=4)),
                     reads=[PB[6].d], writes=[sTl.d])
            dump("sT", sTl, [128, 4, TOK], BF16)
            k.dma("pool", sTd.t, sTl.t[:], reads=[sTl.d], writes=[sTd.d])
            k.barrier()
        with ExitStack() as p12:
            kT = {}
            for br in ("s", "w"):
                for g in range(2):
                    kT[br, g] = T(k, p12, "kT%s%d" % (br, g), [128, LCTX], BF16)
                    k.op("dve", lambda e: e.memset(kT[br, g].t[64:128, :], 0.0), writes=[kT[br, g].d])
                    k.dma("sp", kT[br, g].t[64:69, :], A["kaug"], writes=[kT[br, g].d])
            kcT = [T(k, p12, "kcT%d" % g, [128, 256], BF16) for g in range(2)]
            for g in range(2):
                k.op("dve", lambda e: e.memset(kcT[g].t[:, :], 0.0), writes=[kcT[g].d])
                k.dma("sp", kcT[g].t[64:69, :], A["caug"], writes=[kcT[g].d])
            qT = [T(k, p12, "qT%d" % g, [128, NQT, 512], BF16) for g in range(2)]
            for g in range(2):
                k.op("dve", lambda e: e.memset(qT[g].t[64:128, :, :], 0.0), writes=[qT[g].d])
                k.dma("sp", qT[g].t[64:69, :, :], A["qaug"][:, g].rearrange("r i h q -> r i (h q)"), writes=[qT[g].d])
            Vt = {br: T(k, p12, "V" + br, [128, 32, 2, 65], BF16) for br in ("s", "w")}
            for br in ("s", "w"):
                k.op("pool", lambda e: e.memset(Vt[br].t[:, :, :, 64:65], 1.0), writes=[Vt[br].d])
            Vc = T(k, p12, "Vc", [128, 2, 2, 65], BF16)
            k.op("pool", lambda e: e.memset(Vc.t[:, :, :, 64:65], 1.0), writes=[Vc.d])
            gts = T(k, p12, "gts", [128, NQT, 24], F32)
            expand = T(k, p12, "expand", [128, LCTX], BF16)
            k.op("dve", lambda e: e.memset(expand.t[64:128, :], 0.0), writes=[expand.d])
            k.dma("sp", expand.t[0:64, :], A["expand"], writes=[expand.d])
            ovl = T(k, p12, "ovl", [128, 2, 64], BF16)
            k.dma("sp", ovl.t[:], A["ovl"].rearrange("(c p) j -> p c j", p=128), writes=[ovl.d])

            with ExitStack() as p1:
                NA1 = 1304
                wA = T(k, p1, "wA", [128, 8, NA1], BF16)
                for c in range(8):
                    load_w(wA.t[:, c, :], wA.d, A["w_in"][c * 128:(c + 1) * 128, 0:NA1], NA1, scale=(gmix, c))
                wKV = T(k, p1, "wKV", [128, 8, 256], BF16)
                for c in range(8):
                    k.op("dve", lambda e: e.tensor_copy(wKV.t[:, c, :].rearrange("p (g a d) -> p a g d", g=2, a=2),
                                                        wA.t[:, c, C_KC:C_KC + 256].rearrange("p (a g d) -> p a g d", a=2, g=2)),
                         reads=[wA.d], writes=[wKV.d])
                cpT = [T(k, p1, "cpT%d" % g, [128, LCTX], BF16) for g in range(2)]
                xnT = [T(k, p1, "xnT%d" % i, [128, 8, 512], BF16) for i in range(2)]
                for blk in range(8):
                    xb = xnT[blk % 2]
                    own = blk >= 4
                    load_xn_block(xb, blk * 512, 4)
                    t0 = blk * 512
                    for g in range(2):
                        for br, c0 in (("s", C_KS), ("w", C_KW)):
                            b = next_bank()
                            mm8(b, PB[b].t[0:64, :], lambda c: wA.t[:, c, c0 + 64 * g:c0 + 64 * g + 64], lambda c: xb.t[:, c, :], [wA.d, xb.d])
                            evac_copy(kT[br, g].t[0:64, t0:t0 + 512], kT[br, g].d, b, PB[b].t[0:64, :])
                        b = next_bank()
                        mm8(b, PB[b].t[:, :], lambda c: wKV.t[:, c, g * 128:(g + 1) * 128], lambda c: xb.t[:, c, :], [wKV.d, xb.d])
                        evac_copy(cpT[g].t[:, t0:t0 + 512], cpT[g].d, b, PB[b].t[:, :])
                    for j in range(4):
                        tile_i = blk * 4 + j
                        for br, c0 in (("s", C_VS), ("w", C_VW)):
                            b = next_bank()
                            mm8(b, PB[b].t[:, 0:128], lambda c: xb.t[:, c, j * 128:(j + 1) * 128], lambda c: wA.t[:, c, c0:c0 + 128], [wA.d, xb.d])
                            evac_copy(Vt[br].t[:, tile_i, :, 0:64], Vt[br].d, b, PB[b].t[:, 0:128].rearrange("p (g d) -> p g d", g=2))
                    if own:
                        ob = blk - 4
                        for g in range(2):
                            for hh in range(4):
                                h = 4 * g + hh
                                b = next_bank()
                                mm8(b, PB[b].t[0:64, :], lambda c: wA.t[:, c, C_Q + 64 * h:C_Q + 64 * h + 64], lambda c: xb.t[:, c, :], [wA.d, xb.d])
                                evac_copy(qT[g].t[0:64, ob * 4:(ob + 1) * 4, hh * 128:(hh + 1) * 128], qT[g].d, b, PB[b].t[0:64, :].rearrange("p (i q) -> p i q", i=4))
                        for j in range(4):
                            b = next_bank()
                            mm8(b, PB[b].t[:, 0:24], lambda c: xb.t[:, c, j * 128:(j + 1) * 128], lambda c: wA.t[:, c, C_G:C_G + 24], [wA.d, xb.d])
                            evac_copy(gts.t[:, ob * 4 + j, :], gts.d, b, PB[b].t[:, 0:24], func=AF.Sigmoid)

                w1b = T(k, p1, "w1b", [128, 32, 64], BF16)
                w2b = T(k, p1, "w2b", [64, 64], BF16)
                w2v0 = T(k, p1, "w2v0", [64, 64], BF16)
                peT = T(k, p1, "peT", [128, 32], F32)
                peTb = T(k, p1, "peTb", [128, 32], BF16)
                cbias = {kv: T(k, p1, "cbias" + kv, [64, 1], F32) for kv in "kv"}
                h1T = T(k, p1, "h1T", [64, 256], BF16)
                k.op("pool", lambda e: e.memset(h1T.t[:], 0.0), writes=[h1T.d])
                for kv, p0 in (("k", 0), ("v", 64)):
                    src1 = A["w_cmp_%s1" % kv].rearrange("(l d) j -> d l j", d=64)
                    for lh in range(2):
                        s = stg[stg_i[0] % NSTG]; stg_i[0] += 1
                        sv = s.t[p0:p0 + 64, 0:1024].rearrange("p (l j) -> p l j", l=16)
                        k.dma("sp", sv, src1[:, lh * 16:(lh + 1) * 16, :], writes=[s.d])
                        k.op("dve", lambda e: e.tensor_copy(w1b.t[p0:p0 + 64, lh * 16:(lh + 1) * 16, :], sv), reads=[s.d], writes=[w1b.d])
                    s = stg[stg_i[0] % NSTG]; stg_i[0] += 1
                    w2dst = w2b if kv == "k" else w2v0
                    k.dma("sp", s.t[0:64, 0:64], A["w_cmp_%s2" % kv], writes=[s.d])
                    k.op("dve", lambda e: e.tensor_copy(w2dst.t[0:64, :], s.t[0:64, 0:64]), reads=[s.d], writes=[w2dst.d])
                    k.dma("sp", peT.t[p0:p0 + 64, :], A["pe_cmp_" + kv].rearrange("l d -> d l"), writes=[peT.d], allow_slow_non_contiguous=True)
                    k.op("dve", lambda e: e.tensor_copy(peTb.t[p0:p0 + 64, :], peT.t[p0:p0 + 64, :]), reads=[peT.d], writes=[peTb.d])
                    b = next_bank()
                    for l in range(32):
                        k.op("pe", lambda e: e.matmul(PB[b].t[0:64, 0:1], w1b.t[p0:p0 + 64, l, :], peTb.t[p0:p0 + 64, l:l + 1],
                                                      start=(l == 0), stop=(l == 31)),
                             reads=[w1b.d, peTb.d], writes=[PB[b].d], pe_accum=True)
                    k.op("dve", lambda e: e.tensor_copy(cbias[kv].t[:], PB[b].t[0:64, 0:1]), reads=[PB[b].d], writes=[cbias[kv].d])
                for g in range(2):
                    for kv, p0 in (("k", 0), ("v", 64)):
                        src = cpT[g].t[p0:p0 + 64, :].rearrange("p (n s) -> p n s", s=16)
                        b = next_bank()
                        for l in range(32):
                            a_, b_ = l // 16, l % 16
                            k.op("pe", lambda e: e.matmul(PB[b].t[0:64, 0:255], w1b.t[p0:p0 + 64, l, :], src[:, a_:a_ + 255, b_],
                                                          start=(l == 0), stop=(l == 31)),
                                 reads=[w1b.d, cpT[g].d], writes=[PB[b].d], pe_accum=True)
                        k.op("act", lambda e: e.activation(h1T.t[:, 0:255], PB[b].t[0:64, 0:255], AF.Gelu, bias=cbias[kv].t[:, 0:1]),
                             reads=[PB[b].d, cbias[kv].d], writes=[h1T.d])
                        if kv == "k":
                            b2 = next_bank()
                            k.op("pe", lambda e: e.matmul(PB[b2].t[0:64, 0:255], w2b.t[0:64, :], h1T.t[:, 0:255], start=True, stop=True),
                                 reads=[w2b.d, h1T.d], writes=[PB[b2].d])
                            k.op("dve", lambda e: e.tensor_copy(kcT[g].t[0:64, 0:255], PB[b2].t[0:64, 0:255]), reads=[PB[b2].d], writes=[kcT[g].d])
                        else:
                            for ch in range(2):
                                b2 = next_bank()
                                k.op("pe", lambda e: e.matmul(PB[b2].t[:, 0:64], h1T.t[:, ch * 128:(ch + 1) * 128], w2v0.t[:], start=True, stop=True),
                                     reads=[w2v0.d, h1T.d], writes=[PB[b2].d])
                                k.op("dve", lambda e: e.tensor_copy(Vc.t[:, ch, g, 0:64], PB[b2].t[:, 0:64]), reads=[PB[b2].d], writes=[Vc.d])
                    if g == 0:
                        pass
                dump("kcT", kcT[0], [128, 256], BF16)
                dump("qT", qT[0], [128, NQT, 512], BF16)
                dump("Vs", Vt["s"], [128, 32, 2, 65], BF16)
                k.barrier()
            with ExitStack() as pa:
                Pt = [T(k, pa, "Pt%d" % i, [128, 512], BF16) for i in range(3)]
                selc = {}
                for nm in ("selkeep", "seladd", "selallow"):
                    selc[nm] = T(k, pa, nm, [128, NQT, 64], F32)
                    k.dma("sp", selc[nm].t[:], A[nm].rearrange("i p j -> p i j"), writes=[selc[nm].d])
                negmT = T(k, pa, "negmT", [128, 512], BF16)
                k.op("dve", lambda e: e.memset(negmT.t[64:128, :], 0.0), writes=[negmT.d])
                impq = T(k, pa, "impq", [128, 64], F32)
                impw = T(k, pa, "impw", [128, 64], F32)
                selm = T(k, pa, "selm", [128, 64], F32)
                m8 = T(k, pa, "m8", [128, 16], F32)
                rz = T(k, pa, "rz", [128, 4], F32)
                coef = T(k, pa, "coef", [128, 4], F32)
                oacc = T(k, pa, "oacc", [128, 2, 4, 64], F32)
                otmp = T(k, pa, "otmp", [128, 4, 64], F32)
                onb = T(k, pa, "onb", [128, 512], BF16)
                s_rr = [0]

                def score_tile(lhs_ap, lhs_dep, g, i, extra=None, mrows=128):
                    b = 1 + (s_rr[0] % 3)
                    pt = Pt[s_rr[0] % 3]
                    s_rr[0] += 1
                    k.op("pe", lambda e: e.matmul(PB[b].t[0:mrows, :], lhs_ap, qT[g].t[0:128, i, :], start=True, stop=(extra is None)),
                         reads=[lhs_dep, qT[g].d], writes=[PB[b].d])
                    if extra is not None:
                        k.op("pe", lambda e: e.matmul(PB[b].t[:, :], extra, negmT.t[:, :], start=False, stop=True),
                             reads=[expand.d, negmT.d], writes=[PB[b].d], pe_accum=True)
                    k.op("act", lambda e: e.activation(pt.t[0:mrows, :], PB[b].t[0:mrows, :], AF.Exp, scale=0.125), reads=[PB[b].d], writes=[pt.d])
                    return pt

                def mask_tile(pt, base, cm, qstep, mrows=128):
                    pv = pt.t[0:mrows, :].rearrange("p (h q) -> p h q", h=4)
                    k.op("pool", lambda e: e.affine_select(out=pv, in_=pv, pattern=[[0, 4], [qstep, 128]], compare_op=ALU.is_ge,
                                                           fill=0.0, base=base, channel_multiplier=cm),
                         reads=[pt.d], writes=[pt.d])

                zz = T(k, pa, "zz", [1, 512], BF16)
                k.op("pool", lambda e: e.memset(zz.t[:], 0.0), writes=[zz.d])

                def zero_bank(bank):
                    k.op("pe", lambda e: e.matmul(PB[bank].t[:, :], zz.t[0:1, 0:128], zz.t[0:1, :], start=True, stop=False),
                         reads=[zz.d], writes=[PB[bank].d])

                def pv_acc(pt, bank, v_ap, v_dep, first, last, mrows=128):
                    if first:
                        zero_bank(bank)
                    for hh in range(4):
                        k.op("pe", lambda e: e.matmul(PB[bank].t[:, hh * 65:(hh + 1) * 65], pt.t[0:mrows, hh * 128:(hh + 1) * 128], v_ap,
                                                      start=False, stop=(last and hh == 3)),
                             reads=[pt.d, v_dep], writes=[PB[bank].d], pe_accum=True)

                def finish_branch(bank, br_idx, g, i, first):
                    pov = PB[bank].t[:, 0:260].rearrange("p (h e) -> p h e", e=65)
                    k.op("dve", lambda e: e.tensor_scalar(rz.t[:], pov[:, :, 64], 1e-30, None, ALU.max), reads=[PB[bank].d], writes=[rz.d])
                    k.op("dve", lambda e: e.reciprocal(rz.t[:], rz.t[:]), reads=[rz.d], writes=[rz.d])
                    gsl = gts.t[:, i, br_idx * 8 + 4 * g:br_idx * 8 + 4 * g + 4]
                    k.op("dve", lambda e: e.tensor_tensor(coef.t[:], rz.t[:], gsl, ALU.mult), reads=[rz.d, gts.d], writes=[coef.d])
                    cb = coef.t[:, :, None].to_broadcast([128, 4, 64])
                    if first:
                        k.op("dve", lambda e: e.tensor_tensor(oacc.t[:, g, :, :], pov[:, :, 0:64], cb, ALU.mult),
                             reads=[PB[bank].d, coef.d], writes=[oacc.d])
                    else:
                        k.op("dve", lambda e: e.tensor_tensor(otmp.t[:], pov[:, :, 0:64], cb, ALU.mult),
                             reads=[PB[bank].d, coef.d], writes=[otmp.d])
                        k.op("dve", lambda e: e.tensor_tensor(oacc.t[:, g, :, :], oacc.t[:, g, :, :], otmp.t[:], ALU.add),
                             reads=[otmp.d, oacc.d], writes=[oacc.d])

                for i in range(NQT):
                    for g in range(2):
                        pts = []
                        mrows = [128, min(128, 8 * i + 7)]
                        for kt in range(2):
                            M = mrows[kt]
                            pt = score_tile(kcT[g].t[0:128, kt * 128:kt * 128 + M], kcT[g].d, g, i, mrows=M)
                            mask_tile(pt, 2017 + 128 * i - 2048 * kt, -16, 1, mrows=M)
                            pts.append(pt)
                        for kt in range(2):
                            M = mrows[kt]
                            pv_acc(pts[kt], 4, Vc.t[0:M, kt, g, :], Vc.d, kt == 0, kt == 1, mrows=M)
                            if kt == 0:
                                zero_bank(5)
                            for hh in range(4):
                                k.op("pe", lambda e: e.matmul(PB[5].t[:, hh * 64:(hh + 1) * 64], pts[kt].t[0:M, hh * 128:(hh + 1) * 128], ovl.t[0:M, kt, :],
                                                              start=False, stop=(kt == 1 and hh == 3)),
                                     reads=[pts[kt].d, ovl.d], writes=[PB[5].d], pe_accum=True)
                        finish_branch(4, 0, g, i, True)
                        for hh in range(4):
                            if hh == 0:
                                k.op("dve", lambda e: e.tensor_scalar(impq.t[:], PB[5].t[:, 0:64], rz.t[:, 0:1], None, ALU.mult),
                                     reads=[PB[5].d, rz.d], writes=[impq.d])
                            else:
                                k.op("dve", lambda e: e.scalar_tensor_tensor(impq.t[:], PB[5].t[:, hh * 64:(hh + 1) * 64], rz.t[:, hh:hh + 1], impq.t[:],
                                                                             ALU.mult, ALU.add),
                                     reads=[PB[5].d, rz.d, impq.d], writes=[impq.d])
                        k.op("dve", lambda e: e.tensor_tensor(impq.t[:], impq.t[:], selc["selkeep"].t[:, i, :], ALU.mult),
                             reads=[impq.d, selc["selkeep"].d], writes=[impq.d])
                        k.op("dve", lambda e: e.tensor_tensor(impq.t[:], impq.t[:], selc["seladd"].t[:, i, :], ALU.add),
                             reads=[impq.d, selc["seladd"].d], writes=[impq.d])
                        if i == 3 and g == 1:
                            dump("impq", impq, [128, 64], F32)
                        k.op("dve", lambda e: e.max(out=m8.t[:, 0:8], in_=impq.t[:]), reads=[impq.d], writes=[m8.d])
                        k.op("dve", lambda e: e.match_replace(out=impw.t[:], in_to_replace=m8.t[:, 0:8], in_values=impq.t[:], imm_value=-3.0e38),
                             reads=[impq.d, m8.d], writes=[impw.d])
                        k.op("dve", lambda e: e.max(out=m8.t[:, 8:16], in_=impw.t[:]), reads=[impw.d], writes=[m8.d])
                        k.op("dve", lambda e: e.tensor_scalar(selm.t[:], impq.t[:], m8.t[:, 15:16], None, ALU.is_ge),
                             reads=[impq.d, m8.d], writes=[selm.d])
                        k.op("dve", lambda e: e.tensor_tensor(selm.t[:], selm.t[:], selc["selallow"].t[:, i, :], ALU.mult),
                             reads=[selm.d, selc["selallow"].d], writes=[selm.d])
                        k.op("dve", lambda e: e.tensor_scalar(selm.t[:], selm.t[:], -1.0, BIG8, ALU.add, ALU.mult), reads=[selm.d], writes=[selm.d])
                        pend = None
                        for r, kt in enumerate(range(12 + i, 17 + i)):
                            pt = score_tile(kT["w", g].t[0:128, kt * 128:(kt + 1) * 128], kT["w", g].d, g, i)
                            if r == 4:
                                mask_tile(pt, 0, -1, 1)
                            elif r == 0:
                                mask_tile(pt, -1, 1, -1)
                            if pend is not None:
                                pv_acc(pend[0], 7, Vt["w"].t[:, pend[1], g, :], Vt["w"].d, pend[2] == 0, False)
                            pend = (pt, kt, r)
                        pv_acc(pend[0], 7, Vt["w"].t[:, pend[1], g, :], Vt["w"].d, pend[2] == 0, True)
                        finish_branch(7, 2, g, i, False)
                        k.op("pe", lambda e: e.transpose(PB[0].t[0:64, 0:128], selm.t[:], ident.t[:]), reads=[selm.d, ident.d], writes=[PB[0].d])
                        k.op("act", lambda e: e.copy(negmT.t[0:64, :].rearrange("p (h q) -> p h q", h=4), PB[0].t[0:64, 0:128][:, None, :].to_broadcast([64, 4, 128])),
                             reads=[PB[0].d], writes=[negmT.d])
                        nk = 17 + i
                        pend = None
                        for kt in range(nk):
                            pt = score_tile(kT["s", g].t[0:128, kt * 128:(kt + 1) * 128], kT["s", g].d, g, i,
                                            extra=expand.t[0:128, kt * 128:(kt + 1) * 128])
                            if kt == nk - 1:
                                mask_tile(pt, 0, -1, 1)
                            if pend is not None:
                                pv_acc(pend[0], 6, Vt["s"].t[:, pend[1], g, :], Vt["s"].d, pend[1] == 0, False)
                            pend = (pt, kt)
                        pv_acc(pend[0], 6, Vt["s"].t[:, pend[1], g, :], Vt["s"].d, pend[1] == 0, True)
                        finish_branch(6, 1, g, i, False)
                    k.op("act", lambda e: e.copy(onb.t[:], oacc.t[:].rearrange("p g h d -> p (g h d)")), reads=[oacc.d], writes=[onb.d])
                    pv0 = pbf(0)
                    for c in range(4):
                        k.op("pe", lambda e: e.transpose(pv0[:, c * 128:(c + 1) * 128], onb.t[:, c * 128:(c + 1) * 128], identb.t[:]),
                             reads=[onb.d, identb.d], writes=[PB[0].d], pe_accum=True)
                    k.op("act", lambda e: e.copy(onsaT.t[:, :, i * 128:(i + 1) * 128], pv0[:, 0:512].rearrange("p (c t) -> p c t", c=4)),
                         reads=[PB[0].d], writes=[onsaT.d])
                dump("onsaT", onsaT, [128, 4, TOK], BF16)
                k.barrier()
        xn2d = [DD(k, xn2Td_ap[:, :, i * 128:(i + 1) * 128], "xn2d%d" % i) for i in range(NQT)]
        with ExitStack() as p3:
            wM = T(k, p3, "wM", [128, 8, 2048], BF16)
            wo = T(k, p3, "wo", [128, 8, D], BF16)
            wno = T(k, p3, "wno", [128, 4, D], BF16)
            wco = T(k, p3, "wco", [128, 4, D], BF16)
            for c in range(8):
                load_w(wM.t[:, c, :], wM.d, A["w_in"][c * 128:(c + 1) * 128, C_MG:C_MG + 2048], 2048, scale=(gmix, c))
                load_w(wo.t[:, c, :], wo.d, A["w_o"][c * 128:(c + 1) * 128, :], D)
            for c in range(4):
                load_w(wno.t[:, c, :], wno.d, A["w_nsa_out"][c * 128:(c + 1) * 128, :], D)
                load_w(wco.t[:, c, :], wco.d, A["w_conv_out"][c * 128:(c + 1) * 128, :], D)
            xnTb = T(k, p3, "xnTb", [128, 8, 512], BF16)
            sTb = T(k, p3, "sTb", [128, 4, 512], BF16)
            sg = T(k, p3, "sg", [128, 2, 512], F32)
            mT = T(k, p3, "mT", [128, 8, 512], BF16)
            t1 = T(k, p3, "t1", [128, 512], F32)
            t2 = T(k, p3, "t2", [128, 512], F32)
            x1t = [T(k, p3, "x1t%d" % i, [128, D], F32) for i in range(2)]
            xn2l = [T(k, p3, "xn2l%d" % i, [128, 8, 128], BF16) for i in range(2)]
            for blk in range(4):
                load_xn_block(xnTb, 2048 + 512 * blk, 4)
                k.dma("sp", sTb.t[:], sTd.t[:, :, blk * 512:(blk + 1) * 512], reads=[sTd.d], writes=[sTb.d])
                for m in range(8):
                    ba = next_bank()
                    mm8(ba, PB[ba].t[:, :], lambda c: wM.t[:, c, m * 128:(m + 1) * 128], lambda c: xnTb.t[:, c, :], [wM.d, xnTb.d])
                    k.op("act", lambda e: e.activation(sg.t[:, 0, :], PB[ba].t[:, :], AF.Sigmoid), reads=[PB[ba].d], writes=[sg.d])
                    bb = next_bank()
                    mm8(bb, PB[bb].t[:, :], lambda c: wM.t[:, c, 1024 + m * 128:1024 + (m + 1) * 128], lambda c: xnTb.t[:, c, :], [wM.d, xnTb.d])
                    k.op("act", lambda e: e.activation(sg.t[:, 1, :], PB[bb].t[:, :], AF.Sigmoid), reads=[PB[bb].d], writes=[sg.d])
                    by = next_bank()
                    mm8(by, PB[by].t[:, :], lambda c: wno.t[:, c, m * 128:(m + 1) * 128], lambda c: onsaT.t[:, c, blk * 512:(blk + 1) * 512],
                        [wno.d, onsaT.d], n=4)
                    bz = next_bank()
                    mm8(bz, PB[bz].t[:, :], lambda c: wco.t[:, c, m * 128:(m + 1) * 128], lambda c: sTb.t[:, c, :], [wco.d, sTb.d], n=4)
                    k.op("dve", lambda e: e.tensor_tensor(t1.t[:], sg.t[:, 0, :], PB[by].t[:, :], ALU.mult), reads=[sg.d, PB[by].d], writes=[t1.d])
                    k.op("dve", lambda e: e.tensor_tensor(t2.t[:], sg.t[:, 1, :], PB[bz].t[:, :], ALU.mult), reads=[sg.d, PB[bz].d], writes=[t2.d])
                    k.op("dve", lambda e: e.tensor_tensor(mT.t[:, m, :], t1.t[:], t2.t[:], ALU.add), reads=[t1.d, t2.d], writes=[mT.d])
                for j in range(4):
                    tile_i = blk * 4 + j
                    xi = xin[xin_i[0] % 2]
                    xt = x1t[tile_i % 2]
                    r0 = 2048 + tile_i * 128
                    k.dma("sp", xi.t[:], A["xc"][r0:r0 + 128, :], writes=[xi.d])
                    for dh in range(2):
                        b = next_bank()
                        mm8(b, PB[b].t[:, :], lambda c: mT.t[:, c, j * 128:(j + 1) * 128], lambda c: wo.t[:, c, dh * 512:(dh + 1) * 512], [mT.d, wo.d])
                        k.op("dve", lambda e: e.tensor_tensor(xt.t[:, dh * 512:(dh + 1) * 512], xi.t[:, dh * 512:(dh + 1) * 512], PB[b].t[:, :], ALU.add),
                             reads=[xi.d, PB[b].d], writes=[xt.d])
                    xin_i[0] += 1
                    k.dma("pool", x1d[tile_i].t, xt.t[:], reads=[xt.d], writes=[x1d[tile_i].d], anchor=xt.d)
                    xl = xn2l[tile_i % 2]
                    norm_transpose(xt, xl, 0, 0)
                    k.dma("pool", xn2d[tile_i].t, xl.t[:], reads=[xl.d], writes=[xn2d[tile_i].d], anchor=xl.d)
            if "x1" in debug:
                pass
            k.barrier()
        pX.close()
        with ExitStack() as pp:
            slotA = T(k, pp, "slotA", [128, NQT, 128], F32)
            slotB = T(k, pp, "slotB", [128, NQT, 128], F32)
            slotG = T(k, pp, "slotG", [128, NQT, 128], F32)
            uvd = [DD(k, uvd_ap[e1], "uvd%d" % e1) for e1 in range(128)]
            with ExitStack() as p4:
                cus = [T(k, p4, "cus%d" % i, [128, D], F32) for i in range(2)]
                cvs = [T(k, p4, "cvs%d" % i, [128, D], F32) for i in range(2)]
                cub = [T(k, p4, "cub%d" % i, [128, D], BF16) for i in range(2)]
                cuv = [T(k, p4, "cuv%d" % i, [128, 2 * D], BF16) for i in range(2)]

                def convert_tile(e1):
                    j = e1 % 2
                    k.dma("sp", cus[j].t[:], A["peer_u"][e1 * 128:(e1 + 1) * 128, :], writes=[cus[j].d])
                    k.dma("sp", cvs[j].t[:], A["peer_v"][e1 * 128:(e1 + 1) * 128, :], writes=[cvs[j].d])
                    k.op("act", lambda e: e.copy(cub[j].t[:], cus[j].t[:]), reads=[cus[j].d], writes=[cub[j].d])
                    k.op("pool", lambda e: e.tensor_copy(cuv[j].t[:, D:2 * D], cvs[j].t[:]), reads=[cvs[j].d], writes=[cuv[j].d])
                    pv = pbf(7)
                    for c in range(8):
                        k.op("pe", lambda e: e.transpose(pv[:, c * 128:(c + 1) * 128], cub[j].t[:, c * 128:(c + 1) * 128], identb.t[:]),
                             reads=[cub[j].d, identb.d], writes=[PB[7].d], pe_accum=(c > 0))
                    k.op("act", lambda e: e.copy(cuv[j].t[:, 0:D], pv), reads=[PB[7].d], writes=[cuv[j].d])
                    k.dma("pool", uvd[e1].t, cuv[j].t[:], reads=[cuv[j].d], writes=[uvd[e1].d], anchor=cuv[j].d)

                wq = T(k, p4, "wq", [128, 8, 2048], BF16)
                for c in range(8):
                    load_w(wq.t[:, c, :], wq.d, A["w_peer_q"][c * 128:(c + 1) * 128, :], 2048, scale=(gffn, c))
                KTt = T(k, p4, "KTt", [128, 16, 128], BF16)
                skb = T(k, p4, "skb", [128, 128], BF16)
                for hp in range(16):
                    s = stg[stg_i[0] % NSTG]; stg_i[0] += 1
                    k.dma("sp", s.t[:, 0:128], A["peer_sub_keys"][hp], writes=[s.d])
                    k.op("dve", lambda e: e.tensor_copy(skb.t[:], s.t[:, 0:128]), reads=[s.d], writes=[skb.d])
                    pv = pbf(0)
                    k.op("pe", lambda e: e.transpose(pv[:, 0:128], skb.t[:], identb.t[:]), reads=[skb.d, identb.d], writes=[PB[0].d])
                    k.op("act", lambda e: e.copy(KTt.t[:, hp, :], pv[:, 0:128]), reads=[PB[0].d], writes=[KTt.d])
                xq = [T(k, p4, "xq%d" % i, [128, 8, 128], BF16) for i in range(2)]
                qpT = T(k, p4, "qpT", [128, 16, 128], BF16)
                sc = T(k, p4, "sc", [128, 16, 128], F32)
                wk16 = T(k, p4, "wk16", [128, 16, 128], F32)
                cw8 = T(k, p4, "cw8", [128, 8, 256], F32)
                v12d = [k.dep("v12d%d" % i) for i in range(16)]
                i12d = [k.dep("i12d%d" % i) for i in range(16)]
                wkd = [k.dep("wkd%d" % i) for i in range(16)]
                csd = [k.dep("csd%d" % i) for i in range(8)]
                cid = [k.dep("cid%d" % i) for i in range(8)]
                cwd = [k.dep("cwd%d" % i) for i in range(8)]
                sc2 = T(k, p4, "sc2", [128, 16, 128], F32)
                v12 = T(k, p4, "v12", [128, 16, 16], F32)
                i12 = T(k, p4, "i12", [128, 16, 16], U32)
                i12f = T(k, p4, "i12f", [128, 16, 16], F32)
                cand = T(k, p4, "cand", [128, 8, 16, 16], F32)
                cs = T(k, p4, "cs", [128, 8, 16], F32)
                ci = T(k, p4, "ci", [128, 8, 16], U32)
                r12 = T(k, p4, "r12", [128, 2, 8, 16], U32)
                r12f = T(k, p4, "r12f", [128, 2, 8, 16], F32)
                oh = T(k, p4, "oh", [128, 8, 16, 16], F32)
                gd = T(k, p4, "gd", [128, 8, 16], F32)
                gz = T(k, p4, "gz", [128, 16], F32)
                v12p = [v12, T(k, p4, "v12b", [128, 16, 16], F32)]
                i12p = [i12, T(k, p4, "i12b", [128, 16, 16], U32)]
                wk16p = [wk16, T(k, p4, "wk16b", [128, 16, 128], F32)]
                v12dp = [v12d, [k.dep("v12e%d" % i) for i in range(16)]]
                i12dp = [i12d, [k.dep("i12e%d" % i) for i in range(16)]]
                wkdp = [wkd, [k.dep("wke%d" % i) for i in range(16)]]

                qpT2 = [qpT, T(k, p4, "qpTb", [128, 16, 128], BF16)]

                def pework(ti):
                    xqt = xq[ti % 2]
                    sct = sc if ti % 2 == 0 else sc2
                    k.dma("sp", xqt.t[:], xn2d[ti].t, reads=[xn2d[ti].d], writes=[xqt.d])
                    for hp in range(16):
                        b = next_bank()
                        mm8(b, PB[b].t[:, 0:128], lambda c: wq.t[:, c, hp * 128:(hp + 1) * 128], lambda c: xqt.t[:, c, :], [wq.d, xqt.d])
                        k.op("act", lambda e: e.copy(qpT2[ti % 2].t[:, hp, :], PB[b].t[:, 0:128]), reads=[PB[b].d], writes=[qpT2[ti % 2].d])
                    qpT_ = qpT2[ti % 2]
                    for q4 in range(4):
                        b = next_bank()
                        for r in range(4):
                            hp = q4 * 4 + r
                            k.op("pe", lambda e: e.matmul(PB[b].t[:, r * 128:(r + 1) * 128], qpT_.t[:, hp, :], KTt.t[:, hp, :], start=True, stop=True),
                                 reads=[qpT_.d, KTt.d], writes=[PB[b].d], pe_accum=True)
                        k.op("act", lambda e: e.copy(sct.t[:, q4 * 4:(q4 + 1) * 4, :], PB[b].t[:, :].rearrange("p (r i) -> p r i", r=4)),
                             reads=[PB[b].d], writes=[sct.d])
                    for e1 in range(ti * 8, ti * 8 + 8):
                        convert_tile(e1)

                def head_ops(ti):
                    par = ti % 2
                    sct = sc if par == 0 else sc2
                    v12_, i12_, wk_ = v12p[par], i12p[par], wk16p[par]
                    vd, idp, wd = v12dp[par], i12dp[par], wkdp[par]
                    ops = []
                    for step in range(5):
                        for hp in range(16):
                            sv = sct.t[:, hp, :]
                            wv = wk_.t[:, hp, :]
                            if step == 0:
                                ops.append(lambda hp=hp, sv=sv: k.op("dve", lambda e: e.max(out=v12_.t[:, hp, 0:8], in_=sv), reads=[sct.d], writes=[vd[hp]]))
                            elif step == 1:
                                ops.append(lambda hp=hp, sv=sv: k.op("dve", lambda e: e.max_index(out=i12_.t[:, hp, 0:8], in_max=v12_.t[:, hp, 0:8], in_values=sv),
                                                                     reads=[sct.d, vd[hp]], writes=[idp[hp]]))
                            elif step == 2:
                                ops.append(lambda hp=hp, sv=sv, wv=wv: k.op("dve", lambda e: e.match_replace(out=wv, in_to_replace=v12_.t[:, hp, 0:8], in_values=sv, imm_value=-3.0e38),
                                                                            reads=[sct.d, vd[hp]], writes=[wd[hp]]))
                            elif step == 3:
                                ops.append(lambda hp=hp, wv=wv: k.op("dve", lambda e: e.max(out=v12_.t[:, hp, 8:16], in_=wv), reads=[wd[hp]], writes=[vd[hp]]))
                            else:
                                ops.append(lambda hp=hp, wv=wv: k.op("dve", lambda e: e.max_index(out=i12_.t[:, hp, 8:16], in_max=v12_.t[:, hp, 8:16], in_values=wv),
                                                                     reads=[wd[hp], vd[hp]], writes=[idp[hp]]))
                    return ops

                def tail_ops(ti):
                    par = ti % 2
                    v12_, i12_ = v12p[par], i12p[par]
                    vd, idp = v12dp[par], i12dp[par]
                    ops = []
                    ops.append(lambda: k.op("dve", lambda e: e.tensor_copy(i12f.t[:], i12_.t[:]), reads=idp, writes=[i12f.d]))
                    v4 = v12_.t[:].rearrange("p (h a) r -> p h a r", a=2)
                    i4 = i12f.t[:].rearrange("p (h a) r -> p h a r", a=2)
                    ops.append(lambda: k.op("dve", lambda e: e.tensor_tensor(cand.t[:], v4[:, :, 0, :, None].to_broadcast([128, 8, 16, 16]),
                                                                             v4[:, :, 1, None, :].to_broadcast([128, 8, 16, 16]), ALU.add), reads=vd, writes=[cand.d]))
                    for step in range(5):
                        for h in range(8):
                            cf = cand.t[:, h].rearrange("p a b -> p (a b)")
                            cwf = cw8.t[:, h, :]
                            if step == 0:
                                ops.append(lambda h=h, cf=cf: k.op("dve", lambda e: e.max(out=cs.t[:, h, 0:8], in_=cf), reads=[cand.d], writes=[csd[h]]))
                            elif step == 1:
                                ops.append(lambda h=h, cf=cf: k.op("dve", lambda e: e.max_index(out=ci.t[:, h, 0:8], in_max=cs.t[:, h, 0:8], in_values=cf),
                                                                   reads=[cand.d, csd[h]], writes=[cid[h]]))
                            elif step == 2:
                                ops.append(lambda h=h, cf=cf, cwf=cwf: k.op("dve", lambda e: e.match_replace(out=cwf, in_to_replace=cs.t[:, h, 0:8], in_values=cf, imm_value=-3.0e38),
                                                                            reads=[cand.d, csd[h]], writes=[cwd[h]]))
                            elif step == 3:
                                ops.append(lambda h=h, cwf=cwf: k.op("dve", lambda e: e.max(out=cs.t[:, h, 8:16], in_=cwf), reads=[cwd[h]], writes=[csd[h]]))
                            else:
                                ops.append(lambda h=h, cwf=cwf: k.op("dve", lambda e: e.max_index(out=ci.t[:, h, 8:16], in_max=cs.t[:, h, 8:16], in_values=cwf),
                                                                     reads=[cwd[h], csd[h]], writes=[cid[h]]))
                    ops.append(lambda: k.op("dve", lambda e: e.tensor_single_scalar(r12.t[:, 0], ci.t[:], 4, ALU.logical_shift_right), reads=cid, writes=[r12.d]))
                    ops.append(lambda: k.op("dve", lambda e: e.tensor_single_scalar(r12.t[:, 1], ci.t[:], 15, ALU.bitwise_and), reads=cid, writes=[r12.d]))
                    ops.append(lambda: k.op("dve", lambda e: e.tensor_copy(r12f.t[:], r12.t[:]), reads=[r12.d], writes=[r12f.d]))
                    for half, slot in ((0, slotA), (1, slotB)):
                        ops.append(lambda half=half: k.op("dve", lambda e: e.tensor_tensor(oh.t[:], iotaf.t[:, None, None, 0:16].to_broadcast([128, 8, 16, 16]),
                                                                                            r12f.t[:, half, :, :, None].to_broadcast([128, 8, 16, 16]), ALU.is_equal),
                                                          reads=[iotaf.d, r12f.d], writes=[oh.d]))
                        ops.append(lambda half=half: k.op("dve", lambda e: e.tensor_tensor(oh.t[:], oh.t[:], i4[:, :, half, None, :].to_broadcast([128, 8, 16, 16]), ALU.mult),
                                                          reads=[oh.d, i12f.d], writes=[oh.d]))
                        ops.append(lambda slot=slot: k.op("dve", lambda e: e.reduce_sum(slot.t[:, ti, :].rearrange("p (h k) -> p h k", h=8), oh.t[:], axis=AX.X),
                                                          reads=[oh.d], writes=[slot.d]))
                    ops.append(lambda: k.op("dve", lambda e: e.tensor_tensor(gd.t[:], cs.t[:], cs.t[:, :, 0:1].to_broadcast([128, 8, 16]), ALU.subtract), reads=csd, writes=[gd.d]))
                    ops.append(lambda: k.op("act", lambda e: e.activation(gd.t[:], gd.t[:], AF.Exp), reads=[gd.d], writes=[gd.d]))
                    ops.append(lambda: k.op("dve", lambda e: e.reduce_sum(gz.t[:, 0:8], gd.t[:], axis=AX.X), reads=[gd.d], writes=[gz.d]))
                    ops.append(lambda: k.op("dve", lambda e: e.reciprocal(gz.t[:, 8:16], gz.t[:, 0:8]), reads=[gz.d], writes=[gz.d]))
                    ops.append(lambda: k.op("dve", lambda e: e.tensor_tensor(slotG.t[:, ti, :].rearrange("p (h k) -> p h k", h=8), gd.t[:],
                                                                             gz.t[:, 8:16, None].to_broadcast([128, 8, 16]), ALU.mult), reads=[gd.d, gz.d], writes=[slotG.d]))
                    return ops

                pework(0)
                for o in head_ops(0):
                    o()
                for ti in range(NQT):
                    tl = tail_ops(ti)
                    hd = []
                    if ti + 1 < NQT:
                        pework(ti + 1)
                        hd = head_ops(ti + 1)
                    nh, nt = len(hd), len(tl)
                    hi = 0
                    for idx, o in enumerate(tl):
                        o()
                        want = (idx + 1) * nh // nt
                        while hi < want:
                            hd[hi]()
                            hi += 1
                    while hi < nh:
                        hd[hi]()
                        hi += 1
                dump("slotA", slotA, [128, NQT, 128], F32)
                dump("slotB", slotB, [128, NQT, 128], F32)
                dump("slotG", slotG, [128, NQT, 128], F32)
                k.barrier()

            with ExitStack() as p5:
                TBM = 384
                Gs = T(k, p5, "Gs", [128, TBM, 128], BF16)
                gfin = load_bcast(p5, "gfin", A["g_final"], D)
                x2 = T(k, p5, "x2", [128, D], F32)
                passes = [(0, 2), (2, 3), (5, 3), (8, 3), (11, 3), (14, 2)]
                for ps_, (tile0, ntile) in enumerate(passes):
                    TBp = ntile * 128
                    with ExitStack() as pg:
                        NCH = 4
                        CH = 128 // NCH
                        Aoh = [T(k, pg, "Aoh%d" % c, [128, CH, 128], BF16) for c in range(2)]
                        Boh = [T(k, pg, "Boh%d" % c, [128, CH, 128], BF16) for c in range(2)]
                        trs = [T(k, pg, "trs%d" % i, [128, 128], BF16) for i in range(2)]
                        trsg = T(k, pg, "trsg", [128, 128], F32)
                        g_rr = 0
                        c_rr = 0
                        for tt in range(ntile):
                            ti = tile0 + tt
                            for idx, slot in enumerate((slotA, slotB, slotG)):
                                k.op("pe", lambda e: e.transpose(PB[0].t[:, idx * 128:(idx + 1) * 128], slot.t[:, ti, :], ident.t[:]),
                                     reads=[slot.d, ident.d], writes=[PB[0].d], pe_accum=(idx > 0))
                            for idx in range(2):
                                k.op("act", lambda e: e.copy(trs[idx].t[:], PB[0].t[:, idx * 128:(idx + 1) * 128]), reads=[PB[0].d], writes=[trs[idx].d])
                            k.op("act", lambda e: e.copy(trsg.t[:], PB[0].t[:, 256:384]), reads=[PB[0].d], writes=[trsg.d])
                            ib = iotab.t[:, None, :].to_broadcast([128, CH, 128])
                            for ch in range(NCH):
                                Ac, Bc = Aoh[c_rr % 2], Boh[c_rr % 2]
                                c_rr += 1
                                tsl = slice(ch * CH, (ch + 1) * CH)
                                k.op("dve", lambda e: e.tensor_tensor(Ac.t[:], ib, trs[0].t[:, tsl, None].to_broadcast([128, CH, 128]), ALU.is_equal),
                                     reads=[iotab.d, trs[0].d], writes=[Ac.d])
                                for tq in range(CH):
                                    t_ = ch * CH + tq
                                    k.op("act", lambda e: e.activation(Ac.t[:, tq, :], Ac.t[:, tq, :], AF.Copy, scale=trsg.t[:, t_:t_ + 1]),
                                         reads=[Ac.d, trsg.d], writes=[Ac.d], same_ok=(tq > 0))
                                k.op("dve", lambda e: e.tensor_tensor(Bc.t[:], ib, trs[1].t[:, tsl, None].to_broadcast([128, CH, 128]), ALU.is_equal),
                                     reads=[iotab.d, trs[1].d], writes=[Bc.d])
                                for t4 in range(CH // 4):
                                    b = 1 + (g_rr % 7)
                                    g_rr += 1
                                    for r in range(4):
                                        tq = t4 * 4 + r
                                        k.op("pe", lambda e: e.matmul(PB[b].t[:, r * 128:(r + 1) * 128], Bc.t[:, tq, :], Ac.t[:, tq, :], start=True, stop=True),
                                             reads=[Ac.d, Bc.d], writes=[PB[b].d], pe_accum=(r > 0))
                                    tg = tt * 128 + ch * CH + t4 * 4
                                    k.op("act", lambda e: e.copy(Gs.t[:, tg:tg + 4, :], PB[b].t[:, :].rearrange("p (r i) -> p r i", r=4)),
                                         reads=[PB[b].d], writes=[Gs.d])
                        k.barrier()
                    if ps_ == 0:
                        dump("Gs", Gs, [128, TBM, 128], BF16)
                    with ExitStack() as pd:
                        NB = 4
                        uv = [T(k, pd, "uv%d" % i, [128, 2 * D], BF16) for i in range(NB)]
                        gl = [T(k, pd, "gl%d" % i, [128, TBM], F32) for i in range(NB)]
                        aT = [T(k, pd, "aT%d" % i, [128, TBM], BF16) for i in range(NB)]
                        xp = T(k, pd, "xp", [128, 8, TBM], BF16)
                        for tt in range(ntile):
                            ti = tile0 + tt
                            k.dma("sp", xp.t[:, :, tt * 128:(tt + 1) * 128], xn2d[ti].t, reads=[xn2d[ti].d], writes=[xp.d], join=True)
                        hbanks = [7, 0]

                        def stage2(e1):
                            jb = e1 % NB
                            bh = hbanks[e1 % len(hbanks)]
                            k.op("act", lambda e: e.activation(gl[jb].t[:, 0:TBp], PB[bh].t[:, 0:TBp], AF.Gelu), reads=[PB[bh].d], writes=[gl[jb].d])
                            k.op("dve", lambda e: e.tensor_tensor(aT[jb].t[:, 0:TBp], gl[jb].t[:, 0:TBp], Gs.t[:, 0:TBp, e1], ALU.mult),
                                 reads=[gl[jb].d, Gs.d], writes=[aT[jb].d])
                            for tt in range(ntile):
                                for dh in range(2):
                                    ba = 1 + tt * 2 + dh
                                    k.op("pe", lambda e: e.matmul(PB[ba].t[:, :], aT[jb].t[:, tt * 128:(tt + 1) * 128], uv[jb].t[:, D + dh * 512:D + (dh + 1) * 512],
                                                                  start=(e1 == 0), stop=(e1 == 127)),
                                         reads=[aT[jb].d, uv[jb].d], writes=[PB[ba].d], pe_accum=(e1 > 0))

                        for e1 in range(128):
                            j = e1 % 2
                            jb = e1 % NB
                            k.dma("sp", uv[jb].t[:], uvd[e1].t, reads=[uvd[e1].d], writes=[uv[jb].d])
                            bh = hbanks[e1 % len(hbanks)]
                            for c in range(8):
                                k.op("pe", lambda e: e.matmul(PB[bh].t[:, 0:TBp], uv[jb].t[:, c * 128:(c + 1) * 128], xp.t[:, c, 0:TBp], start=(c == 0), stop=(c == 7)),
                                     reads=[uv[jb].d, xp.d], writes=[PB[bh].d], pe_accum=(c > 0))
                            if len(hbanks) == 1:
                                stage2(e1)
                            elif e1 >= 1:
                                stage2(e1 - 1)
                        if len(hbanks) > 1:
                            stage2(127)
                        for tt in range(ntile):
                            ti = tile0 + tt
                            xi = xin[xin_i[0] % 2]
                            k.dma("sp", xi.t[:], x1d[ti].t, reads=[x1d[ti].d], writes=[xi.d])
                            for dh in range(2):
                                ba = 1 + tt * 2 + dh
                                k.op("dve", lambda e: e.tensor_tensor(x2.t[:, dh * 512:(dh + 1) * 512], xi.t[:, dh * 512:(dh + 1) * 512], PB[ba].t[:, :], ALU.add),
                                     reads=[xi.d, PB[ba].d], writes=[x2.d])
                            ns = nstat[xin_i[0] % 2]
                            xin_i[0] += 1
                            rms_scale(x2, ns)
                            k.op("dve", lambda e: e.scalar_tensor_tensor(x2.t[:], x2.t[:], ns.t[:, 1:2], gfin.t[:], ALU.mult, ALU.mult),
                                 reads=[x2.d, ns.d, gfin.d], writes=[x2.d])
                            od = DD(k, out[ti * 128:(ti + 1) * 128, :], "out%d" % ti)
                            k.dma("pool", od.t, x2.t[:], reads=[x2.d], writes=[od.d], anchor=x2.d)
                            outdeps.append(od.d)
                        k.barrier()
        k.finish(outdeps)
    return nc


_NC_CACHE = {}


def make_in_maps(inputs):
    x = np.ascontiguousarray(np.asarray(inputs["x"], dtype=np.float32))
    shared = {}
    for name, shp in W_SPECS:
        shared[name] = np.ascontiguousarray(np.asarray(inputs[name], dtype=np.float32).reshape(shp))
    in_maps = []
    for c in range(NCORES):
        b, half = c // 2, c % 2
        xc = np.zeros((LCTX, D), np.float32)
        if half == 1:
            xc[:] = x[b]
        else:
            xc[2048:] = x[b, 0:2048]
        m = dict(shared)
        m["xc"] = xc
        m.update(host_consts(half))
        in_maps.append(m)
    return in_maps


def kernel(**inputs):
    if "nc" not in _NC_CACHE:
        _NC_CACHE["nc"] = build()
    nc = _NC_CACHE["nc"]
    in_maps = make_in_maps(inputs)
    res = run_bass_kernel_spmd(nc, in_maps, core_ids=list(range(NCORES)))
    outp = np.zeros((4, 4096, D), np.float32)
    for c in range(NCORES):
        b, half = c // 2, c % 2
        outp[b, half * 2048:(half + 1) * 2048] = np.asarray(res.results[c]["out"], dtype=np.float32)
    return outp
```

```python
import numpy as np
import ml_dtypes
from contextlib import ExitStack
import concourse.bass as bass
import concourse.mybir as mybir
from concourse.bass_utils import run_bass_kernel_spmd


F32 = mybir.dt.float32
BF16 = mybir.dt.bfloat16
U32 = mybir.dt.uint32
I32 = mybir.dt.int32
ALU = mybir.AluOpType
AF = mybir.ActivationFunctionType
AX = mybir.AxisListType


class Dep:
    __slots__ = ("name", "w", "r", "dsem", "dcnt")

    def __init__(self, name=""):
        self.name = name
        self.w = None
        self.r = []
        self.dsem = None
        self.dcnt = 0


class K:
    def __init__(self, nc, stack):
        self.nc = nc
        self.stack = stack
        self.eng = {"pe": nc.tensor, "act": nc.scalar, "dve": nc.vector, "pool": nc.gpsimd, "sp": nc.sync}
        self.sems = {}
        self.cnt = {}
        for e in self.eng:
            self.sems[e] = stack.enter_context(nc.semaphore("c_" + e))
            self.cnt[e] = 0
        self.seen = {e: {} for e in self.eng}
        self.ndsem = 0
        self.dfree = {q: [] for q in self.eng}
        self.downers = []
        self.dtot = {}
        self.n_inst = 0

    def dep(self, name=""):
        return Dep(name)

    MAXD = 88

    def _dsem(self, d, e):
        if d.dsem is None:
            d.dsem = {}
        if e not in d.dsem:
            if self.dfree[e]:
                key = self.dfree[e].pop()
            else:
                if self.ndsem >= self.MAXD:
                    raise RuntimeError("out of DMA semaphores")
                key = "d%d" % self.ndsem
                self.ndsem += 1
                self.sems[key] = self.stack.enter_context(self.nc.semaphore(key))
                self.dtot[key] = 0
            d.dsem[e] = key
            self.downers.append((d, e))
        return d.dsem[e]

    def barrier(self):
        tot = dict(self.cnt)
        tot.update(self.dtot)
        for e, eng in self.eng.items():
            for key, v in tot.items():
                if key == e or v == 0:
                    continue
                if self.seen[e].get(key, 0) < v:
                    self.seen[e][key] = v
                    eng.wait_ge(self.sems[key], v)
        for d, q in self.downers:
            self.dfree[q].append(d.dsem.pop(q))
        self.downers = []

    def _need(self, e, reads, writes, pe_accum=False, same_ok=False):
        need = {}
        def add(p):
            if p is None:
                return
            k, v = p
            if same_ok and k == e:
                return
            if need.get(k, 0) < v:
                need[k] = v
        for d in reads:
            add(d.w)
        for d in writes:
            if not (pe_accum and d.w is not None and d.w[0] == "pe"):
                add(d.w)
            for p in d.r:
                add(p)
        out = []
        for k, v in need.items():
            if self.seen[e].get(k, 0) >= v:
                continue
            self.seen[e][k] = v
            out.append((k, v))
        return out

    def op(self, e, fn, reads=(), writes=(), pe_accum=False, same_ok=False):
        eng = self.eng[e]
        for k, v in self._need(e, reads, writes, pe_accum, same_ok):
            eng.wait_ge(self.sems[k], v)
        ins = fn(eng)
        self.cnt[e] += 1
        ins.then_inc(self.sems[e], 1)
        tag = (e, self.cnt[e])
        for d in writes:
            d.w = tag
            d.r = []
        for d in reads:
            d.r.append(tag)
        self.n_inst += 1
        return ins

    def dma(self, e, out, in_, reads=(), writes=(), join=False, anchor=None, **kw):
        eng = self.eng[e]
        saved = None
        if join and writes:
            saved = [(d, d.w) for d in writes]
            for d in writes:
                if d.w is not None and d.dsem and d.w[0] in d.dsem.values():
                    d.w = None
        for k, v in self._need(e, reads, writes):
            eng.wait_ge(self.sems[k], v)
        if anchor is None:
            anchor = writes[0] if writes else reads[0]
        key = self._dsem(anchor, e)
        self.dtot[key] += 16
        ins = eng.dma_start(out=out, in_=in_, **kw)
        ins.then_inc(self.sems[key], 16)
        tag = (key, self.dtot[key])
        for d in writes:
            d.w = tag
            d.r = []
        for d in reads:
            d.r.append(tag)
        self.n_inst += 1
        return ins

    def finish(self, deps):
        eng = self.eng["sp"]
        for d in deps:
            for p in ([d.w] if d.w else []) + d.r:
                if self.seen["sp"].get(p[0], 0) < p[1]:
                    self.seen["sp"][p[0]] = p[1]
                    eng.wait_ge(self.sems[p[0]], p[1])


BF = ml_dtypes.bfloat16
NCORES = 8
D = 1024
TOK = 2048
LCTX = 4096
NQT = 16
BIG8 = 240000.0
NEGF = -1.0e30
NW = 4376
C_Q, C_KC, C_VC, C_KS, C_VS, C_KW, C_VW, C_G, C_GLU, C_MG = 0, 512, 640, 768, 896, 1024, 1152, 1280, 1304, 2328
NA = 2328
TB = 256


class T:
    n = 0
    def __init__(self, k, st, name, shape, dt, psum=False):
        nc = k.nc
        T.n += 1
        name = "s%d_%s" % (T.n, name)
        self.t = st.enter_context(nc.psum_tensor(name, shape, dt) if psum else nc.sbuf_tensor(name, shape, dt))
        self.d = k.dep(name)


class DD:
    def __init__(self, k, ap, name=""):
        self.t = ap
        self.d = k.dep(name)


def host_consts(half):
    c = {}
    c["ident"] = np.eye(128, dtype=np.float32)
    p = np.arange(LCTX)
    kaug = np.zeros((5, LCTX), np.float32)
    kaug[0] = 128 * (p // 128); kaug[1] = p % 128; kaug[2] = 1; kaug[3] = 1
    kaug[4] = ((p < 2048) & (half == 0)).astype(np.float32)
    c["kaug"] = kaug.astype(BF)
    n = np.arange(256); end = 16 * n + 31
    caug = np.zeros((5, 256), np.float32)
    caug[0] = 128 * (end // 128); caug[1] = end % 128; caug[2] = 1; caug[3] = 1
    caug[4] = ((n == 255) | ((n < 128) & (half == 0))).astype(np.float32)
    c["caug"] = caug.astype(BF)
    qaug = np.zeros((5, 2, NQT, 4, 128), np.float32)
    for g in range(2):
        for hh in range(4):
            h = 4 * g + hh
            sp = 8.0 * 2.0 ** (-(h + 1))
            for i in range(NQT):
                qpos = 2048 + 128 * i + np.arange(128)
                qaug[0, g, i, hh] = sp; qaug[1, g, i, hh] = sp
                qaug[2, g, i, hh] = -sp * (128 * (qpos // 128)); qaug[3, g, i, hh] = -sp * (qpos % 128)
                qaug[4, g, i, hh] = -BIG8
    c["qaug"] = qaug.astype(BF)
    j = np.arange(64)
    c["expand"] = (p[None, :] // 64 == j[:, None]).astype(np.float32).astype(BF)
    st_ = 16 * n
    ovl = ((st_[:, None] <= 64 * j[None, :] + 63) & (end[:, None] >= 64 * j[None, :])).astype(np.float32)
    ovl[255] = 0
    c["ovl"] = ovl.astype(BF)
    keep = np.zeros((NQT, 128, 64), np.float32); add = np.zeros_like(keep); allow = np.zeros_like(keep)
    gb0 = 0 if half == 1 else 32
    for i in range(NQT):
        tblk = 32 + 2 * i + (np.arange(128) // 64)
        jj = j[None, :]; tb = tblk[:, None]
        valid = (jj >= 32) if half == 0 else np.ones_like(jj, bool)
        valid = np.broadcast_to(valid, (128, 64))
        forced = (jj == gb0) | (jj == tb) | ((jj == tb - 1) & valid)
        future = jj > tb
        bad = (future | ~valid) & ~forced
        keep[i] = (~forced & ~bad).astype(np.float32)
        add[i] = np.where(forced, 1e9, np.where(bad, NEGF, 0.0))
        allow[i] = (valid & ~future).astype(np.float32)
    c["selkeep"] = keep; c["seladd"] = add; c["selallow"] = allow
    c["iotaf"] = np.broadcast_to(np.arange(128, dtype=np.float32)[None, :], (128, 128)).copy()
    return c


CONST_SPECS = [("ident", [128, 128], "f"), ("kaug", [5, LCTX], "b"), ("caug", [5, 256], "b"),
               ("qaug", [5, 2, NQT, 4, 128], "b"), ("expand", [64, LCTX], "b"), ("ovl", [256, 64], "b"),
               ("selkeep", [NQT, 128, 64], "f"), ("seladd", [NQT, 128, 64], "f"), ("selallow", [NQT, 128, 64], "f"),
               ("iotaf", [128, 128], "f")]

W_SPECS = [("g_mix", [D]), ("w_in", [D, NW]), ("pe_cmp_k", [32, 64]), ("pe_cmp_v", [32, 64]),
           ("w_cmp_k1", [2048, 64]), ("w_cmp_k2", [64, 64]), ("w_cmp_v1", [2048, 64]), ("w_cmp_v2", [64, 64]),
           ("w_nsa_out", [512, D]), ("w_dw", [31, 512]), ("b_dw", [512]), ("g_conv_ln", [512]), ("b_conv_ln", [512]),
           ("w_conv_out", [512, D]), ("w_o", [D, D]), ("g_ffn", [D]), ("w_peer_q", [D, 2048]),
           ("peer_sub_keys", [16, 128, 128]), ("peer_u", [16384, D]), ("peer_v", [16384, D]), ("g_final", [D])]


def build(debug=()):
    nc = bass.Bass("TRN2", target_bir_lowering=False)
    A = {}
    A["xc"] = nc.dram_tensor("xc", [LCTX, D], F32, kind="ExternalInput").ap()
    for name, shp in W_SPECS:
        A[name] = nc.dram_tensor(name, shp, F32, kind="ExternalInput").ap()
    for name, shp, ty in CONST_SPECS:
        A[name] = nc.dram_tensor(name, shp, F32 if ty == "f" else BF16, kind="ExternalInput").ap()
    out = nc.dram_tensor("out", [TOK, D], F32, kind="ExternalOutput").ap()
    x1d_ap = nc.dram_tensor("x1d", [TOK, D], F32, kind="Internal").ap()
    sTd_ap = nc.dram_tensor("sTd", [128, 4, TOK], BF16, kind="Internal").ap()
    xn2Td_ap = nc.dram_tensor("xn2Td", [128, 8, TOK], BF16, kind="Internal").ap()
    uvd_ap = nc.dram_tensor("uvd", [128, 128, 2 * D], BF16, kind="Internal").ap()
    dbg_aps = {}

    with ExitStack() as st:
        k = K(nc, st)
        x1d = [DD(k, x1d_ap[i * 128:(i + 1) * 128, :], "x1d%d" % i) for i in range(NQT)]
        sTd = DD(k, sTd_ap, "sTd")
        outdeps = []
        PB = [T(k, st, "pb%d" % i, [128, 512], F32, psum=True) for i in range(8)]

        def pbf(i):
            return PB[i].t[:].bitcast(BF16)

        def dump(name, src, shape, dt):
            if name in debug:
                ap = nc.dram_tensor("dbg_" + name, shape, dt, kind="ExternalOutput").ap()
                dd = DD(k, ap, "dbg_" + name)
                k.dma("sp", ap, src.t[:], reads=[src.d], writes=[dd.d], anchor=src.d)
                outdeps.append(dd.d)

        ident = T(k, st, "ident", [128, 128], F32)
        identb = T(k, st, "identb", [128, 128], BF16)
        k.dma("sp", ident.t[:], A["ident"], writes=[ident.d])
        k.op("dve", lambda e: e.tensor_copy(identb.t[:], ident.t[:]), reads=[ident.d], writes=[identb.d])

        def load_vec_cols(stk, name, ap1d, ncol):
            t = T(k, stk, name, [128, ncol], F32)
            k.dma("sp", t.t[:], ap1d.rearrange("(c p) -> p c", p=128), writes=[t.d], allow_slow_non_contiguous=True)
            return t

        def load_bcast(stk, name, ap1d, n):
            t = T(k, stk, name, [128, n], F32)
            k.dma("sp", t.t[:], ap1d.partition_broadcast(128), writes=[t.d])
            return t

        NSTG = 2
        SW = 1024
        stg = [T(k, st, "stg%d" % i, [128, SW], F32) for i in range(NSTG)]
        stg_i = [0]
        cast_eng = ["dve", "act"]

        def load_w(dst_ap, dst_dep, src_ap, ncols, scale=None, npart=128):
            for c0 in range(0, ncols, SW):
                w_ = min(SW, ncols - c0)
                s = stg[stg_i[0] % NSTG]
                eng = cast_eng[stg_i[0] % 2]
                stg_i[0] += 1
                k.dma("sp", s.t[0:npart, 0:w_], src_ap[:, c0:c0 + w_], writes=[s.d])
                dv = dst_ap[:, c0:c0 + w_]
                if scale is not None:
                    sc, col = scale
                    if eng == "act":
                        k.op("act", lambda e: e.activation(dv, s.t[0:npart, 0:w_], AF.Copy, scale=sc.t[0:npart, col:col + 1]),
                             reads=[s.d, sc.d], writes=[dst_dep])
                    else:
                        k.op(eng, lambda e: e.tensor_scalar(dv, s.t[0:npart, 0:w_], sc.t[0:npart, col:col + 1], None, ALU.mult),
                             reads=[s.d, sc.d], writes=[dst_dep])
                elif eng == "act":
                    k.op("act", lambda e: e.copy(dv, s.t[0:npart, 0:w_]), reads=[s.d], writes=[dst_dep])
                else:
                    k.op(eng, lambda e: e.tensor_copy(dv, s.t[0:npart, 0:w_]), reads=[s.d], writes=[dst_dep])

        xin = [T(k, st, "xin%d" % i, [128, D], F32) for i in range(2)]
        xjunk = T(k, st, "xjunk", [128, D], BF16)
        xs_b = [T(k, st, "xsb%d" % i, [128, D], BF16) for i in range(2)]
        nstat = [T(k, st, "nstat%d" % i, [128, 4], F32) for i in range(2)]
        xin_i = [0]

        def rms_scale(src, dstat):
            k.op("act", lambda e: e.activation(xjunk.t[:], src.t[:], AF.Square, accum_out=dstat.t[:, 0:1]),
                 reads=[src.d], writes=[xjunk.d, dstat.d])
            k.op("dve", lambda e: e.tensor_scalar(dstat.t[:, 1:2], dstat.t[:, 0:1], 1.0 / D, 1e-6, ALU.mult, ALU.add),
                 reads=[dstat.d], writes=[dstat.d])
            k.op("act", lambda e: e.sqrt(dstat.t[:, 1:2], dstat.t[:, 1:2]), reads=[dstat.d], writes=[dstat.d])
            k.op("dve", lambda e: e.reciprocal(dstat.t[:, 1:2], dstat.t[:, 1:2]), reads=[dstat.d], writes=[dstat.d])

        def norm_transpose(src, dstT, col0, bank):
            norm_transpose_many([(src, dstT, col0)])

        def norm_transpose_many(items):
            slots = []
            for (src, dstT, col0) in items:
                j = xin_i[0] % 2
                xin_i[0] += 1
                slots.append((nstat[j], xs_b[j]))
            for (src, _, _), (ns, xb) in zip(items, slots):
                k.op("act", lambda e: e.activation(xjunk.t[:], src.t[:], AF.Square, accum_out=ns.t[:, 0:1]),
                     reads=[src.d], writes=[xjunk.d, ns.d])
            for (src, _, _), (ns, xb) in zip(items, slots):
                k.op("dve", lambda e: e.tensor_scalar(ns.t[:, 1:2], ns.t[:, 0:1], 1.0 / D, 1e-6, ALU.mult, ALU.add), reads=[ns.d], writes=[ns.d])
            for (src, _, _), (ns, xb) in zip(items, slots):
                k.op("act", lambda e: e.sqrt(ns.t[:, 1:2], ns.t[:, 1:2]), reads=[ns.d], writes=[ns.d])
            for (src, _, _), (ns, xb) in zip(items, slots):
                k.op("dve", lambda e: e.reciprocal(ns.t[:, 1:2], ns.t[:, 1:2]), reads=[ns.d], writes=[ns.d])
            for (src, _, _), (ns, xb) in zip(items, slots):
                k.op("dve", lambda e: e.tensor_scalar(xb.t[:], src.t[:], ns.t[:, 1:2], None, ALU.mult), reads=[src.d, ns.d], writes=[xb.d])
            banks = [0, 7]
            for idx, ((src, dstT, col0), (ns, xb)) in enumerate(zip(items, slots)):
                pv = pbf(banks[idx])
                for c in range(8):
                    k.op("pe", lambda e: e.transpose(pv[:, c * 128:(c + 1) * 128], xb.t[:, c * 128:(c + 1) * 128], identb.t[:]),
                         reads=[xb.d, identb.d], writes=[PB[banks[idx]].d], pe_accum=(c > 0))
            for idx, ((src, dstT, col0), (ns, xb)) in enumerate(zip(items, slots)):
                pv = pbf(banks[idx])
                dv = dstT.t[:, 0:8, col0:col0 + 128]
                sv = pv.rearrange("p (c t) -> p c t", c=8)
                if idx == 0:
                    k.op("act", lambda e: e.copy(dv, sv), reads=[PB[banks[idx]].d], writes=[dstT.d])
                else:
                    k.op("dve", lambda e: e.tensor_copy(dv, sv), reads=[PB[banks[idx]].d], writes=[dstT.d])

        def load_xn_block(xb, row0, ntile):
            for j0 in range(0, ntile, 2):
                items = []
                for j in range(j0, min(j0 + 2, ntile)):
                    xi = xin[(xin_i[0] + len(items)) % 2]
                    k.dma("sp", xi.t[:], A["xc"][row0 + j * 128:row0 + (j + 1) * 128, :], writes=[xi.d])
                    items.append((xi, xb, j * 128))
                norm_transpose_many(items)

        bank_rr = [0]

        def next_bank():
            b = 1 + (bank_rr[0] % 6)
            bank_rr[0] += 1
            return b

        ev_i = [0]

        def evac_copy(dst_ap, dst_dep, bank, src_ap, func=None):
            ev_i[0] += 1
            if func is not None:
                k.op("act", lambda e: e.activation(dst_ap, src_ap, func), reads=[PB[bank].d], writes=[dst_dep])
            elif ev_i[0] % 2:
                k.op("act", lambda e: e.copy(dst_ap, src_ap), reads=[PB[bank].d], writes=[dst_dep])
            else:
                k.op("dve", lambda e: e.tensor_copy(dst_ap, src_ap), reads=[PB[bank].d], writes=[dst_dep])

        def mm8(bank, out_ap, lhs_fn, rhs_fn, rdeps, n=8):
            for c in range(n):
                k.op("pe", lambda e: e.matmul(out_ap, lhs_fn(c), rhs_fn(c), start=(c == 0), stop=(c == n - 1)),
                     reads=rdeps, writes=[PB[bank].d], pe_accum=True)

        gmix = load_vec_cols(st, "gmix", A["g_mix"], 8)
        gffn = load_vec_cols(st, "gffn", A["g_ffn"], 8)
        iotaf = T(k, st, "iotaf", [128, 128], F32)
        k.dma("sp", iotaf.t[:], A["iotaf"], writes=[iotaf.d])
        iotab = T(k, st, "iotab", [128, 128], BF16)
        k.op("dve", lambda e: e.tensor_copy(iotab.t[:], iotaf.t[:]), reads=[iotaf.d], writes=[iotab.d])
        pX = st.enter_context(ExitStack())
        onsaT = T(k, pX, "onsaT", [128, 4, TOK], BF16)

        with ExitStack() as pc:
            wG = T(k, pc, "wG", [128, 8, 1024], BF16)
            for c in range(8):
                load_w(wG.t[:, c, :], wG.d, A["w_in"][c * 128:(c + 1) * 128, C_GLU:C_GLU + 1024], 1024, scale=(gmix, c))
            xnT = [T(k, pc, "xnTc%d" % i, [128, 8, 512], BF16) for i in range(2)]
            U = T(k, pc, "U", [128, 4, 128 + TOK], BF16)
            Cv = T(k, pc, "Cv", [128, 4, TOK], F32)
            sgt = T(k, pc, "sgt", [128, 512], F32)
            sTl = T(k, pc, "sTl", [128, 4, TOK], BF16)
            wdw = T(k, pc, "wdw", [128, 4, 31], F32)
            for ct in range(4):
                k.dma("sp", wdw.t[:, ct, :], A["w_dw"][:, ct * 128:(ct + 1) * 128].rearrange("k c -> c k"), writes=[wdw.d],
                      allow_slow_non_contiguous=True, join=True)
            bdw = load_vec_cols(pc, "bdw", A["b_dw"], 4)
            gln = load_bcast(pc, "gln", A["g_conv_ln"], 512)
            bln = load_bcast(pc, "bln", A["b_conv_ln"], 512)
            for bi, (row0, ntile, ucol) in enumerate([(1920, 1, 0)] + [(2048 + 512 * b, 4, 128 + 512 * b) for b in range(4)]):
                xb = xnT[bi % 2]
                load_xn_block(xb, row0, ntile)
                ntok = ntile * 128
                for ct in range(4):
                    ba = next_bank()
                    mm8(ba, PB[ba].t[:, 0:ntok], lambda c: wG.t[:, c, ct * 128:(ct + 1) * 128], lambda c: xb.t[:, c, 0:ntok], [wG.d, xb.d])
                    bb = next_bank()
                    mm8(bb, PB[bb].t[:, 0:ntok], lambda c: wG.t[:, c, 512 + ct * 128:512 + (ct + 1) * 128], lambda c: xb.t[:, c, 0:ntok], [wG.d, xb.d])
                    uv = U.t[:, ct, ucol:ucol + ntok]
                    k.op("act", lambda e: e.activation(sgt.t[:, 0:ntok], PB[bb].t[:, 0:ntok], AF.Sigmoid), reads=[PB[bb].d], writes=[sgt.d])
                    k.op("dve", lambda e: e.tensor_tensor(uv, sgt.t[:, 0:ntok], PB[ba].t[:, 0:ntok], ALU.mult), reads=[sgt.d, PB[ba].d], writes=[U.d])
            Dg = T(k, pc, "Dg", [128, 4, 31, 128], BF16)
            for ct in range(4):
                for kk in range(31):
                    k.op("dve", lambda e: e.tensor_scalar(Dg.t[:, ct, kk, :], identb.t[:, :], wdw.t[:, ct, kk:kk + 1], None, ALU.mult),
                         reads=[identb.d, wdw.d], writes=[Dg.d], same_ok=(ct + kk > 0))
            cdeps = [k.dep("cv%d" % ct) for ct in range(4)]
            for ct in range(4):
                for blk in range(4):
                    b = next_bank()
                    for kk in range(31):
                        c0 = 98 + kk + blk * 512
                        k.op("pe", lambda e: e.matmul(PB[b].t[:, :], Dg.t[:, ct, kk, :], U.t[:, ct, c0:c0 + 512], start=(kk == 0), stop=(kk == 30)),
                             reads=[Dg.d, U.d], writes=[PB[b].d], pe_accum=(kk > 0))
                    k.op("act", lambda e: e.activation(Cv.t[:, ct, blk * 512:(blk + 1) * 512], PB[b].t[:, :], AF.Identity, bias=bdw.t[:, ct:ct + 1]),
                         reads=[PB[b].d, bdw.d], writes=[cdeps[ct]], same_ok=(blk > 0))
            lnst = T(k, pc, "lnst", [128, 8], F32)
            un = T(k, pc, "un", [128, 512], F32)
            unb = T(k, pc, "unb", [128, 512], BF16)
            lnj = T(k, pc, "lnj", [128, 512], F32)
            for i in range(NQT):
                for ct in range(4):
                    k.op("pe", lambda e: e.transpose(PB[7].t[:, ct * 128:(ct + 1) * 128], Cv.t[:, ct, i * 128:(i + 1) * 128], ident.t[:]),
                         reads=[cdeps[ct], ident.d], writes=[PB[7].d], pe_accum=True)
                k.op("dve", lambda e: e.reduce_sum(lnst.t[:, 0:1], PB[7].t[:], axis=AX.X), reads=[PB[7].d], writes=[lnst.d])
                k.op("act", lambda e: e.activation(lnj.t[:], PB[7].t[:], AF.Square, accum_out=lnst.t[:, 1:2]), reads=[PB[7].d], writes=[lnj.d, lnst.d])
                k.op("dve", lambda e: e.tensor_scalar(lnst.t[:, 2:3], lnst.t[:, 0:1], 1.0 / 512, None, ALU.mult), reads=[lnst.d], writes=[lnst.d])
                k.op("dve", lambda e: e.tensor_tensor(lnst.t[:, 3:4], lnst.t[:, 2:3], lnst.t[:, 2:3], ALU.mult), reads=[lnst.d], writes=[lnst.d])
                k.op("dve", lambda e: e.scalar_tensor_tensor(lnst.t[:, 4:5], lnst.t[:, 1:2], 1.0 / 512, lnst.t[:, 3:4], ALU.mult, ALU.subtract),
                     reads=[lnst.d], writes=[lnst.d])
                k.op("dve", lambda e: e.tensor_scalar(lnst.t[:, 4:5], lnst.t[:, 4:5], 1e-6, None, ALU.add), reads=[lnst.d], writes=[lnst.d])
                k.op("act", lambda e: e.sqrt(lnst.t[:, 4:5], lnst.t[:, 4:5]), reads=[lnst.d], writes=[lnst.d])
                k.op("dve", lambda e: e.reciprocal(lnst.t[:, 4:5], lnst.t[:, 4:5]), reads=[lnst.d], writes=[lnst.d])
                k.op("dve", lambda e: e.tensor_scalar(un.t[:], PB[7].t[:], lnst.t[:, 2:3], lnst.t[:, 4:5], ALU.subtract, ALU.mult),
                     reads=[PB[7].d, lnst.d], writes=[un.d])
                k.op("dve", lambda e: e.tensor_tensor(un.t[:], un.t[:], gln.t[:], ALU.mult), reads=[un.d, gln.d], writes=[un.d])
                k.op("dve", lambda e: e.tensor_tensor(un.t[:], un.t[:], bln.t[:], ALU.add), reads=[un.d, bln.d], writes=[un.d])
                k.op("act", lambda e: e.activation(unb.t[:], un.t[:], AF.Silu), reads=[un.d], writes=[unb.d])
                pv = pbf(6)
                for ct in range(4):
                    k.op("pe", lambda e: e.transpose(pv[:, ct * 128:(ct + 1) * 128], unb.t[:, ct * 128:(ct + 1) * 128], identb.t[:]),
                         reads=[unb.d, identb.d], writes=[PB[6].d], pe_accum=True)
                k.op("act", lambda e: e.copy(sTl.t[:, :, i * 128:(i + 1) * 128], pv[:, 0:512].rearrange("p (c t) -> p c t", c=4)),
                     reads=[PB[6].d], writes=[sTl.d])
            dump("sT", sTl, [128, 4, TOK], BF16)
            k.dma("pool", sTd.t, sTl.t[:], reads=[sTl.d], writes=[sTd.d])
            k.barrier()
        with ExitStack() as p12:
            kT = {}
            for br in ("s", "w"):
                for g in range(2):
                    kT[br, g] = T(k, p12, "kT%s%d" % (br, g), [128, LCTX], BF16)
                    k.op("dve", lambda e: e.memset(kT[br, g].t[64:128, :], 0.0), writes=[kT[br, g].d])
                    k.dma("sp", kT[br, g].t[64:69, :], A["kaug"], writes=[kT[br, g].d])
            kcT = [T(k, p12, "kcT%d" % g, [128, 256], BF16) for g in range(2)]
            for g in range(2):
                k.op("dve", lambda e: e.memset(kcT[g].t[:, :], 0.0), writes=[kcT[g].d])
                k.dma("sp", kcT[g].t[64:69, :], A["caug"], writes=[kcT[g].d])
            qT = [T(k, p12, "qT%d" % g, [128, NQT, 512], BF16) for g in range(2)]
            for g in range(2):
                k.op("dve", lambda e: e.memset(qT[g].t[64:128, :, :], 0.0), writes=[qT[g].d])
                k.dma("sp", qT[g].t[64:69, :, :], A["qaug"][:, g].rearrange("r i h q -> r i (h q)"), writes=[qT[g].d])
            Vt = {br: T(k, p12, "V" + br, [128, 32, 2, 65], BF16) for br in ("s", "w")}
            for br in ("s", "w"):
                k.op("pool", lambda e: e.memset(Vt[br].t[:, :, :, 64:65], 1.0), writes=[Vt[br].d])
            Vc = T(k, p12, "Vc", [128, 2, 2, 65], BF16)
            k.op("pool", lambda e: e.memset(Vc.t[:, :, :, 64:65], 1.0), writes=[Vc.d])
            gts = T(k, p12, "gts", [128, NQT, 24], F32)
            expand = T(k, p12, "expand", [128, LCTX], BF16)
            k.op("dve", lambda e: e.memset(expand.t[64:128, :], 0.0), writes=[expand.d])
            k.dma("sp", expand.t[0:64, :], A["expand"], writes=[expand.d])
            ovl = T(k, p12, "ovl", [128, 2, 64], BF16)
            k.dma("sp", ovl.t[:], A["ovl"].rearrange("(c p) j -> p c j", p=128), writes=[ovl.d])

            with ExitStack() as p1:
                NA1 = 1304
                wA = T(k, p1, "wA", [128, 8, NA1], BF16)
                for c in range(8):
                    load_w(wA.t[:, c, :], wA.d, A["w_in"][c * 128:(c + 1) * 128, 0:NA1], NA1, scale=(gmix, c))
                wKV = T(k, p1, "wKV", [128, 8, 256], BF16)
                for c in range(8):
                    k.op("dve", lambda e: e.tensor_copy(wKV.t[:, c, :].rearrange("p (g a d) -> p a g d", g=2, a=2),
                                                        wA.t[:, c, C_KC:C_KC + 256].rearrange("p (a g d) -> p a g d", a=2, g=2)),
                         reads=[wA.d], writes=[wKV.d])
                cpT = [T(k, p1, "cpT%d" % g, [128, LCTX], BF16) for g in range(2)]
                xnT = [T(k, p1, "xnT%d" % i, [128, 8, 512], BF16) for i in range(2)]
                for blk in range(8):
                    xb = xnT[blk % 2]
                    own = blk >= 4
                    load_xn_block(xb, blk * 512, 4)
                    t0 = blk * 512
                    for g in range(2):
                        for br, c0 in (("s", C_KS), ("w", C_KW)):
                            b = next_bank()
                            mm8(b, PB[b].t[0:64, :], lambda c: wA.t[:, c, c0 + 64 * g:c0 + 64 * g + 64], lambda c: xb.t[:, c, :], [wA.d, xb.d])
                            evac_copy(kT[br, g].t[0:64, t0:t0 + 512], kT[br, g].d, b, PB[b].t[0:64, :])
                        b = next_bank()
                        mm8(b, PB[b].t[:, :], lambda c: wKV.t[:, c, g * 128:(g + 1) * 128], lambda c: xb.t[:, c, :], [wKV.d, xb.d])
                        evac_copy(cpT[g].t[:, t0:t0 + 512], cpT[g].d, b, PB[b].t[:, :])
                    for j in range(4):
                        tile_i = blk * 4 + j
                        for br, c0 in (("s", C_VS), ("w", C_VW)):
                            b = next_bank()
                            mm8(b, PB[b].t[:, 0:128], lambda c: xb.t[:, c, j * 128:(j + 1) * 128], lambda c: wA.t[:, c, c0:c0 + 128], [wA.d, xb.d])
                            evac_copy(Vt[br].t[:, tile_i, :, 0:64], Vt[br].d, b, PB[b].t[:, 0:128].rearrange("p (g d) -> p g d", g=2))
                    if own:
                        ob = blk - 4
                        for g in range(2):
                            for hh in range(4):
                                h = 4 * g + hh
                                b = next_bank()
                                mm8(b, PB[b].t[0:64, :], lambda c: wA.t[:, c, C_Q + 64 * h:C_Q + 64 * h + 64], lambda c: xb.t[:, c, :], [wA.d, xb.d])
                                evac_copy(qT[g].t[0:64, ob * 4:(ob + 1) * 4, hh * 128:(hh + 1) * 128], qT[g].d, b, PB[b].t[0:64, :].rearrange("p (i q) -> p i q", i=4))
                        for j in range(4):
                            b = next_bank()
                            mm8(b, PB[b].t[:, 0:24], lambda c: xb.t[:, c, j * 128:(j + 1) * 128], lambda c: wA.t[:, c, C_G:C_G + 24], [wA.d, xb.d])
                            evac_copy(gts.t[:, ob * 4 + j, :], gts.d, b, PB[b].t[:, 0:24], func=AF.Sigmoid)

                w1b = T(k, p1, "w1b", [128, 32, 64], BF16)
                w2b = T(k, p1, "w2b", [64, 64], BF16)
                w2v0 = T(k, p1, "w2v0", [64, 64], BF16)
                peT = T(k, p1, "peT", [128, 32], F32)
                peTb = T(k, p1, "peTb", [128, 32], BF16)
                cbias = {kv: T(k, p1, "cbias" + kv, [64, 1], F32) for kv in "kv"}
                h1T = T(k, p1, "h1T", [64, 256], BF16)
                k.op("pool", lambda e: e.memset(h1T.t[:], 0.0), writes=[h1T.d])
                for kv, p0 in (("k", 0), ("v", 64)):
                    src1 = A["w_cmp_%s1" % kv].rearrange("(l d) j -> d l j", d=64)
                    for lh in range(2):
                        s = stg[stg_i[0] % NSTG]; stg_i[0] += 1
                        sv = s.t[p0:p0 + 64, 0:1024].rearrange("p (l j) -> p l j", l=16)
                        k.dma("sp", sv, src1[:, lh * 16:(lh + 1) * 16, :], writes=[s.d])
                        k.op("dve", lambda e: e.tensor_copy(w1b.t[p0:p0 + 64, lh * 16:(lh + 1) * 16, :], sv), reads=[s.d], writes=[w1b.d])
                    s = stg[stg_i[0] % NSTG]; stg_i[0] += 1
                    w2dst = w2b if kv == "k" else w2v0
                    k.dma("sp", s.t[0:64, 0:64], A["w_cmp_%s2" % kv], writes=[s.d])
                    k.op("dve", lambda e: e.tensor_copy(w2dst.t[0:64, :], s.t[0:64, 0:64]), reads=[s.d], writes=[w2dst.d])
                    k.dma("sp", peT.t[p0:p0 + 64, :], A["pe_cmp_" + kv].rearrange("l d -> d l"), writes=[peT.d], allow_slow_non_contiguous=True)
                    k.op("dve", lambda e: e.tensor_copy(peTb.t[p0:p0 + 64, :], peT.t[p0:p0 + 64, :]), reads=[peT.d], writes=[peTb.d])
                    b = next_bank()
                    for l in range(32):
                        k.op("pe", lambda e: e.matmul(PB[b].t[0:64, 0:1], w1b.t[p0:p0 + 64, l, :], peTb.t[p0:p0 + 64, l:l + 1],
                                                      start=(l == 0), stop=(l == 31)),
                             reads=[w1b.d, peTb.d], writes=[PB[b].d], pe_accum=True)
                    k.op("dve", lambda e: e.tensor_copy(cbias[kv].t[:], PB[b].t[0:64, 0:1]), reads=[PB[b].d], writes=[cbias[kv].d])
                for g in range(2):
                    for kv, p0 in (("k", 0), ("v", 64)):
                        src = cpT[g].t[p0:p0 + 64, :].rearrange("p (n s) -> p n s", s=16)
                        b = next_bank()
                        for l in range(32):
                            a_, b_ = l // 16, l % 16
                            k.op("pe", lambda e: e.matmul(PB[b].t[0:64, 0:255], w1b.t[p0:p0 + 64, l, :], src[:, a_:a_ + 255, b_],
                                                          start=(l == 0), stop=(l == 31)),
                                 reads=[w1b.d, cpT[g].d], writes=[PB[b].d], pe_accum=True)
                        k.op("act", lambda e: e.activation(h1T.t[:, 0:255], PB[b].t[0:64, 0:255], AF.Gelu, bias=cbias[kv].t[:, 0:1]),
                             reads=[PB[b].d, cbias[kv].d], writes=[h1T.d])
                        if kv == "k":
                            b2 = next_bank()
                            k.op("pe", lambda e: e.matmul(PB[b2].t[0:64, 0:255], w2b.t[0:64, :], h1T.t[:, 0:255], start=True, stop=True),
                                 reads=[w2b.d, h1T.d], writes=[PB[b2].d])
                            k.op("dve", lambda e: e.tensor_copy(kcT[g].t[0:64, 0:255], PB[b2].t[0:64, 0:255]), reads=[PB[b2].d], writes=[kcT[g].d])
                        else:
                            for ch in range(2):
                                b2 = next_bank()
                                k.op("pe", lambda e: e.matmul(PB[b2].t[:, 0:64], h1T.t[:, ch * 128:(ch + 1) * 128], w2v0.t[:], start=True, stop=True),
                                     reads=[w2v0.d, h1T.d], writes=[PB[b2].d])
                                k.op("dve", lambda e: e.tensor_copy(Vc.t[:, ch, g, 0:64], PB[b2].t[:, 0:64]), reads=[PB[b2].d], writes=[Vc.d])
                    if g == 0:
                        pass
                dump("kcT", kcT[0], [128, 256], BF16)
                dump("qT", qT[0], [128, NQT, 512], BF16)
                dump("Vs", Vt["s"], [128, 32, 2, 65], BF16)
                k.barrier()
            with ExitStack() as pa:
                Pt = [T(k, pa, "Pt%d" % i, [128, 512], BF16) for i in range(3)]
                selc = {}
                for nm in ("selkeep", "seladd", "selallow"):
                    selc[nm] = T(k, pa, nm, [128, NQT, 64], F32)
                    k.dma("sp", selc[nm].t[:], A[nm].rearrange("i p j -> p i j"), writes=[selc[nm].d])
                negmT = T(k, pa, "negmT", [128, 512], BF16)
                k.op("dve", lambda e: e.memset(negmT.t[64:128, :], 0.0), writes=[negmT.d])
                impq = T(k, pa, "impq", [128, 64], F32)
                impw = T(k, pa, "impw", [128, 64], F32)
                selm = T(k, pa, "selm", [128, 64], F32)
                m8 = T(k, pa, "m8", [128, 16], F32)
                rz = T(k, pa, "rz", [128, 4], F32)
                coef = T(k, pa, "coef", [128, 4], F32)
                oacc = T(k, pa, "oacc", [128, 2, 4, 64], F32)
                otmp = T(k, pa, "otmp", [128, 4, 64], F32)
                onb = T(k, pa, "onb", [128, 512], BF16)
                s_rr = [0]

                def score_tile(lhs_ap, lhs_dep, g, i, extra=None, mrows=128):
                    b = 1 + (s_rr[0] % 3)
                    pt = Pt[s_rr[0] % 3]
                    s_rr[0] += 1
                    k.op("pe", lambda e: e.matmul(PB[b].t[0:mrows, :], lhs_ap, qT[g].t[0:128, i, :], start=True, stop=(extra is None)),
                         reads=[lhs_dep, qT[g].d], writes=[PB[b].d])
                    if extra is not None:
                        k.op("pe", lambda e: e.matmul(PB[b].t[:, :], extra, negmT.t[:, :], start=False, stop=True),
                             reads=[expand.d, negmT.d], writes=[PB[b].d], pe_accum=True)
                    k.op("act", lambda e: e.activation(pt.t[0:mrows, :], PB[b].t[0:mrows, :], AF.Exp, scale=0.125), reads=[PB[b].d], writes=[pt.d])
                    return pt

                def mask_tile(pt, base, cm, qstep, mrows=128):
                    pv = pt.t[0:mrows, :].rearrange("p (h q) -> p h q", h=4)
                    k.op("pool", lambda e: e.affine_select(out=pv, in_=pv, pattern=[[0, 4], [qstep, 128]], compare_op=ALU.is_ge,
                                                           fill=0.0, base=base, channel_multiplier=cm),
                         reads=[pt.d], writes=[pt.d])

                zz = T(k, pa, "zz", [1, 512], BF16)
                k.op("pool", lambda e: e.memset(zz.t[:], 0.0), writes=[zz.d])

                def zero_bank(bank):
                    k.op("pe", lambda e: e.matmul(PB[bank].t[:, :], zz.t[0:1, 0:128], zz.t[0:1, :], start=True, stop=False),
                         reads=[zz.d], writes=[PB[bank].d])

                def pv_acc(pt, bank, v_ap, v_dep, first, last, mrows=128):
                    if first:
                        zero_bank(bank)
                    for hh in range(4):
                        k.op("pe", lambda e: e.matmul(PB[bank].t[:, hh * 65:(hh + 1) * 65], pt.t[0:mrows, hh * 128:(hh + 1) * 128], v_ap,
                                                      start=False, stop=(last and hh == 3)),
                             reads=[pt.d, v_dep], writes=[PB[bank].d], pe_accum=True)

                def finish_branch(bank, br_idx, g, i, first):
                    pov = PB[bank].t[:, 0:260].rearrange("p (h e) -> p h e", e=65)
                    k.op("dve", lambda e: e.tensor_scalar(rz.t[:], pov[:, :, 64], 1e-30, None, ALU.max), reads=[PB[bank].d], writes=[rz.d])
                    k.op("dve", lambda e: e.reciprocal(rz.t[:], rz.t[:]), reads=[rz.d], writes=[rz.d])
                    gsl = gts.t[:, i, br_idx * 8 + 4 * g:br_idx * 8 + 4 * g + 4]
                    k.op("dve", lambda e: e.tensor_tensor(coef.t[:], rz.t[:], gsl, ALU.mult), reads=[rz.d, gts.d], writes=[coef.d])
                    cb = coef.t[:, :, None].to_broadcast([128, 4, 64])
                    if first:
                        k.op("dve", lambda e: e.tensor_tensor(oacc.t[:, g, :, :], pov[:, :, 0:64], cb, ALU.mult),
                             reads=[PB[bank].d, coef.d], writes=[oacc.d])
                    else:
                        k.op("dve", lambda e: e.tensor_tensor(otmp.t[:], pov[:, :, 0:64], cb, ALU.mult),
                             reads=[PB[bank].d, coef.d], writes=[otmp.d])
                        k.op("dve", lambda e: e.tensor_tensor(oacc.t[:, g, :, :], oacc.t[:, g, :, :], otmp.t[:], ALU.add),
                             reads=[otmp.d, oacc.d], writes=[oacc.d])

                for i in range(NQT):
                    for g in range(2):
                        pts = []
                        mrows = [128, min(128, 8 * i + 7)]
                        for kt in range(2):
                            M = mrows[kt]
                            pt = score_tile(kcT[g].t[0:128, kt * 128:kt * 128 + M], kcT[g].d, g, i, mrows=M)
                            mask_tile(pt, 2017 + 128 * i - 2048 * kt, -16, 1, mrows=M)
                            pts.append(pt)
                        for kt in range(2):
                            M = mrows[kt]
                            pv_acc(pts[kt], 4, Vc.t[0:M, kt, g, :], Vc.d, kt == 0, kt == 1, mrows=M)
                            if kt == 0:
                                zero_bank(5)
                            for hh in range(4):
                                k.op("pe", lambda e: e.matmul(PB[5].t[:, hh * 64:(hh + 1) * 64], pts[kt].t[0:M, hh * 128:(hh + 1) * 128], ovl.t[0:M, kt, :],
                                                              start=False, stop=(kt == 1 and hh == 3)),
                                     reads=[pts[kt].d, ovl.d], writes=[PB[5].d], pe_accum=True)
                        finish_branch(4, 0, g, i, True)
                        for hh in range(4):
                            if hh == 0:
                                k.op("dve", lambda e: e.tensor_scalar(impq.t[:], PB[5].t[:, 0:64], rz.t[:, 0:1], None, ALU.mult),
                                     reads=[PB[5].d, rz.d], writes=[impq.d])
                            else:
                                k.op("dve", lambda e: e.scalar_tensor_tensor(impq.t[:], PB[5].t[:, hh * 64:(hh + 1) * 64], rz.t[:, hh:hh + 1], impq.t[:],
                                                                             ALU.mult, ALU.add),
                                     reads=[PB[5].d, rz.d, impq.d], writes=[impq.d])
                        k.op("dve", lambda e: e.tensor_tensor(impq.t[:], impq.t[:], selc["selkeep"].t[:, i, :], ALU.mult),
                             reads=[impq.d, selc["selkeep"].d], writes=[impq.d])
                        k.op("dve", lambda e: e.tensor_tensor(impq.t[:], impq.t[:], selc["seladd"].t[:, i, :], ALU.add),
                             reads=[impq.d, selc["seladd"].d], writes=[impq.d])
                        if i == 3 and g == 1:
                            dump("impq", impq, [128, 64], F32)
                        k.op("dve", lambda e: e.max(out=m8.t[:, 0:8], in_=impq.t[:]), reads=[impq.d], writes=[m8.d])
                        k.op("dve", lambda e: e.match_replace(out=impw.t[:], in_to_replace=m8.t[:, 0:8], in_values=impq.t[:], imm_value=-3.0e38),
                             reads=[impq.d, m8.d], writes=[impw.d])
                        k.op("dve", lambda e: e.max(out=m8.t[:, 8:16], in_=impw.t[:]), reads=[impw.d], writes=[m8.d])
                        k.op("dve", lambda e: e.tensor_scalar(selm.t[:], impq.t[:], m8.t[:, 15:16], None, ALU.is_ge),
                             reads=[impq.d, m8.d], writes=[selm.d])
                        k.op("dve", lambda e: e.tensor_tensor(selm.t[:], selm.t[:], selc["selallow"].t[:, i, :], ALU.mult),
                             reads=[selm.d, selc["selallow"].d], writes=[selm.d])
                        k.op("dve", lambda e: e.tensor_scalar(selm.t[:], selm.t[:], -1.0, BIG8, ALU.add, ALU.mult), reads=[selm.d], writes=[selm.d])
                        pend = None
                        for r, kt in enumerate(range(12 + i, 17 + i)):
                            pt = score_tile(kT["w", g].t[0:128, kt * 128:(kt + 1) * 128], kT["w", g].d, g, i)
                            if r == 4:
                                mask_tile(pt, 0, -1, 1)
                            elif r == 0:
                                mask_tile(pt, -1, 1, -1)
                            if pend is not None:
                                pv_acc(pend[0], 7, Vt["w"].t[:, pend[1], g, :], Vt["w"].d, pend[2] == 0, False)
                            pend = (pt, kt, r)
                        pv_acc(pend[0], 7, Vt["w"].t[:, pend[1], g, :], Vt["w"].d, pend[2] == 0, True)
                        finish_branch(7, 2, g, i, False)
                        k.op("pe", lambda e: e.transpose(PB[0].t[0:64, 0:128], selm.t[:], ident.t[:]), reads=[selm.d, ident.d], writes=[PB[0].d])
                        k.op("act", lambda e: e.copy(negmT.t[0:64, :].rearrange("p (h q) -> p h q", h=4), PB[0].t[0:64, 0:128][:, None, :].to_broadcast([64, 4, 128])),
                             reads=[PB[0].d], writes=[negmT.d])
                        nk = 17 + i
                        pend = None
                        for kt in range(nk):
                            pt = score_tile(kT["s", g].t[0:128, kt * 128:(kt + 1) * 128], kT["s", g].d, g, i,
                                            extra=expand.t[0:128, kt * 128:(kt + 1) * 128])
                            if kt == nk - 1:
                                mask_tile(pt, 0, -1, 1)
                            if pend is not None:
                                pv_acc(pend[0], 6, Vt["s"].t[:, pend[1], g, :], Vt["s"].d, pend[1] == 0, False)
                            pend = (pt, kt)
                        pv_acc(pend[0], 6, Vt["s"].t[:, pend[1], g, :], Vt["s"].d, pend[1] == 0, True)
                        finish_branch(6, 1, g, i, False)
                    k.op("act", lambda e: e.copy(onb.t[:], oacc.t[:].rearrange("p g h d -> p (g h d)")), reads=[oacc.d], writes=[onb.d])
                    pv0 = pbf(0)
                    for c in range(4):
                        k.op("pe", lambda e: e.transpose(pv0[:, c * 128:(c + 1) * 128], onb.t[:, c * 128:(c + 1) * 128], identb.t[:]),
                             reads=[onb.d, identb.d], writes=[PB[0].d], pe_accum=True)
                    k.op("act", lambda e: e.copy(onsaT.t[:, :, i * 128:(i + 1) * 128], pv0[:, 0:512].rearrange("p (c t) -> p c t", c=4)),
                         reads=[PB[0].d], writes=[onsaT.d])
                dump("onsaT", onsaT, [128, 4, TOK], BF16)
                k.barrier()
        xn2d = [DD(k, xn2Td_ap[:, :, i * 128:(i + 1) * 128], "xn2d%d" % i) for i in range(NQT)]
        with ExitStack() as p3:
            wM = T(k, p3, "wM", [128, 8, 2048], BF16)
            wo = T(k, p3, "wo", [128, 8, D], BF16)
            wno = T(k, p3, "wno", [128, 4, D], BF16)
            wco = T(k, p3, "wco", [128, 4, D], BF16)
            for c in range(8):
                load_w(wM.t[:, c, :], wM.d, A["w_in"][c * 128:(c + 1) * 128, C_MG:C_MG + 2048], 2048, scale=(gmix, c))
                load_w(wo.t[:, c, :], wo.d, A["w_o"][c * 128:(c + 1) * 128, :], D)
            for c in range(4):
                load_w(wno.t[:, c, :], wno.d, A["w_nsa_out"][c * 128:(c + 1) * 128, :], D)
                load_w(wco.t[:, c, :], wco.d, A["w_conv_out"][c * 128:(c + 1) * 128, :], D)
            xnTb = T(k, p3, "xnTb", [128, 8, 512], BF16)
            sTb = T(k, p3, "sTb", [128, 4, 512], BF16)
            sg = T(k, p3, "sg", [128, 2, 512], F32)
            mT = T(k, p3, "mT", [128, 8, 512], BF16)
            t1 = T(k, p3, "t1", [128, 512], F32)
            t2 = T(k, p3, "t2", [128, 512], F32)
            x1t = [T(k, p3, "x1t%d" % i, [128, D], F32) for i in range(2)]
            xn2l = [T(k, p3, "xn2l%d" % i, [128, 8, 128], BF16) for i in range(2)]
            for blk in range(4):
                load_xn_block(xnTb, 2048 + 512 * blk, 4)
                k.dma("sp", sTb.t[:], sTd.t[:, :, blk * 512:(blk + 1) * 512], reads=[sTd.d], writes=[sTb.d])
                for m in range(8):
                    ba = next_bank()
                    mm8(ba, PB[ba].t[:, :], lambda c: wM.t[:, c, m * 128:(m + 1) * 128], lambda c: xnTb.t[:, c, :], [wM.d, xnTb.d])
                    k.op("act", lambda e: e.activation(sg.t[:, 0, :], PB[ba].t[:, :], AF.Sigmoid), reads=[PB[ba].d], writes=[sg.d])
                    bb = next_bank()
                    mm8(bb, PB[bb].t[:, :], lambda c: wM.t[:, c, 1024 + m * 128:1024 + (m + 1) * 128], lambda c: xnTb.t[:, c, :], [wM.d, xnTb.d])
                    k.op("act", lambda e: e.activation(sg.t[:, 1, :], PB[bb].t[:, :], AF.Sigmoid), reads=[PB[bb].d], writes=[sg.d])
                    by = next_bank()
                    mm8(by, PB[by].t[:, :], lambda c: wno.t[:, c, m * 128:(m + 1) * 128], lambda c: onsaT.t[:, c, blk * 512:(blk + 1) * 512],
                        [wno.d, onsaT.d], n=4)
                    bz = next_bank()
                    mm8(bz, PB[bz].t[:, :], lambda c: wco.t[:, c, m * 128:(m + 1) * 128], lambda c: sTb.t[:, c, :], [wco.d, sTb.d], n=4)
                    k.op("dve", lambda e: e.tensor_tensor(t1.t[:], sg.t[:, 0, :], PB[by].t[:, :], ALU.mult), reads=[sg.d, PB[by].d], writes=[t1.d])
                    k.op("dve", lambda e: e.tensor_tensor(t2.t[:], sg.t[:, 1, :], PB[bz].t[:, :], ALU.mult), reads=[sg.d, PB[bz].d], writes=[t2.d])
                    k.op("dve", lambda e: e.tensor_tensor(mT.t[:, m, :], t1.t[:], t2.t[:], ALU.add), reads=[t1.d, t2.d], writes=[mT.d])
                for jp in range(0, 4, 2):
                    items = []
                    for jj in range(2):
                        j = jp + jj
                        tile_i = blk * 4 + j
                        xi = xin[(xin_i[0] + jj) % 2]
                        xt = x1t[jj]
                        r0 = 2048 + tile_i * 128
                        k.dma("sp", xi.t[:], A["xc"][r0:r0 + 128, :], writes=[xi.d])
                        for dh in range(2):
                            b = next_bank()
                            mm8(b, PB[b].t[:, :], lambda c: mT.t[:, c, j * 128:(j + 1) * 128], lambda c: wo.t[:, c, dh * 512:(dh + 1) * 512], [mT.d, wo.d])
                            k.op("dve", lambda e: e.tensor_tensor(xt.t[:, dh * 512:(dh + 1) * 512], xi.t[:, dh * 512:(dh + 1) * 512], PB[b].t[:, :], ALU.add),
                                 reads=[xi.d, PB[b].d], writes=[xt.d])
                        k.dma("pool", x1d[tile_i].t, xt.t[:], reads=[xt.d], writes=[x1d[tile_i].d], anchor=xt.d)
                        items.append((xt, xn2l[jj], 0))
                    norm_transpose_many(items)
                    for jj in range(2):
                        tile_i = blk * 4 + jp + jj
                        k.dma("pool", xn2d[tile_i].t, xn2l[jj].t[:], reads=[xn2l[jj].d], writes=[xn2d[tile_i].d], anchor=xn2l[jj].d)
            if "x1" in debug:
                pass
            k.barrier()
        pX.close()
        with ExitStack() as pp:
            slotA = T(k, pp, "slotA", [128, NQT, 128], F32)
            slotB = T(k, pp, "slotB", [128, NQT, 128], F32)
            slotG = T(k, pp, "slotG", [128, NQT, 128], F32)
            uvd = [DD(k, uvd_ap[e1], "uvd%d" % e1) for e1 in range(128)]
            with ExitStack() as p4:
                cus = [T(k, p4, "cus%d" % i, [128, D], F32) for i in range(2)]
                cvs = [T(k, p4, "cvs%d" % i, [128, D], F32) for i in range(2)]
                cub = [T(k, p4, "cub%d" % i, [128, D], BF16) for i in range(2)]
                cuv = [T(k, p4, "cuv%d" % i, [128, 2 * D], BF16) for i in range(2)]

                def convert_tile(e1):
                    j = e1 % 2
                    k.dma("sp", cus[j].t[:], A["peer_u"][e1 * 128:(e1 + 1) * 128, :], writes=[cus[j].d])
                    k.dma("sp", cvs[j].t[:], A["peer_v"][e1 * 128:(e1 + 1) * 128, :], writes=[cvs[j].d])
                    k.op("act", lambda e: e.copy(cub[j].t[:], cus[j].t[:]), reads=[cus[j].d], writes=[cub[j].d])
                    k.op("pool", lambda e: e.tensor_copy(cuv[j].t[:, D:2 * D], cvs[j].t[:]), reads=[cvs[j].d], writes=[cuv[j].d])
                    pv = pbf(7)
                    for c in range(8):
                        k.op("pe", lambda e: e.transpose(pv[:, c * 128:(c + 1) * 128], cub[j].t[:, c * 128:(c + 1) * 128], identb.t[:]),
                             reads=[cub[j].d, identb.d], writes=[PB[7].d], pe_accum=(c > 0))
                    k.op("act", lambda e: e.copy(cuv[j].t[:, 0:D], pv), reads=[PB[7].d], writes=[cuv[j].d])
                    k.dma("pool", uvd[e1].t, cuv[j].t[:], reads=[cuv[j].d], writes=[uvd[e1].d], anchor=cuv[j].d)

                wq = T(k, p4, "wq", [128, 8, 2048], BF16)
                for c in range(8):
                    load_w(wq.t[:, c, :], wq.d, A["w_peer_q"][c * 128:(c + 1) * 128, :], 2048, scale=(gffn, c))
                KTt = T(k, p4, "KTt", [128, 16, 128], BF16)
                skb = T(k, p4, "skb", [128, 128], BF16)
                for hp in range(16):
                    s = stg[stg_i[0] % NSTG]; stg_i[0] += 1
                    k.dma("sp", s.t[:, 0:128], A["peer_sub_keys"][hp], writes=[s.d])
                    k.op("dve", lambda e: e.tensor_copy(skb.t[:], s.t[:, 0:128]), reads=[s.d], writes=[skb.d])
                    pv = pbf(0)
                    k.op("pe", lambda e: e.transpose(pv[:, 0:128], skb.t[:], identb.t[:]), reads=[skb.d, identb.d], writes=[PB[0].d])
                    k.op("act", lambda e: e.copy(KTt.t[:, hp, :], pv[:, 0:128]), reads=[PB[0].d], writes=[KTt.d])
                xq = [T(k, p4, "xq%d" % i, [128, 8, 128], BF16) for i in range(2)]
                qpT = T(k, p4, "qpT", [128, 16, 128], BF16)
                sc = T(k, p4, "sc", [128, 16, 128], F32)
                wk16 = T(k, p4, "wk16", [128, 16, 128], F32)
                cw8 = T(k, p4, "cw8", [128, 8, 256], F32)
                v12d = [k.dep("v12d%d" % i) for i in range(16)]
                i12d = [k.dep("i12d%d" % i) for i in range(16)]
                wkd = [k.dep("wkd%d" % i) for i in range(16)]
                csd = [k.dep("csd%d" % i) for i in range(8)]
                cid = [k.dep("cid%d" % i) for i in range(8)]
                cwd = [k.dep("cwd%d" % i) for i in range(8)]
                sc2 = T(k, p4, "sc2", [128, 16, 128], F32)
                v12 = T(k, p4, "v12", [128, 16, 16], F32)
                i12 = T(k, p4, "i12", [128, 16, 16], U32)
                i12f = T(k, p4, "i12f", [128, 16, 16], F32)
                cand = T(k, p4, "cand", [128, 8, 16, 16], F32)
                cs = T(k, p4, "cs", [128, 8, 16], F32)
                ci = T(k, p4, "ci", [128, 8, 16], U32)
                r12 = T(k, p4, "r12", [128, 2, 8, 16], U32)
                r12f = T(k, p4, "r12f", [128, 2, 8, 16], F32)
                oh = T(k, p4, "oh", [128, 8, 16, 16], F32)
                gd = T(k, p4, "gd", [128, 8, 16], F32)
                gz = T(k, p4, "gz", [128, 16], F32)
                v12p = [v12, T(k, p4, "v12b", [128, 16, 16], F32)]
                i12p = [i12, T(k, p4, "i12b", [128, 16, 16], U32)]
                wk16p = [wk16, T(k, p4, "wk16b", [128, 16, 128], F32)]
                v12dp = [v12d, [k.dep("v12e%d" % i) for i in range(16)]]
                i12dp = [i12d, [k.dep("i12e%d" % i) for i in range(16)]]
                wkdp = [wkd, [k.dep("wke%d" % i) for i in range(16)]]

                qpT2 = [qpT, T(k, p4, "qpTb", [128, 16, 128], BF16)]

                def pework(ti):
                    xqt = xq[ti % 2]
                    sct = sc if ti % 2 == 0 else sc2
                    k.dma("sp", xqt.t[:], xn2d[ti].t, reads=[xn2d[ti].d], writes=[xqt.d])
                    for hp in range(16):
                        b = next_bank()
                        mm8(b, PB[b].t[:, 0:128], lambda c: wq.t[:, c, hp * 128:(hp + 1) * 128], lambda c: xqt.t[:, c, :], [wq.d, xqt.d])
                        k.op("act", lambda e: e.copy(qpT2[ti % 2].t[:, hp, :], PB[b].t[:, 0:128]), reads=[PB[b].d], writes=[qpT2[ti % 2].d])
                    qpT_ = qpT2[ti % 2]
                    for q4 in range(4):
                        b = next_bank()
                        for r in range(4):
                            hp = q4 * 4 + r
                            k.op("pe", lambda e: e.matmul(PB[b].t[:, r * 128:(r + 1) * 128], qpT_.t[:, hp, :], KTt.t[:, hp, :], start=True, stop=True),
                                 reads=[qpT_.d, KTt.d], writes=[PB[b].d], pe_accum=True)
                        k.op("act", lambda e: e.copy(sct.t[:, q4 * 4:(q4 + 1) * 4, :], PB[b].t[:, :].rearrange("p (r i) -> p r i", r=4)),
                             reads=[PB[b].d], writes=[sct.d])
                    for e1 in range(ti * 8, ti * 8 + 8):
                        convert_tile(e1)

                def head_ops(ti):
                    par = ti % 2
                    sct = sc if par == 0 else sc2
                    v12_, i12_, wk_ = v12p[par], i12p[par], wk16p[par]
                    vd, idp, wd = v12dp[par], i12dp[par], wkdp[par]
                    ops = []
                    for step in range(5):
                        for hp in range(16):
                            sv = sct.t[:, hp, :]
                            wv = wk_.t[:, hp, :]
                            if step == 0:
                                ops.append(lambda hp=hp, sv=sv: k.op("dve", lambda e: e.max(out=v12_.t[:, hp, 0:8], in_=sv), reads=[sct.d], writes=[vd[hp]]))
                            elif step == 1:
                                ops.append(lambda hp=hp, sv=sv: k.op("dve", lambda e: e.max_index(out=i12_.t[:, hp, 0:8], in_max=v12_.t[:, hp, 0:8], in_values=sv),
                                                                     reads=[sct.d, vd[hp]], writes=[idp[hp]]))
                            elif step == 2:
                                ops.append(lambda hp=hp, sv=sv, wv=wv: k.op("dve", lambda e: e.match_replace(out=wv, in_to_replace=v12_.t[:, hp, 0:8], in_values=sv, imm_value=-3.0e38),
                                                                            reads=[sct.d, vd[hp]], writes=[wd[hp]]))
                            elif step == 3:
                                ops.append(lambda hp=hp, wv=wv: k.op("dve", lambda e: e.max(out=v12_.t[:, hp, 8:16], in_=wv), reads=[wd[hp]], writes=[vd[hp]]))
                            else:
                                ops.append(lambda hp=hp, wv=wv: k.op("dve", lambda e: e.max_index(out=i12_.t[:, hp, 8:16], in_max=v12_.t[:, hp, 8:16], in_values=wv),
                                                                     reads=[wd[hp], vd[hp]], writes=[idp[hp]]))
                    return ops

                def tail_ops(ti):
                    par = ti % 2
                    v12_, i12_ = v12p[par], i12p[par]
                    vd, idp = v12dp[par], i12dp[par]
                    ops = []
                    ops.append(lambda: k.op("dve", lambda e: e.tensor_copy(i12f.t[:], i12_.t[:]), reads=idp, writes=[i12f.d]))
                    v4 = v12_.t[:].rearrange("p (h a) r -> p h a r", a=2)
                    i4 = i12f.t[:].rearrange("p (h a) r -> p h a r", a=2)
                    ops.append(lambda: k.op("dve", lambda e: e.tensor_tensor(cand.t[:], v4[:, :, 0, :, None].to_broadcast([128, 8, 16, 16]),
                                                                             v4[:, :, 1, None, :].to_broadcast([128, 8, 16, 16]), ALU.add), reads=vd, writes=[cand.d]))
                    for step in range(5):
                        for h in range(8):
                            cf = cand.t[:, h].rearrange("p a b -> p (a b)")
                            cwf = cw8.t[:, h, :]
                            if step == 0:
                                ops.append(lambda h=h, cf=cf: k.op("dve", lambda e: e.max(out=cs.t[:, h, 0:8], in_=cf), reads=[cand.d], writes=[csd[h]]))
                            elif step == 1:
                                ops.append(lambda h=h, cf=cf: k.op("dve", lambda e: e.max_index(out=ci.t[:, h, 0:8], in_max=cs.t[:, h, 0:8], in_values=cf),
                                                                   reads=[cand.d, csd[h]], writes=[cid[h]]))
                            elif step == 2:
                                ops.append(lambda h=h, cf=cf, cwf=cwf: k.op("dve", lambda e: e.match_replace(out=cwf, in_to_replace=cs.t[:, h, 0:8], in_values=cf, imm_value=-3.0e38),
                                                                            reads=[cand.d, csd[h]], writes=[cwd[h]]))
                            elif step == 3:
                                ops.append(lambda h=h, cwf=cwf: k.op("dve", lambda e: e.max(out=cs.t[:, h, 8:16], in_=cwf), reads=[cwd[h]], writes=[csd[h]]))
                            else:
                                ops.append(lambda h=h, cwf=cwf: k.op("dve", lambda e: e.max_index(out=ci.t[:, h, 8:16], in_max=cs.t[:, h, 8:16], in_values=cwf),
                                                                     reads=[cwd[h], csd[h]], writes=[cid[h]]))
                    ops.append(lambda: k.op("dve", lambda e: e.tensor_single_scalar(r12.t[:, 0], ci.t[:], 4, ALU.logical_shift_right), reads=cid, writes=[r12.d]))
                    ops.append(lambda: k.op("dve", lambda e: e.tensor_single_scalar(r12.t[:, 1], ci.t[:], 15, ALU.bitwise_and), reads=cid, writes=[r12.d]))
                    ops.append(lambda: k.op("dve", lambda e: e.tensor_copy(r12f.t[:], r12.t[:]), reads=[r12.d], writes=[r12f.d]))
                    for half, slot in ((0, slotA), (1, slotB)):
                        ops.append(lambda half=half: k.op("dve", lambda e: e.tensor_tensor(oh.t[:], iotaf.t[:, None, None, 0:16].to_broadcast([128, 8, 16, 16]),
                                                                                            r12f.t[:, half, :, :, None].to_broadcast([128, 8, 16, 16]), ALU.is_equal),
                                                          reads=[iotaf.d, r12f.d], writes=[oh.d]))
                        ops.append(lambda half=half: k.op("dve", lambda e: e.tensor_tensor(oh.t[:], oh.t[:], i4[:, :, half, None, :].to_broadcast([128, 8, 16, 16]), ALU.mult),
                                                          reads=[oh.d, i12f.d], writes=[oh.d]))
                        ops.append(lambda slot=slot: k.op("dve", lambda e: e.reduce_sum(slot.t[:, ti, :].rearrange("p (h k) -> p h k", h=8), oh.t[:], axis=AX.X),
                                                          reads=[oh.d], writes=[slot.d]))
                    ops.append(lambda: k.op("dve", lambda e: e.tensor_tensor(gd.t[:], cs.t[:], cs.t[:, :, 0:1].to_broadcast([128, 8, 16]), ALU.subtract), reads=csd, writes=[gd.d]))
                    ops.append(lambda: k.op("act", lambda e: e.activation(gd.t[:], gd.t[:], AF.Exp), reads=[gd.d], writes=[gd.d]))
                    ops.append(lambda: k.op("dve", lambda e: e.reduce_sum(gz.t[:, 0:8], gd.t[:], axis=AX.X), reads=[gd.d], writes=[gz.d]))
                    ops.append(lambda: k.op("dve", lambda e: e.reciprocal(gz.t[:, 8:16], gz.t[:, 0:8]), reads=[gz.d], writes=[gz.d]))
                    ops.append(lambda: k.op("dve", lambda e: e.tensor_tensor(slotG.t[:, ti, :].rearrange("p (h k) -> p h k", h=8), gd.t[:],
                                                                             gz.t[:, 8:16, None].to_broadcast([128, 8, 16]), ALU.mult), reads=[gd.d, gz.d], writes=[slotG.d]))
                    return ops

                pework(0)
                for o in head_ops(0):
                    o()
                for ti in range(NQT):
                    tl = tail_ops(ti)
                    hd = []
                    if ti + 1 < NQT:
                        pework(ti + 1)
                        hd = head_ops(ti + 1)
                    nh, nt = len(hd), len(tl)
                    hi = 0
                    for idx, o in enumerate(tl):
                        o()
                        want = (idx + 1) * nh // nt
                        while hi < want:
                            hd[hi]()
                            hi += 1
                    while hi < nh:
                        hd[hi]()
                        hi += 1
                dump("slotA", slotA, [128, NQT, 128], F32)
                dump("slotB", slotB, [128, NQT, 128], F32)
                dump("slotG", slotG, [128, NQT, 128], F32)
                k.barrier()

            with ExitStack() as p5:
                TBM = 384
                Gs = T(k, p5, "Gs", [128, TBM, 128], BF16)
                gfin = load_bcast(p5, "gfin", A["g_final"], D)
                x2 = T(k, p5, "x2", [128, D], F32)
                passes = [(0, 2), (2, 3), (5, 3), (8, 3), (11, 3), (14, 2)]
                for ps_, (tile0, ntile) in enumerate(passes):
                    TBp = ntile * 128
                    with ExitStack() as pg:
                        NCH = 4
                        CH = 128 // NCH
                        Aoh = [T(k, pg, "Aoh%d" % c, [128, CH, 128], BF16) for c in range(2)]
                        Boh = [T(k, pg, "Boh%d" % c, [128, CH, 128], BF16) for c in range(2)]
                        trs = [T(k, pg, "trs%d" % i, [128, 128], BF16) for i in range(3)]
                        g_rr = 0
                        c_rr = 0
                        for tt in range(ntile):
                            ti = tile0 + tt
                            for idx, slot in enumerate((slotA, slotB, slotG)):
                                k.op("pe", lambda e: e.transpose(PB[0].t[:, idx * 128:(idx + 1) * 128], slot.t[:, ti, :], ident.t[:]),
                                     reads=[slot.d, ident.d], writes=[PB[0].d], pe_accum=(idx > 0))
                            for idx in range(3):
                                k.op("act", lambda e: e.copy(trs[idx].t[:], PB[0].t[:, idx * 128:(idx + 1) * 128]), reads=[PB[0].d], writes=[trs[idx].d])
                            ib = iotab.t[:, None, :].to_broadcast([128, CH, 128])
                            for ch in range(NCH):
                                Ac, Bc = Aoh[c_rr % 2], Boh[c_rr % 2]
                                c_rr += 1
                                tsl = slice(ch * CH, (ch + 1) * CH)
                                k.op("dve", lambda e: e.tensor_tensor(Ac.t[:], ib, trs[0].t[:, tsl, None].to_broadcast([128, CH, 128]), ALU.is_equal),
                                     reads=[iotab.d, trs[0].d], writes=[Ac.d])
                                k.op("dve", lambda e: e.tensor_tensor(Ac.t[:], Ac.t[:], trs[2].t[:, tsl, None].to_broadcast([128, CH, 128]), ALU.mult),
                                     reads=[Ac.d, trs[2].d], writes=[Ac.d])
                                k.op("dve", lambda e: e.tensor_tensor(Bc.t[:], ib, trs[1].t[:, tsl, None].to_broadcast([128, CH, 128]), ALU.is_equal),
                                     reads=[iotab.d, trs[1].d], writes=[Bc.d])
                                for t4 in range(CH // 4):
                                    b = 1 + (g_rr % 7)
                                    g_rr += 1
                                    for r in range(4):
                                        tq = t4 * 4 + r
                                        k.op("pe", lambda e: e.matmul(PB[b].t[:, r * 128:(r + 1) * 128], Bc.t[:, tq, :], Ac.t[:, tq, :], start=True, stop=True),
                                             reads=[Ac.d, Bc.d], writes=[PB[b].d], pe_accum=(r > 0))
                                    tg = tt * 128 + ch * CH + t4 * 4
                                    k.op("act", lambda e: e.copy(Gs.t[:, tg:tg + 4, :], PB[b].t[:, :].rearrange("p (r i) -> p r i", r=4)),
                                         reads=[PB[b].d], writes=[Gs.d])
                        k.barrier()
                    if ps_ == 0:
                        dump("Gs", Gs, [128, TBM, 128], BF16)
                    with ExitStack() as pd:
                        NB = 4
                        uv = [T(k, pd, "uv%d" % i, [128, 2 * D], BF16) for i in range(NB)]
                        gl = [T(k, pd, "gl%d" % i, [128, TBM], F32) for i in range(NB)]
                        aT = [T(k, pd, "aT%d" % i, [128, TBM], BF16) for i in range(NB)]
                        xp = T(k, pd, "xp", [128, 8, TBM], BF16)
                        for tt in range(ntile):
                            ti = tile0 + tt
                            k.dma("sp", xp.t[:, :, tt * 128:(tt + 1) * 128], xn2d[ti].t, reads=[xn2d[ti].d], writes=[xp.d], join=True)
                        hbanks = [7, 0]

                        def stage2(e1):
                            jb = e1 % NB
                            bh = hbanks[e1 % len(hbanks)]
                            k.op("act", lambda e: e.activation(gl[jb].t[:, 0:TBp], PB[bh].t[:, 0:TBp], AF.Gelu), reads=[PB[bh].d], writes=[gl[jb].d])
                            k.op("dve", lambda e: e.tensor_tensor(aT[jb].t[:, 0:TBp], gl[jb].t[:, 0:TBp], Gs.t[:, 0:TBp, e1], ALU.mult),
                                 reads=[gl[jb].d, Gs.d], writes=[aT[jb].d])
                            for tt in range(ntile):
                                for dh in range(2):
                                    ba = 1 + tt * 2 + dh
                                    k.op("pe", lambda e: e.matmul(PB[ba].t[:, :], aT[jb].t[:, tt * 128:(tt + 1) * 128], uv[jb].t[:, D + dh * 512:D + (dh + 1) * 512],
                                                                  start=(e1 == 0), stop=(e1 == 127)),
                                         reads=[aT[jb].d, uv[jb].d], writes=[PB[ba].d], pe_accum=(e1 > 0))

                        for e1 in range(128):
                            j = e1 % 2
                            jb = e1 % NB
                            k.dma("sp", uv[jb].t[:], uvd[e1].t, reads=[uvd[e1].d], writes=[uv[jb].d])
                            bh = hbanks[e1 % len(hbanks)]
                            for c in range(8):
                                k.op("pe", lambda e: e.matmul(PB[bh].t[:, 0:TBp], uv[jb].t[:, c * 128:(c + 1) * 128], xp.t[:, c, 0:TBp], start=(c == 0), stop=(c == 7)),
                                     reads=[uv[jb].d, xp.d], writes=[PB[bh].d], pe_accum=(c > 0))
                            if len(hbanks) == 1:
                                stage2(e1)
                            elif e1 >= 1:
                                stage2(e1 - 1)
                        if len(hbanks) > 1:
                            stage2(127)
                        for tt in range(ntile):
                            ti = tile0 + tt
                            xi = xin[xin_i[0] % 2]
                            k.dma("sp", xi.t[:], x1d[ti].t, reads=[x1d[ti].d], writes=[xi.d])
                            for dh in range(2):
                                ba = 1 + tt * 2 + dh
                                k.op("dve", lambda e: e.tensor_tensor(x2.t[:, dh * 512:(dh + 1) * 512], xi.t[:, dh * 512:(dh + 1) * 512], PB[ba].t[:, :], ALU.add),
                                     reads=[xi.d, PB[ba].d], writes=[x2.d])
                            ns = nstat[xin_i[0] % 2]
                            xin_i[0] += 1
                            rms_scale(x2, ns)
                            k.op("dve", lambda e: e.scalar_tensor_tensor(x2.t[:], x2.t[:], ns.t[:, 1:2], gfin.t[:], ALU.mult, ALU.mult),
                                 reads=[x2.d, ns.d, gfin.d], writes=[x2.d])
                            od = DD(k, out[ti * 128:(ti + 1) * 128, :], "out%d" % ti)
                            k.dma("pool", od.t, x2.t[:], reads=[x2.d], writes=[od.d], anchor=x2.d)
                            outdeps.append(od.d)
                        k.barrier()
        k.finish(outdeps)
    return nc


_NC_CACHE = {}


def make_in_maps(inputs):
    x = np.ascontiguousarray(np.asarray(inputs["x"], dtype=np.float32))
    shared = {}
    for name, shp in W_SPECS:
        shared[name] = np.ascontiguousarray(np.asarray(inputs[name], dtype=np.float32).reshape(shp))
    in_maps = []
    for c in range(NCORES):
        b, half = c // 2, c % 2
        xc = np.zeros((LCTX, D), np.float32)
        if half == 1:
            xc[:] = x[b]
        else:
            xc[2048:] = x[b, 0:2048]
        m = dict(shared)
        m["xc"] = xc
        m.update(host_consts(half))
        in_maps.append(m)
    return in_maps


def kernel(**inputs):
    if "nc" not in _NC_CACHE:
        _NC_CACHE["nc"] = build()
    nc = _NC_CACHE["nc"]
    in_maps = make_in_maps(inputs)
    res = run_bass_kernel_spmd(nc, in_maps, core_ids=list(range(NCORES)))
    outp = np.zeros((4, 4096, D), np.float32)
    for c in range(NCORES):
        b, half = c // 2, c % 2
        outp[b, half * 2048:(half + 1) * 2048] = np.asarray(res.results[c]["out"], dtype=np.float32)
    return outp
```

```python
import numpy as np
import ml_dtypes
from contextlib import ExitStack
import concourse.bass as bass
import concourse.mybir as mybir
from concourse.bass_utils import run_bass_kernel_spmd


F32 = mybir.dt.float32
BF16 = mybir.dt.bfloat16
U32 = mybir.dt.uint32
I32 = mybir.dt.int32
ALU = mybir.AluOpType
AF = mybir.ActivationFunctionType
AX = mybir.AxisListType


class Dep:
    __slots__ = ("name", "w", "r", "dsem", "dcnt")

    def __init__(self, name=""):
        self.name = name
        self.w = None
        self.r = []
        self.dsem = None
        self.dcnt = 0


class K:
    def __init__(self, nc, stack):
        self.nc = nc
        self.stack = stack
        self.eng = {"pe": nc.tensor, "act": nc.scalar, "dve": nc.vector, "pool": nc.gpsimd, "sp": nc.sync}
        self.sems = {}
        self.cnt = {}
        for e in self.eng:
            self.sems[e] = stack.enter_context(nc.semaphore("c_" + e))
            self.cnt[e] = 0
        self.seen = {e: {} for e in self.eng}
        self.ndsem = 0
        self.dfree = {q: [] for q in self.eng}
        self.downers = []
        self.dtot = {}
        self.n_inst = 0

    def dep(self, name=""):
        return Dep(name)

    MAXD = 88

    def _dsem(self, d, e):
        if d.dsem is None:
            d.dsem = {}
        if e not in d.dsem:
            if self.dfree[e]:
                key = self.dfree[e].pop()
            else:
                if self.ndsem >= self.MAXD:
                    raise RuntimeError("out of DMA semaphores")
                key = "d%d" % self.ndsem
                self.ndsem += 1
                self.sems[key] = self.stack.enter_context(self.nc.semaphore(key))
                self.dtot[key] = 0
            d.dsem[e] = key
            self.downers.append((d, e))
        return d.dsem[e]

    def barrier(self):
        tot = dict(self.cnt)
        tot.update(self.dtot)
        for e, eng in self.eng.items():
            for key, v in tot.items():
                if key == e or v == 0:
                    continue
                if self.seen[e].get(key, 0) < v:
                    self.seen[e][key] = v
                    eng.wait_ge(self.sems[key], v)
        for d, q in self.downers:
            self.dfree[q].append(d.dsem.pop(q))
        self.downers = []

    def _need(self, e, reads, writes, pe_accum=False, same_ok=False):
        need = {}
        def add(p):
            if p is None:
                return
            k, v = p
            if same_ok and k == e:
                return
            if need.get(k, 0) < v:
                need[k] = v
        for d in reads:
            add(d.w)
        for d in writes:
            if not (pe_accum and d.w is not None and d.w[0] == "pe"):
                add(d.w)
            for p in d.r:
                add(p)
        out = []
        for k, v in need.items():
            if self.seen[e].get(k, 0) >= v:
                continue
            self.seen[e][k] = v
            out.append((k, v))
        return out

    def op(self, e, fn, reads=(), writes=(), pe_accum=False, same_ok=False):
        eng = self.eng[e]
        for k, v in self._need(e, reads, writes, pe_accum, same_ok):
            eng.wait_ge(self.sems[k], v)
        ins = fn(eng)
        self.cnt[e] += 1
        ins.then_inc(self.sems[e], 1)
        tag = (e, self.cnt[e])
        for d in writes:
            d.w = tag
            d.r = []
        for d in reads:
            d.r.append(tag)
        self.n_inst += 1
        return ins

    def dma(self, e, out, in_, reads=(), writes=(), join=False, anchor=None, **kw):
        eng = self.eng[e]
        saved = None
        if join and writes:
            saved = [(d, d.w) for d in writes]
            for d in writes:
                if d.w is not None and d.dsem and d.w[0] in d.dsem.values():
                    d.w = None
        for k, v in self._need(e, reads, writes):
            eng.wait_ge(self.sems[k], v)
        if anchor is None:
            anchor = writes[0] if writes else reads[0]
        key = self._dsem(anchor, e)
        self.dtot[key] += 16
        ins = eng.dma_start(out=out, in_=in_, **kw)
        ins.then_inc(self.sems[key], 16)
        tag = (key, self.dtot[key])
        for d in writes:
            d.w = tag
            d.r = []
        for d in reads:
            d.r.append(tag)
        self.n_inst += 1
        return ins

    def finish(self, deps):
        eng = self.eng["sp"]
        for d in deps:
            for p in ([d.w] if d.w else []) + d.r:
                if self.seen["sp"].get(p[0], 0) < p[1]:
                    self.seen["sp"][p[0]] = p[1]
                    eng.wait_ge(self.sems[p[0]], p[1])


BF = ml_dtypes.bfloat16
NCORES = 8
D = 1024
TOK = 2048
LCTX = 4096
NQT = 16
BIG8 = 240000.0
NEGF = -1.0e30
NW = 4376
C_Q, C_KC, C_VC, C_KS, C_VS, C_KW, C_VW, C_G, C_GLU, C_MG = 0, 512, 640, 768, 896, 1024, 1152, 1280, 1304, 2328
NA = 2328
TB = 256


class T:
    n = 0
    def __init__(self, k, st, name, shape, dt, psum=False):
        nc = k.nc
        T.n += 1
        name = "s%d_%s" % (T.n, name)
        self.t = st.enter_context(nc.psum_tensor(name, shape, dt) if psum else nc.sbuf_tensor(name, shape, dt))
        self.d = k.dep(name)


class DD:
    def __init__(self, k, ap, name=""):
        self.t = ap
        self.d = k.dep(name)


def host_consts(half):
    c = {}
    c["ident"] = np.eye(128, dtype=np.float32)
    p = np.arange(LCTX)
    kaug = np.zeros((5, LCTX), np.float32)
    kaug[0] = 128 * (p // 128); kaug[1] = p % 128; kaug[2] = 1; kaug[3] = 1
    kaug[4] = ((p < 2048) & (half == 0)).astype(np.float32)
    c["kaug"] = kaug.astype(BF)
    n = np.arange(256); end = 16 * n + 31
    caug = np.zeros((5, 256), np.float32)
    caug[0] = 128 * (end // 128); caug[1] = end % 128; caug[2] = 1; caug[3] = 1
    caug[4] = ((n == 255) | ((n < 128) & (half == 0))).astype(np.float32)
    c["caug"] = caug.astype(BF)
    qaug = np.zeros((5, 2, NQT, 4, 128), np.float32)
    for g in range(2):
        for hh in range(4):
            h = 4 * g + hh
            sp = 8.0 * 2.0 ** (-(h + 1))
            for i in range(NQT):
                qpos = 2048 + 128 * i + np.arange(128)
                qaug[0, g, i, hh] = sp; qaug[1, g, i, hh] = sp
                qaug[2, g, i, hh] = -sp * (128 * (qpos // 128)); qaug[3, g, i, hh] = -sp * (qpos % 128)
                qaug[4, g, i, hh] = -BIG8
    c["qaug"] = qaug.astype(BF)
    j = np.arange(64)
    c["expand"] = (p[None, :] // 64 == j[:, None]).astype(np.float32).astype(BF)
    st_ = 16 * n
    ovl = ((st_[:, None] <= 64 * j[None, :] + 63) & (end[:, None] >= 64 * j[None, :])).astype(np.float32)
    ovl[255] = 0
    c["ovl"] = ovl.astype(BF)
    keep = np.zeros((NQT, 128, 64), np.float32); add = np.zeros_like(keep); allow = np.zeros_like(keep)
    gb0 = 0 if half == 1 else 32
    for i in range(NQT):
        tblk = 32 + 2 * i + (np.arange(128) // 64)
        jj = j[None, :]; tb = tblk[:, None]
        valid = (jj >= 32) if half == 0 else np.ones_like(jj, bool)
        valid = np.broadcast_to(valid, (128, 64))
        forced = (jj == gb0) | (jj == tb) | ((jj == tb - 1) & valid)
        future = jj > tb
        bad = (future | ~valid) & ~forced
        keep[i] = (~forced & ~bad).astype(np.float32)
        add[i] = np.where(forced, 1e9, np.where(bad, NEGF, 0.0))
        allow[i] = (valid & ~future).astype(np.float32)
    c["selkeep"] = keep; c["seladd"] = add; c["selallow"] = allow
    c["iotaf"] = np.broadcast_to(np.arange(128, dtype=np.float32)[None, :], (128, 128)).copy()
    return c


CONST_SPECS = [("ident", [128, 128], "f"), ("kaug", [5, LCTX], "b"), ("caug", [5, 256], "b"),
               ("qaug", [5, 2, NQT, 4, 128], "b"), ("expand", [64, LCTX], "b"), ("ovl", [256, 64], "b"),
               ("selkeep", [NQT, 128, 64], "f"), ("seladd", [NQT, 128, 64], "f"), ("selallow", [NQT, 128, 64], "f"),
               ("iotaf", [128, 128], "f")]

W_SPECS = [("g_mix", [D]), ("w_in", [D, NW]), ("pe_cmp_k", [32, 64]), ("pe_cmp_v", [32, 64]),
           ("w_cmp_k1", [2048, 64]), ("w_cmp_k2", [64, 64]), ("w_cmp_v1", [2048, 64]), ("w_cmp_v2", [64, 64]),
           ("w_nsa_out", [512, D]), ("w_dw", [31, 512]), ("b_dw", [512]), ("g_conv_ln", [512]), ("b_conv_ln", [512]),
           ("w_conv_out", [512, D]), ("w_o", [D, D]), ("g_ffn", [D]), ("w_peer_q", [D, 2048]),
           ("peer_sub_keys", [16, 128, 128]), ("peer_u", [16384, D]), ("peer_v", [16384, D]), ("g_final", [D])]


def build(debug=()):
    nc = bass.Bass("TRN2", target_bir_lowering=False)
    A = {}
    A["xc"] = nc.dram_tensor("xc", [LCTX, D], F32, kind="ExternalInput").ap()
    for name, shp in W_SPECS:
        A[name] = nc.dram_tensor(name, shp, F32, kind="ExternalInput").ap()
    for name, shp, ty in CONST_SPECS:
        A[name] = nc.dram_tensor(name, shp, F32 if ty == "f" else BF16, kind="ExternalInput").ap()
    out = nc.dram_tensor("out", [TOK, D], F32, kind="ExternalOutput").ap()
    x1d_ap = nc.dram_tensor("x1d", [TOK, D], F32, kind="Internal").ap()
    sTd_ap = nc.dram_tensor("sTd", [128, 4, TOK], BF16, kind="Internal").ap()
    xn2Td_ap = nc.dram_tensor("xn2Td", [128, 8, TOK], BF16, kind="Internal").ap()
    uvd_ap = nc.dram_tensor("uvd", [128, 128, 2 * D], BF16, kind="Internal").ap()
    dbg_aps = {}

    with ExitStack() as st:
        k = K(nc, st)
        x1d = [DD(k, x1d_ap[i * 128:(i + 1) * 128, :], "x1d%d" % i) for i in range(NQT)]
        sTd = DD(k, sTd_ap, "sTd")
        outdeps = []
        PB = [T(k, st, "pb%d" % i, [128, 512], F32, psum=True) for i in range(8)]

        def pbf(i):
            return PB[i].t[:].bitcast(BF16)

        def dump(name, src, shape, dt):
            if name in debug:
                ap = nc.dram_tensor("dbg_" + name, shape, dt, kind="ExternalOutput").ap()
                dd = DD(k, ap, "dbg_" + name)
                k.dma("sp", ap, src.t[:], reads=[src.d], writes=[dd.d], anchor=src.d)
                outdeps.append(dd.d)

        ident = T(k, st, "ident", [128, 128], F32)
        identb = T(k, st, "identb", [128, 128], BF16)
        k.dma("sp", ident.t[:], A["ident"], writes=[ident.d])
        k.op("dve", lambda e: e.tensor_copy(identb.t[:], ident.t[:]), reads=[ident.d], writes=[identb.d])

        def load_vec_cols(stk, name, ap1d, ncol):
            t = T(k, stk, name, [128, ncol], F32)
            k.dma("sp", t.t[:], ap1d.rearrange("(c p) -> p c", p=128), writes=[t.d], allow_slow_non_contiguous=True)
            return t

        def load_bcast(stk, name, ap1d, n):
            t = T(k, stk, name, [128, n], F32)
            k.dma("sp", t.t[:], ap1d.partition_broadcast(128), writes=[t.d])
            return t

        NSTG = 2
        SW = 1024
        stg = [T(k, st, "stg%d" % i, [128, SW], F32) for i in range(NSTG)]
        stg_i = [0]
        cast_eng = ["dve", "act"]

        def load_w(dst_ap, dst_dep, src_ap, ncols, scale=None, npart=128):
            for c0 in range(0, ncols, SW):
                w_ = min(SW, ncols - c0)
                s = stg[stg_i[0] % NSTG]
                eng = cast_eng[stg_i[0] % 2]
                stg_i[0] += 1
                k.dma("sp", s.t[0:npart, 0:w_], src_ap[:, c0:c0 + w_], writes=[s.d])
                dv = dst_ap[:, c0:c0 + w_]
                if scale is not None:
                    sc, col = scale
                    if eng == "act":
                        k.op("act", lambda e: e.activation(dv, s.t[0:npart, 0:w_], AF.Copy, scale=sc.t[0:npart, col:col + 1]),
                             reads=[s.d, sc.d], writes=[dst_dep])
                    else:
                        k.op(eng, lambda e: e.tensor_scalar(dv, s.t[0:npart, 0:w_], sc.t[0:npart, col:col + 1], None, ALU.mult),
                             reads=[s.d, sc.d], writes=[dst_dep])
                elif eng == "act":
                    k.op("act", lambda e: e.copy(dv, s.t[0:npart, 0:w_]), reads=[s.d], writes=[dst_dep])
                else:
                    k.op(eng, lambda e: e.tensor_copy(dv, s.t[0:npart, 0:w_]), reads=[s.d], writes=[dst_dep])

        xin = [T(k, st, "xin%d" % i, [128, D], F32) for i in range(2)]
        xjunk = T(k, st, "xjunk", [128, D], BF16)
        xs_b = [T(k, st, "xsb%d" % i, [128, D], BF16) for i in range(2)]
        nstat = [T(k, st, "nstat%d" % i, [128, 4], F32) for i in range(2)]
        xin_i = [0]

        def rms_scale(src, dstat):
            k.op("act", lambda e: e.activation(xjunk.t[:], src.t[:], AF.Square, accum_out=dstat.t[:, 0:1]),
                 reads=[src.d], writes=[xjunk.d, dstat.d])
            k.op("dve", lambda e: e.tensor_scalar(dstat.t[:, 1:2], dstat.t[:, 0:1], 1.0 / D, 1e-6, ALU.mult, ALU.add),
                 reads=[dstat.d], writes=[dstat.d])
            k.op("act", lambda e: e.sqrt(dstat.t[:, 1:2], dstat.t[:, 1:2]), reads=[dstat.d], writes=[dstat.d])
            k.op("dve", lambda e: e.reciprocal(dstat.t[:, 1:2], dstat.t[:, 1:2]), reads=[dstat.d], writes=[dstat.d])

        def norm_transpose(src, dstT, col0, bank):
            norm_transpose_many([(src, dstT, col0)])

        def norm_transpose_many(items):
            slots = []
            for (src, dstT, col0) in items:
                j = xin_i[0] % 2
                xin_i[0] += 1
                slots.append((nstat[j], xs_b[j]))
            for (src, _, _), (ns, xb) in zip(items, slots):
                k.op("act", lambda e: e.activation(xjunk.t[:], src.t[:], AF.Square, accum_out=ns.t[:, 0:1]),
                     reads=[src.d], writes=[xjunk.d, ns.d])
            for (src, _, _), (ns, xb) in zip(items, slots):
                k.op("dve", lambda e: e.tensor_scalar(ns.t[:, 1:2], ns.t[:, 0:1], 1.0 / D, 1e-6, ALU.mult, ALU.add), reads=[ns.d], writes=[ns.d])
            for (src, _, _), (ns, xb) in zip(items, slots):
                k.op("act", lambda e: e.sqrt(ns.t[:, 1:2], ns.t[:, 1:2]), reads=[ns.d], writes=[ns.d])
            for (src, _, _), (ns, xb) in zip(items, slots):
                k.op("dve", lambda e: e.reciprocal(ns.t[:, 1:2], ns.t[:, 1:2]), reads=[ns.d], writes=[ns.d])
            for (src, _, _), (ns, xb) in zip(items, slots):
                k.op("dve", lambda e: e.tensor_scalar(xb.t[:], src.t[:], ns.t[:, 1:2], None, ALU.mult), reads=[src.d, ns.d], writes=[xb.d])
            banks = [0, 7]
            for idx, ((src, dstT, col0), (ns, xb)) in enumerate(zip(items, slots)):
                pv = pbf(banks[idx])
                for c in range(8):
                    k.op("pe", lambda e: e.transpose(pv[:, c * 128:(c + 1) * 128], xb.t[:, c * 128:(c + 1) * 128], identb.t[:]),
                         reads=[xb.d, identb.d], writes=[PB[banks[idx]].d], pe_accum=(c > 0))
            for idx, ((src, dstT, col0), (ns, xb)) in enumerate(zip(items, slots)):
                pv = pbf(banks[idx])
                dv = dstT.t[:, 0:8, col0:col0 + 128]
                sv = pv.rearrange("p (c t) -> p c t", c=8)
                if idx == 0:
                    k.op("act", lambda e: e.copy(dv, sv), reads=[PB[banks[idx]].d], writes=[dstT.d])
                else:
                    k.op("dve", lambda e: e.tensor_copy(dv, sv), reads=[PB[banks[idx]].d], writes=[dstT.d])

        def load_xn_block(xb, row0, ntile):
            for j0 in range(0, ntile, 2):
                items = []
                for j in range(j0, min(j0 + 2, ntile)):
                    xi = xin[(xin_i[0] + len(items)) % 2]
                    k.dma("sp", xi.t[:], A["xc"][row0 + j * 128:row0 + (j + 1) * 128, :], writes=[xi.d])
                    items.append((xi, xb, j * 128))
                norm_transpose_many(items)

        bank_rr = [0]

        def next_bank():
            b = 1 + (bank_rr[0] % 6)
            bank_rr[0] += 1
            return b

        ev_i = [0]

        def evac_copy(dst_ap, dst_dep, bank, src_ap, func=None):
            ev_i[0] += 1
            if func is not None:
                k.op("act", lambda e: e.activation(dst_ap, src_ap, func), reads=[PB[bank].d], writes=[dst_dep])
            elif ev_i[0] % 2:
                k.op("act", lambda e: e.copy(dst_ap, src_ap), reads=[PB[bank].d], writes=[dst_dep])
            else:
                k.op("dve", lambda e: e.tensor_copy(dst_ap, src_ap), reads=[PB[bank].d], writes=[dst_dep])

        def mm8(bank, out_ap, lhs_fn, rhs_fn, rdeps, n=8):
            for c in range(n):
                k.op("pe", lambda e: e.matmul(out_ap, lhs_fn(c), rhs_fn(c), start=(c == 0), stop=(c == n - 1)),
                     reads=rdeps, writes=[PB[bank].d], pe_accum=True)

        gmix = load_vec_cols(st, "gmix", A["g_mix"], 8)
        gffn = load_vec_cols(st, "gffn", A["g_ffn"], 8)
        iotaf = T(k, st, "iotaf", [128, 128], F32)
        k.dma("sp", iotaf.t[:], A["iotaf"], writes=[iotaf.d])
        iotab = T(k, st, "iotab", [128, 128], BF16)
        k.op("dve", lambda e: e.tensor_copy(iotab.t[:], iotaf.t[:]), reads=[iotaf.d], writes=[iotab.d])
        pX = st.enter_context(ExitStack())
        onsaT = T(k, pX, "onsaT", [128, 4, TOK], BF16)

        with ExitStack() as pc:
            wG = T(k, pc, "wG", [128, 8, 1024], BF16)
            for c in range(8):
                load_w(wG.t[:, c, :], wG.d, A["w_in"][c * 128:(c + 1) * 128, C_GLU:C_GLU + 1024], 1024, scale=(gmix, c))
            xnT = [T(k, pc, "xnTc%d" % i, [128, 8, 512], BF16) for i in range(2)]
            U = T(k, pc, "U", [128, 4, 128 + TOK], BF16)
            Cv = T(k, pc, "Cv", [128, 4, TOK], F32)
            sgt = T(k, pc, "sgt", [128, 512], F32)
            sTl = T(k, pc, "sTl", [128, 4, TOK], BF16)
            wdw = T(k, pc, "wdw", [128, 4, 31], F32)
            for ct in range(4):
                k.dma("sp", wdw.t[:, ct, :], A["w_dw"][:, ct * 128:(ct + 1) * 128].rearrange("k c -> c k"), writes=[wdw.d],
                      allow_slow_non_contiguous=True, join=True)
            bdw = load_vec_cols(pc, "bdw", A["b_dw"], 4)
            gln = load_bcast(pc, "gln", A["g_conv_ln"], 512)
            bln = load_bcast(pc, "bln", A["b_conv_ln"], 512)
            for bi, (row0, ntile, ucol) in enumerate([(1920, 1, 0)] + [(2048 + 512 * b, 4, 128 + 512 * b) for b in range(4)]):
                xb = xnT[bi % 2]
                load_xn_block(xb, row0, ntile)
                ntok = ntile * 128
                for ct in range(4):
                    ba = next_bank()
                    mm8(ba, PB[ba].t[:, 0:ntok], lambda c: wG.t[:, c, ct * 128:(ct + 1) * 128], lambda c: xb.t[:, c, 0:ntok], [wG.d, xb.d])
                    bb = next_bank()
                    mm8(bb, PB[bb].t[:, 0:ntok], lambda c: wG.t[:, c, 512 + ct * 128:512 + (ct + 1) * 128], lambda c: xb.t[:, c, 0:ntok], [wG.d, xb.d])
                    uv = U.t[:, ct, ucol:ucol + ntok]
                    k.op("act", lambda e: e.activation(sgt.t[:, 0:ntok], PB[bb].t[:, 0:ntok], AF.Sigmoid), reads=[PB[bb].d], writes=[sgt.d])
                    k.op("dve", lambda e: e.tensor_tensor(uv, sgt.t[:, 0:ntok], PB[ba].t[:, 0:ntok], ALU.mult), reads=[sgt.d, PB[ba].d], writes=[U.d])
            Dg = T(k, pc, "Dg", [128, 4, 31, 128], BF16)
            for ct in range(4):
                for kk in range(31):
                    k.op("dve", lambda e: e.tensor_scalar(Dg.t[:, ct, kk, :], identb.t[:, :], wdw.t[:, ct, kk:kk + 1], None, ALU.mult),
                         reads=[identb.d, wdw.d], writes=[Dg.d], same_ok=(ct + kk > 0))
            cdeps = [k.dep("cv%d" % ct) for ct in range(4)]
            for ct in range(4):
                for blk in range(4):
                    b = next_bank()
                    for kk in range(31):
                        c0 = 98 + kk + blk * 512
                        k.op("pe", lambda e: e.matmul(PB[b].t[:, :], Dg.t[:, ct, kk, :], U.t[:, ct, c0:c0 + 512], start=(kk == 0), stop=(kk == 30)),
                             reads=[Dg.d, U.d], writes=[PB[b].d], pe_accum=(kk > 0))
                    k.op("act", lambda e: e.activation(Cv.t[:, ct, blk * 512:(blk + 1) * 512], PB[b].t[:, :], AF.Identity, bias=bdw.t[:, ct:ct + 1]),
                         reads=[PB[b].d, bdw.d], writes=[cdeps[ct]], same_ok=(blk > 0))
            lnst = T(k, pc, "lnst", [128, 8], F32)
            un = T(k, pc, "un", [128, 512], F32)
            unb = T(k, pc, "unb", [128, 512], BF16)
            lnj = T(k, pc, "lnj", [128, 512], F32)
            for i in range(NQT):
                for ct in range(4):
                    k.op("pe", lambda e: e.transpose(PB[7].t[:, ct * 128:(ct + 1) * 128], Cv.t[:, ct, i * 128:(i + 1) * 128], ident.t[:]),
                         reads=[cdeps[ct], ident.d], writes=[PB[7].d], pe_accum=True)
                k.op("dve", lambda e: e.reduce_sum(lnst.t[:, 0:1], PB[7].t[:], axis=AX.X), reads=[PB[7].d], writes=[lnst.d])
                k.op("act", lambda e: e.activation(lnj.t[:], PB[7].t[:], AF.Square, accum_out=lnst.t[:, 1:2]), reads=[PB[7].d], writes=[lnj.d, lnst.d])
                k.op("dve", lambda e: e.tensor_scalar(lnst.t[:, 2:3], lnst.t[:, 0:1], 1.0 / 512, None, ALU.mult), reads=[lnst.d], writes=[lnst.d])
                k.op("dve", lambda e: e.tensor_tensor(lnst.t[:, 3:4], lnst.t[:, 2:3], lnst.t[:, 2:3], ALU.mult), reads=[lnst.d], writes=[lnst.d])
                k.op("dve", lambda e: e.scalar_tensor_tensor(lnst.t[:, 4:5], lnst.t[:, 1:2], 1.0 / 512, lnst.t[:, 3:4], ALU.mult, ALU.subtract),
                     reads=[lnst.d], writes=[lnst.d])
                k.op("dve", lambda e: e.tensor_scalar(lnst.t[:, 4:5], lnst.t[:, 4:5], 1e-6, None, ALU.add), reads=[lnst.d], writes=[lnst.d])
                k.op("act", lambda e: e.sqrt(lnst.t[:, 4:5], lnst.t[:, 4:5]), reads=[lnst.d], writes=[lnst.d])
                k.op("dve", lambda e: e.reciprocal(lnst.t[:, 4:5], lnst.t[:, 4:5]), reads=[lnst.d], writes=[lnst.d])
                k.op("dve", lambda e: e.tensor_scalar(un.t[:], PB[7].t[:], lnst.t[:, 2:3], lnst.t[:, 4:5], ALU.subtract, ALU.mult),
                     reads=[PB[7].d, lnst.d], writes=[un.d])
                k.op("dve", lambda e: e.tensor_tensor(un.t[:], un.t[:], gln.t[:], ALU.mult), reads=[un.d, gln.d], writes=[un.d])
                k.op("dve", lambda e: e.tensor_tensor(un.t[:], un.t[:], bln.t[:], ALU.add), reads=[un.d, bln.d], writes=[un.d])
                k.op("act", lambda e: e.activation(unb.t[:], un.t[:], AF.Silu), reads=[un.d], writes=[unb.d])
                pv = pbf(6)
                for ct in range(4):
                    k.op("pe", lambda e: e.transpose(pv[:, ct * 128:(ct + 1) * 128], unb.t[:, ct * 128:(ct + 1) * 128], identb.t[:]),
                         reads=[unb.d, identb.d], writes=[PB[6].d], pe_accum=True)
                k.op("act", lambda e: e.copy(sTl.t[:, :, i * 128:(i + 1) * 128], pv[:, 0:512].rearrange("p (c t) -> p c t", c=4)),
                     reads=[PB[6].d], writes=[sTl.d])
            dump("sT", sTl, [128, 4, TOK], BF16)
            k.dma("pool", sTd.t, sTl.t[:], reads=[sTl.d], writes=[sTd.d])
            k.barrier()
        with ExitStack() as p12:
            kT = {}
            for br in ("s", "w"):
                for g in range(2):
                    kT[br, g] = T(k, p12, "kT%s%d" % (br, g), [128, LCTX], BF16)
                    k.op("dve", lambda e: e.memset(kT[br, g].t[64:128, :], 0.0), writes=[kT[br, g].d])
                    k.dma("sp", kT[br, g].t[64:69, :], A["kaug"], writes=[kT[br, g].d])
            kcT = [T(k, p12, "kcT%d" % g, [128, 256], BF16) for g in range(2)]
            for g in range(2):
                k.op("dve", lambda e: e.memset(kcT[g].t[:, :], 0.0), writes=[kcT[g].d])
                k.dma("sp", kcT[g].t[64:69, :], A["caug"], writes=[kcT[g].d])
            qT = [T(k, p12, "qT%d" % g, [128, NQT, 512], BF16) for g in range(2)]
            for g in range(2):
                k.op("dve", lambda e: e.memset(qT[g].t[64:128, :, :], 0.0), writes=[qT[g].d])
                k.dma("sp", qT[g].t[64:69, :, :], A["qaug"][:, g].rearrange("r i h q -> r i (h q)"), writes=[qT[g].d])
            Vt = {br: T(k, p12, "V" + br, [128, 32, 2, 65], BF16) for br in ("s", "w")}
            for br in ("s", "w"):
                k.op("pool", lambda e: e.memset(Vt[br].t[:, :, :, 64:65], 1.0), writes=[Vt[br].d])
            Vc = T(k, p12, "Vc", [128, 2, 2, 65], BF16)
            k.op("pool", lambda e: e.memset(Vc.t[:, :, :, 64:65], 1.0), writes=[Vc.d])
            gts = T(k, p12, "gts", [128, NQT, 24], F32)
            expand = T(k, p12, "expand", [128, LCTX], BF16)
            k.op("dve", lambda e: e.memset(expand.t[64:128, :], 0.0), writes=[expand.d])
            k.dma("sp", expand.t[0:64, :], A["expand"], writes=[expand.d])
            ovl = T(k, p12, "ovl", [128, 2, 64], BF16)
            k.dma("sp", ovl.t[:], A["ovl"].rearrange("(c p) j -> p c j", p=128), writes=[ovl.d])

            with ExitStack() as p1:
                NA1 = 1304
                wA = T(k, p1, "wA", [128, 8, NA1], BF16)
                for c in range(8):
                    load_w(wA.t[:, c, :], wA.d, A["w_in"][c * 128:(c + 1) * 128, 0:NA1], NA1, scale=(gmix, c))
                wKV = T(k, p1, "wKV", [128, 8, 256], BF16)
                for c in range(8):
                    k.op("dve", lambda e: e.tensor_copy(wKV.t[:, c, :].rearrange("p (g a d) -> p a g d", g=2, a=2),
                                                        wA.t[:, c, C_KC:C_KC + 256].rearrange("p (a g d) -> p a g d", a=2, g=2)),
                         reads=[wA.d], writes=[wKV.d])
                cpT = [T(k, p1, "cpT%d" % g, [128, LCTX], BF16) for g in range(2)]
                xnT = [T(k, p1, "xnT%d" % i, [128, 8, 512], BF16) for i in range(2)]
                for blk in range(8):
                    xb = xnT[blk % 2]
                    own = blk >= 4
                    load_xn_block(xb, blk * 512, 4)
                    t0 = blk * 512
                    for g in range(2):
                        for br, c0 in (("s", C_KS), ("w", C_KW)):
                            b = next_bank()
                            mm8(b, PB[b].t[0:64, :], lambda c: wA.t[:, c, c0 + 64 * g:c0 + 64 * g + 64], lambda c: xb.t[:, c, :], [wA.d, xb.d])
                            evac_copy(kT[br, g].t[0:64, t0:t0 + 512], kT[br, g].d, b, PB[b].t[0:64, :])
                        b = next_bank()
                        mm8(b, PB[b].t[:, :], lambda c: wKV.t[:, c, g * 128:(g + 1) * 128], lambda c: xb.t[:, c, :], [wKV.d, xb.d])
                        evac_copy(cpT[g].t[:, t0:t0 + 512], cpT[g].d, b, PB[b].t[:, :])
                    for j in range(4):
                        tile_i = blk * 4 + j
                        for br, c0 in (("s", C_VS), ("w", C_VW)):
                            b = next_bank()
                            mm8(b, PB[b].t[:, 0:128], lambda c: xb.t[:, c, j * 128:(j + 1) * 128], lambda c: wA.t[:, c, c0:c0 + 128], [wA.d, xb.d])
                            evac_copy(Vt[br].t[:, tile_i, :, 0:64], Vt[br].d, b, PB[b].t[:, 0:128].rearrange("p (g d) -> p g d", g=2))
                    if own:
                        ob = blk - 4
                        for g in range(2):
                            for hh in range(4):
                                h = 4 * g + hh
                                b = next_bank()
                                mm8(b, PB[b].t[0:64, :], lambda c: wA.t[:, c, C_Q + 64 * h:C_Q + 64 * h + 64], lambda c: xb.t[:, c, :], [wA.d, xb.d])
                                evac_copy(qT[g].t[0:64, ob * 4:(ob + 1) * 4, hh * 128:(hh + 1) * 128], qT[g].d, b, PB[b].t[0:64, :].rearrange("p (i q) -> p i q", i=4))
                        for j in range(4):
                            b = next_bank()
                            mm8(b, PB[b].t[:, 0:24], lambda c: xb.t[:, c, j * 128:(j + 1) * 128], lambda c: wA.t[:, c, C_G:C_G + 24], [wA.d, xb.d])
                            evac_copy(gts.t[:, ob * 4 + j, :], gts.d, b, PB[b].t[:, 0:24], func=AF.Sigmoid)

                w1b = T(k, p1, "w1b", [128, 32, 64], BF16)
                w2b = T(k, p1, "w2b", [64, 64], BF16)
                w2v0 = T(k, p1, "w2v0", [64, 64], BF16)
                peT = T(k, p1, "peT", [128, 32], F32)
                peTb = T(k, p1, "peTb", [128, 32], BF16)
                cbias = {kv: T(k, p1, "cbias" + kv, [64, 1], F32) for kv in "kv"}
                h1T = T(k, p1, "h1T", [64, 256], BF16)
                k.op("pool", lambda e: e.memset(h1T.t[:], 0.0), writes=[h1T.d])
                for kv, p0 in (("k", 0), ("v", 64)):
                    src1 = A["w_cmp_%s1" % kv].rearrange("(l d) j -> d l j", d=64)
                    for lh in range(2):
                        s = stg[stg_i[0] % NSTG]; stg_i[0] += 1
                        sv = s.t[p0:p0 + 64, 0:1024].rearrange("p (l j) -> p l j", l=16)
                        k.dma("sp", sv, src1[:, lh * 16:(lh + 1) * 16, :], writes=[s.d])
                        k.op("dve", lambda e: e.tensor_copy(w1b.t[p0:p0 + 64, lh * 16:(lh + 1) * 16, :], sv), reads=[s.d], writes=[w1b.d])
                    s = stg[stg_i[0] % NSTG]; stg_i[0] += 1
                    w2dst = w2b if kv == "k" else w2v0
                    k.dma("sp", s.t[0:64, 0:64], A["w_cmp_%s2" % kv], writes=[s.d])
                    k.op("dve", lambda e: e.tensor_copy(w2dst.t[0:64, :], s.t[0:64, 0:64]), reads=[s.d], writes=[w2dst.d])
                    k.dma("sp", peT.t[p0:p0 + 64, :], A["pe_cmp_" + kv].rearrange("l d -> d l"), writes=[peT.d], allow_slow_non_contiguous=True)
                    k.op("dve", lambda e: e.tensor_copy(peTb.t[p0:p0 + 64, :], peT.t[p0:p0 + 64, :]), reads=[peT.d], writes=[peTb.d])
                    b = next_bank()
                    for l in range(32):
                        k.op("pe", lambda e: e.matmul(PB[b].t[0:64, 0:1], w1b.t[p0:p0 + 64, l, :], peTb.t[p0:p0 + 64, l:l + 1],
                                                      start=(l == 0), stop=(l == 31)),
                             reads=[w1b.d, peTb.d], writes=[PB[b].d], pe_accum=True)
                    k.op("dve", lambda e: e.tensor_copy(cbias[kv].t[:], PB[b].t[0:64, 0:1]), reads=[PB[b].d], writes=[cbias[kv].d])
                for g in range(2):
                    for kv, p0 in (("k", 0), ("v", 64)):
                        src = cpT[g].t[p0:p0 + 64, :].rearrange("p (n s) -> p n s", s=16)
                        b = next_bank()
                        for l in range(32):
                            a_, b_ = l // 16, l % 16
                            k.op("pe", lambda e: e.matmul(PB[b].t[0:64, 0:255], w1b.t[p0:p0 + 64, l, :], src[:, a_:a_ + 255, b_],
                                                          start=(l == 0), stop=(l == 31)),
                                 reads=[w1b.d, cpT[g].d], writes=[PB[b].d], pe_accum=True)
                        k.op("act", lambda e: e.activation(h1T.t[:, 0:255], PB[b].t[0:64, 0:255], AF.Gelu, bias=cbias[kv].t[:, 0:1]),
                             reads=[PB[b].d, cbias[kv].d], writes=[h1T.d])
                        if kv == "k":
                            b2 = next_bank()
                            k.op("pe", lambda e: e.matmul(PB[b2].t[0:64, 0:255], w2b.t[0:64, :], h1T.t[:, 0:255], start=True, stop=True),
                                 reads=[w2b.d, h1T.d], writes=[PB[b2].d])
                            k.op("dve", lambda e: e.tensor_copy(kcT[g].t[0:64, 0:255], PB[b2].t[0:64, 0:255]), reads=[PB[b2].d], writes=[kcT[g].d])
                        else:
                            for ch in range(2):
                                b2 = next_bank()
                                k.op("pe", lambda e: e.matmul(PB[b2].t[:, 0:64], h1T.t[:, ch * 128:(ch + 1) * 128], w2v0.t[:], start=True, stop=True),
                                     reads=[w2v0.d, h1T.d], writes=[PB[b2].d])
                                k.op("dve", lambda e: e.tensor_copy(Vc.t[:, ch, g, 0:64], PB[b2].t[:, 0:64]), reads=[PB[b2].d], writes=[Vc.d])
                    if g == 0:
                        pass
                dump("kcT", kcT[0], [128, 256], BF16)
                dump("qT", qT[0], [128, NQT, 512], BF16)
                dump("Vs", Vt["s"], [128, 32, 2, 65], BF16)
                k.barrier()
            with ExitStack() as pa:
                Pt = [T(k, pa, "Pt%d" % i, [128, 512], BF16) for i in range(3)]
                selc = {}
                for nm in ("selkeep", "seladd", "selallow"):
                    selc[nm] = T(k, pa, nm, [128, NQT, 64], F32)
                    k.dma("sp", selc[nm].t[:], A[nm].rearrange("i p j -> p i j"), writes=[selc[nm].d])
                negmT = T(k, pa, "negmT", [128, 512], BF16)
                k.op("dve", lambda e: e.memset(negmT.t[64:128, :], 0.0), writes=[negmT.d])
                impq = T(k, pa, "impq", [128, 64], F32)
                impw = T(k, pa, "impw", [128, 64], F32)
                selm = T(k, pa, "selm", [128, 64], F32)
                m8 = T(k, pa, "m8", [128, 16], F32)
                rz = T(k, pa, "rz", [128, 4], F32)
                coef = T(k, pa, "coef", [128, 4], F32)
                oacc = T(k, pa, "oacc", [128, 2, 4, 64], F32)
                otmp = T(k, pa, "otmp", [128, 4, 64], F32)
                onb = T(k, pa, "onb", [128, 512], BF16)
                s_rr = [0]

                def score_tile(lhs_ap, lhs_dep, g, i, extra=None, mrows=128):
                    b = 1 + (s_rr[0] % 3)
                    pt = Pt[s_rr[0] % 3]
                    s_rr[0] += 1
                    k.op("pe", lambda e: e.matmul(PB[b].t[0:mrows, :], lhs_ap, qT[g].t[0:128, i, :], start=True, stop=(extra is None)),
                         reads=[lhs_dep, qT[g].d], writes=[PB[b].d])
                    if extra is not None:
                        k.op("pe", lambda e: e.matmul(PB[b].t[:, :], extra, negmT.t[:, :], start=False, stop=True),
                             reads=[expand.d, negmT.d], writes=[PB[b].d], pe_accum=True)
                    k.op("act", lambda e: e.activation(pt.t[0:mrows, :], PB[b].t[0:mrows, :], AF.Exp, scale=0.125), reads=[PB[b].d], writes=[pt.d])
                    return pt

                def mask_tile(pt, base, cm, qstep, mrows=128):
                    pv = pt.t[0:mrows, :].rearrange("p (h q) -> p h q", h=4)
                    k.op("pool", lambda e: e.affine_select(out=pv, in_=pv, pattern=[[0, 4], [qstep, 128]], compare_op=ALU.is_ge,
                                                           fill=0.0, base=base, channel_multiplier=cm),
                         reads=[pt.d], writes=[pt.d])

                zz = T(k, pa, "zz", [1, 512], BF16)
                k.op("pool", lambda e: e.memset(zz.t[:], 0.0), writes=[zz.d])

                def zero_bank(bank):
                    k.op("pe", lambda e: e.matmul(PB[bank].t[:, :], zz.t[0:1, 0:128], zz.t[0:1, :], start=True, stop=False),
                         reads=[zz.d], writes=[PB[bank].d])

                def pv_acc(pt, bank, v_ap, v_dep, first, last, mrows=128):
                    if first:
                        zero_bank(bank)
                    for hh in range(4):
                        k.op("pe", lambda e: e.matmul(PB[bank].t[:, hh * 65:(hh + 1) * 65], pt.t[0:mrows, hh * 128:(hh + 1) * 128], v_ap,
                                                      start=False, stop=(last and hh == 3)),
                             reads=[pt.d, v_dep], writes=[PB[bank].d], pe_accum=True)

                def finish_branch(bank, br_idx, g, i, first):
                    pov = PB[bank].t[:, 0:260].rearrange("p (h e) -> p h e", e=65)
                    k.op("dve", lambda e: e.tensor_scalar(rz.t[:], pov[:, :, 64], 1e-30, None, ALU.max), reads=[PB[bank].d], writes=[rz.d])
                    k.op("dve", lambda e: e.reciprocal(rz.t[:], rz.t[:]), reads=[rz.d], writes=[rz.d])
                    gsl = gts.t[:, i, br_idx * 8 + 4 * g:br_idx * 8 + 4 * g + 4]
                    k.op("dve", lambda e: e.tensor_tensor(coef.t[:], rz.t[:], gsl, ALU.mult), reads=[rz.d, gts.d], writes=[coef.d])
                    cb = coef.t[:, :, None].to_broadcast([128, 4, 64])
                    if first:
                        k.op("dve", lambda e: e.tensor_tensor(oacc.t[:, g, :, :], pov[:, :, 0:64], cb, ALU.mult),
                             reads=[PB[bank].d, coef.d], writes=[oacc.d])
                    else:
                        k.op("dve", lambda e: e.tensor_tensor(otmp.t[:], pov[:, :, 0:64], cb, ALU.mult),
                             reads=[PB[bank].d, coef.d], writes=[otmp.d])
                        k.op("dve", lambda e: e.tensor_tensor(oacc.t[:, g, :, :], oacc.t[:, g, :, :], otmp.t[:], ALU.add),
                             reads=[otmp.d, oacc.d], writes=[oacc.d])

                for i in range(NQT):
                    for g in range(2):
                        pts = []
                        mrows = [128, min(128, 8 * i + 7)]
                        for kt in range(2):
                            M = mrows[kt]
                            pt = score_tile(kcT[g].t[0:128, kt * 128:kt * 128 + M], kcT[g].d, g, i, mrows=M)
                            mask_tile(pt, 2017 + 128 * i - 2048 * kt, -16, 1, mrows=M)
                            pts.append(pt)
                        for kt in range(2):
                            M = mrows[kt]
                            pv_acc(pts[kt], 4, Vc.t[0:M, kt, g, :], Vc.d, kt == 0, kt == 1, mrows=M)
                            if kt == 0:
                                zero_bank(5)
                            for hh in range(4):
                                k.op("pe", lambda e: e.matmul(PB[5].t[:, hh * 64:(hh + 1) * 64], pts[kt].t[0:M, hh * 128:(hh + 1) * 128], ovl.t[0:M, kt, :],
                                                              start=False, stop=(kt == 1 and hh == 3)),
                                     reads=[pts[kt].d, ovl.d], writes=[PB[5].d], pe_accum=True)
                        finish_branch(4, 0, g, i, True)
                        for hh in range(4):
                            if hh == 0:
                                k.op("dve", lambda e: e.tensor_scalar(impq.t[:], PB[5].t[:, 0:64], rz.t[:, 0:1], None, ALU.mult),
                                     reads=[PB[5].d, rz.d], writes=[impq.d])
                            else:
                                k.op("dve", lambda e: e.scalar_tensor_tensor(impq.t[:], PB[5].t[:, hh * 64:(hh + 1) * 64], rz.t[:, hh:hh + 1], impq.t[:],
                                                                             ALU.mult, ALU.add),
                                     reads=[PB[5].d, rz.d, impq.d], writes=[impq.d])
                        k.op("dve", lambda e: e.tensor_tensor(impq.t[:], impq.t[:], selc["selkeep"].t[:, i, :], ALU.mult),
                             reads=[impq.d, selc["selkeep"].d], writes=[impq.d])
                        k.op("dve", lambda e: e.tensor_tensor(impq.t[:], impq.t[:], selc["seladd"].t[:, i, :], ALU.add),
                             reads=[impq.d, selc["seladd"].d], writes=[impq.d])
                        if i == 3 and g == 1:
                            dump("impq", impq, [128, 64], F32)
                        k.op("dve", lambda e: e.max(out=m8.t[:, 0:8], in_=impq.t[:]), reads=[impq.d], writes=[m8.d])
                        k.op("dve", lambda e: e.match_replace(out=impw.t[:], in_to_replace=m8.t[:, 0:8], in_values=impq.t[:], imm_value=-3.0e38),
                             reads=[impq.d, m8.d], writes=[impw.d])
                        k.op("dve", lambda e: e.max(out=m8.t[:, 8:16], in_=impw.t[:]), reads=[impw.d], writes=[m8.d])
                        k.op("dve", lambda e: e.tensor_scalar(selm.t[:], impq.t[:], m8.t[:, 15:16], None, ALU.is_ge),
                             reads=[impq.d, m8.d], writes=[selm.d])
                        k.op("dve", lambda e: e.tensor_tensor(selm.t[:], selm.t[:], selc["selallow"].t[:, i, :], ALU.mult),
                             reads=[selm.d, selc["selallow"].d], writes=[selm.d])
                        k.op("dve", lambda e: e.tensor_scalar(selm.t[:], selm.t[:], -1.0, BIG8, ALU.add, ALU.mult), reads=[selm.d], writes=[selm.d])
                        pend = None
                        for r, kt in enumerate(range(12 + i, 17 + i)):
                            pt = score_tile(kT["w", g].t[0:128, kt * 128:(kt + 1) * 128], kT["w", g].d, g, i)
                            if r == 4:
                                mask_tile(pt, 0, -1, 1)
                            elif r == 0:
                                mask_tile(pt, -1, 1, -1)
                            if pend is not None:
                                pv_acc(pend[0], 7, Vt["w"].t[:, pend[1], g, :], Vt["w"].d, pend[2] == 0, False)
                            pend = (pt, kt, r)
                        pv_acc(pend[0], 7, Vt["w"].t[:, pend[1], g, :], Vt["w"].d, pend[2] == 0, True)
                        finish_branch(7, 2, g, i, False)
                        k.op("pe", lambda e: e.transpose(PB[0].t[0:64, 0:128], selm.t[:], ident.t[:]), reads=[selm.d, ident.d], writes=[PB[0].d])
                        k.op("act", lambda e: e.copy(negmT.t[0:64, :].rearrange("p (h q) -> p h q", h=4), PB[0].t[0:64, 0:128][:, None, :].to_broadcast([64, 4, 128])),
                             reads=[PB[0].d], writes=[negmT.d])
                        nk = 17 + i
                        pend = None
                        for kt in range(nk):
                            pt = score_tile(kT["s", g].t[0:128, kt * 128:(kt + 1) * 128], kT["s", g].d, g, i,
                                            extra=expand.t[0:128, kt * 128:(kt + 1) * 128])
                            if kt == nk - 1:
                                mask_tile(pt, 0, -1, 1)
                            if pend is not None:
                                pv_acc(pend[0], 6, Vt["s"].t[:, pend[1], g, :], Vt["s"].d, pend[1] == 0, False)
                            pend = (pt, kt)
                        pv_acc(pend[0], 6, Vt["s"].t[:, pend[1], g, :], Vt["s"].d, pend[1] == 0, True)
                        finish_branch(6, 1, g, i, False)
                    k.op("act", lambda e: e.copy(onb.t[:], oacc.t[:].rearrange("p g h d -> p (g h d)")), reads=[oacc.d], writes=[onb.d])
                    pv0 = pbf(0)
                    for c in range(4):
                        k.op("pe", lambda e: e.transpose(pv0[:, c * 128:(c + 1) * 128], onb.t[:, c * 128:(c + 1) * 128], identb.t[:]),
                             reads=[onb.d, identb.d], writes=[PB[0].d], pe_accum=True)
                    k.op("act", lambda e: e.copy(onsaT.t[:, :, i * 128:(i + 1) * 128], pv0[:, 0:512].rearrange("p (c t) -> p c t", c=4)),
                         reads=[PB[0].d], writes=[onsaT.d])
                dump("onsaT", onsaT, [128, 4, TOK], BF16)
                k.barrier()
        xn2d = [DD(k, xn2Td_ap[:, :, i * 128:(i + 1) * 128], "xn2d%d" % i) for i in range(NQT)]
        with ExitStack() as p3:
            wM = T(k, p3, "wM", [128, 8, 2048], BF16)
            wo = T(k, p3, "wo", [128, 8, D], BF16)
            wno = T(k, p3, "wno", [128, 4, D], BF16)
            wco = T(k, p3, "wco", [128, 4, D], BF16)
            for c in range(8):
                load_w(wM.t[:, c, :], wM.d, A["w_in"][c * 128:(c + 1) * 128, C_MG:C_MG + 2048], 2048, scale=(gmix, c))
                load_w(wo.t[:, c, :], wo.d, A["w_o"][c * 128:(c + 1) * 128, :], D)
            for c in range(4):
                load_w(wno.t[:, c, :], wno.d, A["w_nsa_out"][c * 128:(c + 1) * 128, :], D)
                load_w(wco.t[:, c, :], wco.d, A["w_conv_out"][c * 128:(c + 1) * 128, :], D)
            xnTb = T(k, p3, "xnTb", [128, 8, 512], BF16)
            sTb = T(k, p3, "sTb", [128, 4, 512], BF16)
            sg = T(k, p3, "sg", [128, 2, 512], F32)
            mT = T(k, p3, "mT", [128, 8, 512], BF16)
            t1 = T(k, p3, "t1", [128, 512], F32)
            t2 = T(k, p3, "t2", [128, 512], F32)
            x1t = [T(k, p3, "x1t%d" % i, [128, D], F32) for i in range(2)]
            xn2l = [T(k, p3, "xn2l%d" % i, [128, 8, 128], BF16) for i in range(2)]
            for blk in range(4):
                load_xn_block(xnTb, 2048 + 512 * blk, 4)
                k.dma("sp", sTb.t[:], sTd.t[:, :, blk * 512:(blk + 1) * 512], reads=[sTd.d], writes=[sTb.d])
                for m in range(8):
                    ba = next_bank()
                    mm8(ba, PB[ba].t[:, :], lambda c: wM.t[:, c, m * 128:(m + 1) * 128], lambda c: xnTb.t[:, c, :], [wM.d, xnTb.d])
                    k.op("act", lambda e: e.activation(sg.t[:, 0, :], PB[ba].t[:, :], AF.Sigmoid), reads=[PB[ba].d], writes=[sg.d])
                    bb = next_bank()
                    mm8(bb, PB[bb].t[:, :], lambda c: wM.t[:, c, 1024 + m * 128:1024 + (m + 1) * 128], lambda c: xnTb.t[:, c, :], [wM.d, xnTb.d])
                    k.op("act", lambda e: e.activation(sg.t[:, 1, :], PB[bb].t[:, :], AF.Sigmoid), reads=[PB[bb].d], writes=[sg.d])
                    by = next_bank()
                    mm8(by, PB[by].t[:, :], lambda c: wno.t[:, c, m * 128:(m + 1) * 128], lambda c: onsaT.t[:, c, blk * 512:(blk + 1) * 512],
                        [wno.d, onsaT.d], n=4)
                    bz = next_bank()
                    mm8(bz, PB[bz].t[:, :], lambda c: wco.t[:, c, m * 128:(m + 1) * 128], lambda c: sTb.t[:, c, :], [wco.d, sTb.d], n=4)
                    k.op("dve", lambda e: e.tensor_tensor(t1.t[:], sg.t[:, 0, :], PB[by].t[:, :], ALU.mult), reads=[sg.d, PB[by].d], writes=[t1.d])
                    k.op("dve", lambda e: e.tensor_tensor(t2.t[:], sg.t[:, 1, :], PB[bz].t[:, :], ALU.mult), reads=[sg.d, PB[bz].d], writes=[t2.d])
                    k.op("dve", lambda e: e.tensor_tensor(mT.t[:, m, :], t1.t[:], t2.t[:], ALU.add), reads=[t1.d, t2.d], writes=[mT.d])
                for j in range(4):
                    tile_i = blk * 4 + j
                    xi = xin[xin_i[0] % 2]
                    xt = x1t[tile_i % 2]
                    r0 = 2048 + tile_i * 128
                    k.dma("sp", xi.t[:], A["xc"][r0:r0 + 128, :], writes=[xi.d])
                    for dh in range(2):
                        b = next_bank()
                        mm8(b, PB[b].t[:, :], lambda c: mT.t[:, c, j * 128:(j + 1) * 128], lambda c: wo.t[:, c, dh * 512:(dh + 1) * 512], [mT.d, wo.d])
                        k.op("dve", lambda e: e.tensor_tensor(xt.t[:, dh * 512:(dh + 1) * 512], xi.t[:, dh * 512:(dh + 1) * 512], PB[b].t[:, :], ALU.add),
                             reads=[xi.d, PB[b].d], writes=[xt.d])
                    xin_i[0] += 1
                    k.dma("pool", x1d[tile_i].t, xt.t[:], reads=[xt.d], writes=[x1d[tile_i].d], anchor=xt.d)
                    xl = xn2l[tile_i % 2]
                    norm_transpose(xt, xl, 0, 0)
                    k.dma("pool", xn2d[tile_i].t, xl.t[:], reads=[xl.d], writes=[xn2d[tile_i].d], anchor=xl.d)
            if "x1" in debug:
                pass
            k.barrier()
        pX.close()
        with ExitStack() as pp:
            slotA = T(k, pp, "slotA", [128, NQT, 128], F32)
            slotB = T(k, pp, "slotB", [128, NQT, 128], F32)
            slotG = T(k, pp, "slotG", [128, NQT, 128], F32)
            uvd = [DD(k, uvd_ap[e1], "uvd%d" % e1) for e1 in range(128)]
            with ExitStack() as p4:
                cus = [T(k, p4, "cus%d" % i, [128, D], F32) for i in range(2)]
                cvs = [T(k, p4, "cvs%d" % i, [128, D], F32) for i in range(2)]
                cub = [T(k, p4, "cub%d" % i, [128, D], BF16) for i in range(2)]
                cuv = [T(k, p4, "cuv%d" % i, [128, 2 * D], BF16) for i in range(2)]

                def convert_tile(e1):
                    j = e1 % 2
                    k.dma("sp", cus[j].t[:], A["peer_u"][e1 * 128:(e1 + 1) * 128, :], writes=[cus[j].d])
                    k.dma("sp", cvs[j].t[:], A["peer_v"][e1 * 128:(e1 + 1) * 128, :], writes=[cvs[j].d])
                    k.op("act", lambda e: e.copy(cub[j].t[:], cus[j].t[:]), reads=[cus[j].d], writes=[cub[j].d])
                    k.op("pool", lambda e: e.tensor_copy(cuv[j].t[:, D:2 * D], cvs[j].t[:]), reads=[cvs[j].d], writes=[cuv[j].d])
                    pv = pbf(7)
                    for c in range(8):
                        k.op("pe", lambda e: e.transpose(pv[:, c * 128:(c + 1) * 128], cub[j].t[:, c * 128:(c + 1) * 128], identb.t[:]),
                             reads=[cub[j].d, identb.d], writes=[PB[7].d], pe_accum=(c > 0))
                    k.op("act", lambda e: e.copy(cuv[j].t[:, 0:D], pv), reads=[PB[7].d], writes=[cuv[j].d])
                    k.dma("pool", uvd[e1].t, cuv[j].t[:], reads=[cuv[j].d], writes=[uvd[e1].d], anchor=cuv[j].d)

                wq = T(k, p4, "wq", [128, 8, 2048], BF16)
                for c in range(8):
                    load_w(wq.t[:, c, :], wq.d, A["w_peer_q"][c * 128:(c + 1) * 128, :], 2048, scale=(gffn, c))
                KTt = T(k, p4, "KTt", [128, 16, 128], BF16)
                skb = T(k, p4, "skb", [128, 128], BF16)
                for hp in range(16):
                    s = stg[stg_i[0] % NSTG]; stg_i[0] += 1
                    k.dma("sp", s.t[:, 0:128], A["peer_sub_keys"][hp], writes=[s.d])
                    k.op("dve", lambda e: e.tensor_copy(skb.t[:], s.t[:, 0:128]), reads=[s.d], writes=[skb.d])
                    pv = pbf(0)
                    k.op("pe", lambda e: e.transpose(pv[:, 0:128], skb.t[:], identb.t[:]), reads=[skb.d, identb.d], writes=[PB[0].d])
                    k.op("act", lambda e: e.copy(KTt.t[:, hp, :], pv[:, 0:128]), reads=[PB[0].d], writes=[KTt.d])
                xq = [T(k, p4, "xq%d" % i, [128, 8, 128], BF16) for i in range(2)]
                qpT = T(k, p4, "qpT", [128, 16, 128], BF16)
                sc = T(k, p4, "sc", [128, 16, 128], F32)
                wk16 = T(k, p4, "wk16", [128, 16, 128], F32)
                cw8 = T(k, p4, "cw8", [128, 8, 256], F32)
                v12d = [k.dep("v12d%d" % i) for i in range(16)]
                i12d = [k.dep("i12d%d" % i) for i in range(16)]
                wkd = [k.dep("wkd%d" % i) for i in range(16)]
                csd = [k.dep("csd%d" % i) for i in range(8)]
                cid = [k.dep("cid%d" % i) for i in range(8)]
                cwd = [k.dep("cwd%d" % i) for i in range(8)]
                sc2 = T(k, p4, "sc2", [128, 16, 128], F32)
                v12 = T(k, p4, "v12", [128, 16, 16], F32)
                i12 = T(k, p4, "i12", [128, 16, 16], U32)
                i12f = T(k, p4, "i12f", [128, 16, 16], F32)
                cand = T(k, p4, "cand", [128, 8, 16, 16], F32)
                cs = T(k, p4, "cs", [128, 8, 16], F32)
                ci = T(k, p4, "ci", [128, 8, 16], U32)
                r12 = T(k, p4, "r12", [128, 2, 8, 16], U32)
                r12f = T(k, p4, "r12f", [128, 2, 8, 16], F32)
                oh = T(k, p4, "oh", [128, 8, 16, 16], F32)
                gd = T(k, p4, "gd", [128, 8, 16], F32)
                gz = T(k, p4, "gz", [128, 16], F32)
                v12p = [v12, T(k, p4, "v12b", [128, 16, 16], F32)]
                i12p = [i12, T(k, p4, "i12b", [128, 16, 16], U32)]
                wk16p = [wk16, T(k, p4, "wk16b", [128, 16, 128], F32)]
                v12dp = [v12d, [k.dep("v12e%d" % i) for i in range(16)]]
                i12dp = [i12d, [k.dep("i12e%d" % i) for i in range(16)]]
                wkdp = [wkd, [k.dep("wke%d" % i) for i in range(16)]]

                qpT2 = [qpT, T(k, p4, "qpTb", [128, 16, 128], BF16)]

                def pework(ti):
                    xqt = xq[ti % 2]
                    sct = sc if ti % 2 == 0 else sc2
                    k.dma("sp", xqt.t[:], xn2d[ti].t, reads=[xn2d[ti].d], writes=[xqt.d])
                    for hp in range(16):
                        b = next_bank()
                        mm8(b, PB[b].t[:, 0:128], lambda c: wq.t[:, c, hp * 128:(hp + 1) * 128], lambda c: xqt.t[:, c, :], [wq.d, xqt.d])
                        k.op("act", lambda e: e.copy(qpT2[ti % 2].t[:, hp, :], PB[b].t[:, 0:128]), reads=[PB[b].d], writes=[qpT2[ti % 2].d])
                    qpT_ = qpT2[ti % 2]
                    for q4 in range(4):
                        b = next_bank()
                        for r in range(4):
                            hp = q4 * 4 + r
                            k.op("pe", lambda e: e.matmul(PB[b].t[:, r * 128:(r + 1) * 128], qpT_.t[:, hp, :], KTt.t[:, hp, :], start=True, stop=True),
                                 reads=[qpT_.d, KTt.d], writes=[PB[b].d], pe_accum=True)
                        k.op("act", lambda e: e.copy(sct.t[:, q4 * 4:(q4 + 1) * 4, :], PB[b].t[:, :].rearrange("p (r i) -> p r i", r=4)),
                             reads=[PB[b].d], writes=[sct.d])
                    for e1 in range(ti * 8, ti * 8 + 8):
                        convert_tile(e1)

                def head_ops(ti):
                    par = ti % 2
                    sct = sc if par == 0 else sc2
                    v12_, i12_, wk_ = v12p[par], i12p[par], wk16p[par]
                    vd, idp, wd = v12dp[par], i12dp[par], wkdp[par]
                    ops = []
                    for step in range(5):
                        for hp in range(16):
                            sv = sct.t[:, hp, :]
                            wv = wk_.t[:, hp, :]
                            if step == 0:
                                ops.append(lambda hp=hp, sv=sv: k.op("dve", lambda e: e.max(out=v12_.t[:, hp, 0:8], in_=sv), reads=[sct.d], writes=[vd[hp]]))
                            elif step == 1:
                                ops.append(lambda hp=hp, sv=sv: k.op("dve", lambda e: e.max_index(out=i12_.t[:, hp, 0:8], in_max=v12_.t[:, hp, 0:8], in_values=sv),
                                                                     reads=[sct.d, vd[hp]], writes=[idp[hp]]))
                            elif step == 2:
                                ops.append(lambda hp=hp, sv=sv, wv=wv: k.op("dve", lambda e: e.match_replace(out=wv, in_to_replace=v12_.t[:, hp, 0:8], in_values=sv, imm_value=-3.0e38),
                                                                            reads=[sct.d, vd[hp]], writes=[wd[hp]]))
                            elif step == 3:
                                ops.append(lambda hp=hp, wv=wv: k.op("dve", lambda e: e.max(out=v12_.t[:, hp, 8:16], in_=wv), reads=[wd[hp]], writes=[vd[hp]]))
                            else:
                                ops.append(lambda hp=hp, wv=wv: k.op("dve", lambda e: e.max_index(out=i12_.t[:, hp, 8:16], in_max=v12_.t[:, hp, 8:16], in_values=wv),
                                                                     reads=[wd[hp], vd[hp]], writes=[idp[hp]]))
                    return ops

                def tail_ops(ti):
                    par = ti % 2
                    v12_, i12_ = v12p[par], i12p[par]
                    vd, idp = v12dp[par], i12dp[par]
                    ops = []
                    ops.append(lambda: k.op("dve", lambda e: e.tensor_copy(i12f.t[:], i12_.t[:]), reads=idp, writes=[i12f.d]))
                    v4 = v12_.t[:].rearrange("p (h a) r -> p h a r", a=2)
                    i4 = i12f.t[:].rearrange("p (h a) r -> p h a r", a=2)
                    ops.append(lambda: k.op("dve", lambda e: e.tensor_tensor(cand.t[:], v4[:, :, 0, :, None].to_broadcast([128, 8, 16, 16]),
                                                                             v4[:, :, 1, None, :].to_broadcast([128, 8, 16, 16]), ALU.add), reads=vd, writes=[cand.d]))
                    for step in range(5):
                        for h in range(8):
                            cf = cand.t[:, h].rearrange("p a b -> p (a b)")
                            cwf = cw8.t[:, h, :]
                            if step == 0:
                                ops.append(lambda h=h, cf=cf: k.op("dve", lambda e: e.max(out=cs.t[:, h, 0:8], in_=cf), reads=[cand.d], writes=[csd[h]]))
                            elif step == 1:
                                ops.append(lambda h=h, cf=cf: k.op("dve", lambda e: e.max_index(out=ci.t[:, h, 0:8], in_max=cs.t[:, h, 0:8], in_values=cf),
                                                                   reads=[cand.d, csd[h]], writes=[cid[h]]))
                            elif step == 2:
                                ops.append(lambda h=h, cf=cf, cwf=cwf: k.op("dve", lambda e: e.match_replace(out=cwf, in_to_replace=cs.t[:, h, 0:8], in_values=cf, imm_value=-3.0e38),
                                                                            reads=[cand.d, csd[h]], writes=[cwd[h]]))
                            elif step == 3:
                                ops.append(lambda h=h, cwf=cwf: k.op("dve", lambda e: e.max(out=cs.t[:, h, 8:16], in_=cwf), reads=[cwd[h]], writes=[csd[h]]))
                            else:
                                ops.append(lambda h=h, cwf=cwf: k.op("dve", lambda e: e.max_index(out=ci.t[:, h, 8:16], in_max=cs.t[:, h, 8:16], in_values=cwf),
                                                                     reads=[cwd[h], csd[h]], writes=[cid[h]]))
                    ops.append(lambda: k.op("dve", lambda e: e.tensor_single_scalar(r12.t[:, 0], ci.t[:], 4, ALU.logical_shift_right), reads=cid, writes=[r12.d]))
                    ops.append(lambda: k.op("dve", lambda e: e.tensor_single_scalar(r12.t[:, 1], ci.t[:], 15, ALU.bitwise_and), reads=cid, writes=[r12.d]))
                    ops.append(lambda: k.op("dve", lambda e: e.tensor_copy(r12f.t[:], r12.t[:]), reads=[r12.d], writes=[r12f.d]))
                    for half, slot in ((0, slotA), (1, slotB)):
                        ops.append(lambda half=half: k.op("dve", lambda e: e.tensor_tensor(oh.t[:], iotaf.t[:, None, None, 0:16].to_broadcast([128, 8, 16, 16]),
                                                                                            r12f.t[:, half, :, :, None].to_broadcast([128, 8, 16, 16]), ALU.is_equal),
                                                          reads=[iotaf.d, r12f.d], writes=[oh.d]))
                        ops.append(lambda half=half: k.op("dve", lambda e: e.tensor_tensor(oh.t[:], oh.t[:], i4[:, :, half, None, :].to_broadcast([128, 8, 16, 16]), ALU.mult),
                                                          reads=[oh.d, i12f.d], writes=[oh.d]))
                        ops.append(lambda slot=slot: k.op("dve", lambda e: e.reduce_sum(slot.t[:, ti, :].rearrange("p (h k) -> p h k", h=8), oh.t[:], axis=AX.X),
                                                          reads=[oh.d], writes=[slot.d]))
                    ops.append(lambda: k.op("dve", lambda e: e.tensor_tensor(gd.t[:], cs.t[:], cs.t[:, :, 0:1].to_broadcast([128, 8, 16]), ALU.subtract), reads=csd, writes=[gd.d]))
                    ops.append(lambda: k.op("act", lambda e: e.activation(gd.t[:], gd.t[:], AF.Exp), reads=[gd.d], writes=[gd.d]))
                    ops.append(lambda: k.op("dve", lambda e: e.reduce_sum(gz.t[:, 0:8], gd.t[:], axis=AX.X), reads=[gd.d], writes=[gz.d]))
                    ops.append(lambda: k.op("dve", lambda e: e.reciprocal(gz.t[:, 8:16], gz.t[:, 0:8]), reads=[gz.d], writes=[gz.d]))
                    ops.append(lambda: k.op("dve", lambda e: e.tensor_tensor(slotG.t[:, ti, :].rearrange("p (h k) -> p h k", h=8), gd.t[:],
                                                                             gz.t[:, 8:16, None].to_broadcast([128, 8, 16]), ALU.mult), reads=[gd.d, gz.d], writes=[slotG.d]))
                    return ops

                pework(0)
                for o in head_ops(0):
                    o()
                for ti in range(NQT):
                    tl = tail_ops(ti)
                    hd = []
                    if ti + 1 < NQT:
                        pework(ti + 1)
                        hd = head_ops(ti + 1)
                    nh, nt = len(hd), len(tl)
                    hi = 0
                    for idx, o in enumerate(tl):
                        o()
                        want = (idx + 1) * nh // nt
                        while hi < want:
                            hd[hi]()
                            hi += 1
                    while hi < nh:
                        hd[hi]()
                        hi += 1
                dump("slotA", slotA, [128, NQT, 128], F32)
                dump("slotB", slotB, [128, NQT, 128], F32)
                dump("slotG", slotG, [128, NQT, 128], F32)
                k.barrier()

            with ExitStack() as p5:
                TBM = 384
                Gs = T(k, p5, "Gs", [128, TBM, 128], BF16)
                gfin = load_bcast(p5, "gfin", A["g_final"], D)
                x2 = T(k, p5, "x2", [128, D], F32)
                passes = [(0, 2), (2, 3), (5, 3), (8, 3), (11, 3), (14, 2)]
                for ps_, (tile0, ntile) in enumerate(passes):
                    TBp = ntile * 128
                    with ExitStack() as pg:
                        NCH = 4
                        CH = 128 // NCH
                        Aoh = [T(k, pg, "Aoh%d" % c, [128, CH, 128], BF16) for c in range(2)]
                        Boh = [T(k, pg, "Boh%d" % c, [128, CH, 128], BF16) for c in range(2)]
                        trs = [T(k, pg, "trs%d" % i, [128, 128], BF16) for i in range(3)]
                        g_rr = 0
                        c_rr = 0
                        for tt in range(ntile):
                            ti = tile0 + tt
                            for idx, slot in enumerate((slotA, slotB, slotG)):
                                k.op("pe", lambda e: e.transpose(PB[0].t[:, idx * 128:(idx + 1) * 128], slot.t[:, ti, :], ident.t[:]),
                                     reads=[slot.d, ident.d], writes=[PB[0].d], pe_accum=(idx > 0))
                            for idx in range(3):
                                k.op("act", lambda e: e.copy(trs[idx].t[:], PB[0].t[:, idx * 128:(idx + 1) * 128]), reads=[PB[0].d], writes=[trs[idx].d])
                            ib = iotab.t[:, None, :].to_broadcast([128, CH, 128])
                            for ch in range(NCH):
                                Ac, Bc = Aoh[c_rr % 2], Boh[c_rr % 2]
                                c_rr += 1
                                tsl = slice(ch * CH, (ch + 1) * CH)
                                k.op("dve", lambda e: e.tensor_tensor(Ac.t[:], ib, trs[0].t[:, tsl, None].to_broadcast([128, CH, 128]), ALU.is_equal),
                                     reads=[iotab.d, trs[0].d], writes=[Ac.d])
                                k.op("dve", lambda e: e.tensor_tensor(Ac.t[:], Ac.t[:], trs[2].t[:, tsl, None].to_broadcast([128, CH, 128]), ALU.mult),
                                     reads=[Ac.d, trs[2].d], writes=[Ac.d])
                                k.op("dve", lambda e: e.tensor_tensor(Bc.t[:], ib, trs[1].t[:, tsl, None].to_broadcast([128, CH, 128]), ALU.is_equal),
                                     reads=[iotab.d, trs[1].d], writes=[Bc.d])
                                for t4 in range(CH // 4):
                                    b = 1 + (g_rr % 7)
                                    g_rr += 1
                                    for r in range(4):
                                        tq = t4 * 4 + r
                                        k.op("pe", lambda e: e.matmul(PB[b].t[:, r * 128:(r + 1) * 128], Bc.t[:, tq, :], Ac.t[:, tq, :], start=True, stop=True),
                                             reads=[Ac.d, Bc.d], writes=[PB[b].d], pe_accum=(r > 0))
                                    tg = tt * 128 + ch * CH + t4 * 4
                                    k.op("act", lambda e: e.copy(Gs.t[:, tg:tg + 4, :], PB[b].t[:, :].rearrange("p (r i) -> p r i", r=4)),
                                         reads=[PB[b].d], writes=[Gs.d])
                        k.barrier()
                    if ps_ == 0:
                        dump("Gs", Gs, [128, TBM, 128], BF16)
                    with ExitStack() as pd:
                        NB = 6
                        uv = [T(k, pd, "uv%d" % i, [128, 2 * D], BF16) for i in range(NB)]
                        gl = [T(k, pd, "gl%d" % i, [128, TBM], F32) for i in range(NB)]
                        aT = [T(k, pd, "aT%d" % i, [128, TBM], BF16) for i in range(NB)]
                        xp = T(k, pd, "xp", [128, 8, TBM], BF16)
                        for tt in range(ntile):
                            ti = tile0 + tt
                            k.dma("sp", xp.t[:, :, tt * 128:(tt + 1) * 128], xn2d[ti].t, reads=[xn2d[ti].d], writes=[xp.d], join=True)
                        hbanks = [7, 0]

                        def stage2(e1):
                            jb = e1 % NB
                            bh = hbanks[e1 % len(hbanks)]
                            k.op("act", lambda e: e.activation(gl[jb].t[:, 0:TBp], PB[bh].t[:, 0:TBp], AF.Gelu), reads=[PB[bh].d], writes=[gl[jb].d])
                            k.op("dve", lambda e: e.tensor_tensor(aT[jb].t[:, 0:TBp], gl[jb].t[:, 0:TBp], Gs.t[:, 0:TBp, e1], ALU.mult),
                                 reads=[gl[jb].d, Gs.d], writes=[aT[jb].d])
                            for tt in range(ntile):
                                for dh in range(2):
                                    ba = 1 + tt * 2 + dh
                                    k.op("pe", lambda e: e.matmul(PB[ba].t[:, :], aT[jb].t[:, tt * 128:(tt + 1) * 128], uv[jb].t[:, D + dh * 512:D + (dh + 1) * 512],
                                                                  start=(e1 == 0), stop=(e1 == 127)),
                                         reads=[aT[jb].d, uv[jb].d], writes=[PB[ba].d], pe_accum=(e1 > 0))

                        for e1 in range(128):
                            j = e1 % 2
                            jb = e1 % NB
                            k.dma("sp", uv[jb].t[:], uvd[e1].t, reads=[uvd[e1].d], writes=[uv[jb].d])
                            bh = hbanks[e1 % len(hbanks)]
                            for c in range(8):
                                k.op("pe", lambda e: e.matmul(PB[bh].t[:, 0:TBp], uv[jb].t[:, c * 128:(c + 1) * 128], xp.t[:, c, 0:TBp], start=(c == 0), stop=(c == 7)),
                                     reads=[uv[jb].d, xp.d], writes=[PB[bh].d], pe_accum=(c > 0))
                            if len(hbanks) == 1:
                                stage2(e1)
                            elif e1 >= 1:
                                stage2(e1 - 1)
                        if len(hbanks) > 1:
                            stage2(127)
                        for tt in range(ntile):
                            ti = tile0 + tt
                            xi = xin[xin_i[0] % 2]
                            k.dma("sp", xi.t[:], x1d[ti].t, reads=[x1d[ti].d], writes=[xi.d])
                            for dh in range(2):
                                ba = 1 + tt * 2 + dh
                                k.op("dve", lambda e: e.tensor_tensor(x2.t[:, dh * 512:(dh + 1) * 512], xi.t[:, dh * 512:(dh + 1) * 512], PB[ba].t[:, :], ALU.add),
                                     reads=[xi.d, PB[ba].d], writes=[x2.d])
                            ns = nstat[xin_i[0] % 2]
                            xin_i[0] += 1
                            rms_scale(x2, ns)
                            k.op("dve", lambda e: e.scalar_tensor_tensor(x2.t[:], x2.t[:], ns.t[:, 1:2], gfin.t[:], ALU.mult, ALU.mult),
                                 reads=[x2.d, ns.d, gfin.d], writes=[x2.d])
                            od = DD(k, out[ti * 128:(ti + 1) * 128, :], "out%d" % ti)
                            k.dma("pool", od.t, x2.t[:], reads=[x2.d], writes=[od.d], anchor=x2.d)
                            outdeps.append(od.d)
                        k.barrier()
        k.finish(outdeps)
    return nc


_NC_CACHE = {}


def make_in_maps(inputs):
    x = np.ascontiguousarray(np.asarray(inputs["x"], dtype=np.float32))
    shared = {}
    for name, shp in W_SPECS:
        shared[name] = np.ascontiguousarray(np.asarray(inputs[name], dtype=np.float32).reshape(shp))
    in_maps = []
    for c in range(NCORES):
        b, half = c // 2, c % 2
        xc = np.zeros((LCTX, D), np.float32)
        if half == 1:
            xc[:] = x[b]
        else:
            xc[2048:] = x[b, 0:2048]
        m = dict(shared)
        m["xc"] = xc
        m.update(host_consts(half))
        in_maps.append(m)
    return in_maps


def kernel(**inputs):
    if "nc" not in _NC_CACHE:
        _NC_CACHE["nc"] = build()
    nc = _NC_CACHE["nc"]
    in_maps = make_in_maps(inputs)
    res = run_bass_kernel_spmd(nc, in_maps, core_ids=list(range(NCORES)))
    outp = np.zeros((4, 4096, D), np.float32)
    for c in range(NCORES):
        b, half = c // 2, c % 2
        outp[b, half * 2048:(half + 1) * 2048] = np.asarray(res.results[c]["out"], dtype=np.float32)
    return outp
```
